# Optimizing a Trainium2 kernel written in Bass

```python
import math
import jax, jax.numpy as jnp
from jax import lax
import numpy as np

D_MODEL = 1024
BATCH = 8
SEQ = 4096
DEPTH = 1

ATTN_HEADS = 16
ATTN_HEAD_DIM = 64
ATTN_WIDTH = ATTN_HEADS * ATTN_HEAD_DIM
DILATION_PATTERNS = ((128, 1), (512, 4), (2048, 16))
GLA_HEADS = 4
GLA_KEY_WIDTH = D_MODEL // 2
GLA_VALUE_WIDTH = D_MODEL
GLA_KEY_DIM = GLA_KEY_WIDTH // GLA_HEADS
GLA_VALUE_DIM = GLA_VALUE_WIDTH // GLA_HEADS
GLA_GATE_RANK = 16
GLA_GATE_TAU = 16.0
GLA_CHUNK = 64
N_BRANCHES = 2
NORM_EPS = 1e-6
GROUP_NORM_EPS = 1e-5

IN_SPLITS = (
    ATTN_WIDTH,
    ATTN_WIDTH,
    ATTN_WIDTH,
    ATTN_WIDTH,
    GLA_KEY_WIDTH,
    GLA_KEY_WIDTH,
    GLA_VALUE_WIDTH,
    GLA_VALUE_WIDTH,
    GLA_GATE_RANK,
    D_MODEL,
    D_MODEL,
)
IN_COLS = int(sum(IN_SPLITS))
IN_OFFSETS = tuple(int(o) for o in np.cumsum(IN_SPLITS)[:-1])

kernel_name = 'hybrid_dilated_attn_gla_gated_block'


def rmsnorm(x, gain):
    xf = x.astype(jnp.float32)
    y = xf * lax.rsqrt(jnp.mean(xf * xf, axis=-1, keepdims=True) + NORM_EPS) * gain.astype(jnp.float32)
    return y.astype(x.dtype)


def alibi_slopes(n_heads):
    return 2.0 ** (-8.0 * (jnp.arange(n_heads, dtype=jnp.float32) + 1.0) / n_heads)


def dilated_attention(q, k, v, window, dilation, slopes):
    bsz, seq, heads, hd = q.shape
    n = window // dilation
    unit = n * dilation
    seq_pad = -(-seq // unit) * unit
    length = seq_pad // dilation
    nb = length // n

    def strided(t):
        t = jnp.pad(t, ((0, 0), (0, seq_pad - seq), (0, 0), (0, 0)))
        t = t.reshape(bsz, length, dilation, heads, hd).transpose(0, 2, 1, 3, 4)
        return t.reshape(bsz, dilation, nb, n, heads, hd)

    def with_prev(t):
        prev = jnp.pad(t, ((0, 0), (0, 0), (1, 0), (0, 0), (0, 0), (0, 0)))[:, :, :-1]
        return jnp.concatenate([prev, t], axis=3)

    qs = strided(q)
    kb = with_prev(strided(k))
    vb = with_prev(strided(v))

    scores = jnp.einsum('brnqhd,brnkhd->brnhqk', qs, kb).astype(jnp.float32) * (hd ** -0.5)
    qi = jnp.arange(n)[:, None]
    kj = jnp.arange(2 * n)[None, :]
    steps = n + qi - kj
    band = (steps >= 0) & (steps <= n)
    first = (jnp.arange(nb)[:, None, None] == 0) & (kj < n)[None]
    valid = band[None] & ~first
    bias = -slopes[:, None, None] * (steps * dilation).astype(jnp.float32)[None]
    s = jnp.where(valid[:, None], scores + bias, -jnp.inf)
    m = jnp.max(s, axis=-1, keepdims=True)
    p = jnp.exp(s - m)
    den = jnp.sum(p, axis=-1, keepdims=True)
    o = jnp.einsum('brnhqk,brnkhd->brnqhd', p / den, vb.astype(jnp.float32))
    lse = (m + jnp.log(den))[..., 0]

    o = o.reshape(bsz, dilation, length, heads, hd).transpose(0, 2, 1, 3, 4)
    o = o.reshape(bsz, seq_pad, heads, hd)[:, :seq]
    lse = lse.transpose(0, 1, 2, 4, 3).reshape(bsz, dilation, length, heads)
    lse = lse.transpose(0, 2, 1, 3).reshape(bsz, seq_pad, heads)[:, :seq]
    return o, lse


def dilated_mixture_attention(q, k, v):
    slopes = alibi_slopes(q.shape[2])
    outs, lses = [], []
    for window, dilation in DILATION_PATTERNS:
        o, lse = dilated_attention(q, k, v, window, dilation, slopes)
        outs.append(o)
        lses.append(lse)
    w = jax.nn.softmax(jnp.stack(lses, axis=0), axis=0)
    return jnp.sum(w[..., None] * jnp.stack(outs, axis=0), axis=0)


def gla_chunked(q, k, v, log_a):
    bsz, seq, heads, dk = q.shape
    dv = v.shape[-1]
    c = GLA_CHUNK
    nc = seq // c
    q = q.astype(jnp.float32).reshape(bsz, nc, c, heads, dk) * (dk ** -0.5)
    k = k.astype(jnp.float32).reshape(bsz, nc, c, heads, dk)
    v = v.astype(jnp.float32).reshape(bsz, nc, c, heads, dv)
    b = jnp.cumsum(log_a.astype(jnp.float32).reshape(bsz, nc, c, heads, dk), axis=2)
    q_dec = q * jnp.exp(b)
    k_inv = k * jnp.exp(-b)
    causal = jnp.tril(jnp.ones((c, c), dtype=bool))
    a_intra = jnp.where(causal, jnp.einsum('bnihd,bnjhd->bnhij', q_dec, k_inv), 0.0)
    o_intra = jnp.einsum('bnhij,bnjhe->bnihe', a_intra, v)
    b_last = b[:, :, -1]
    k_to_end = k * jnp.exp(b_last[:, :, None] - b)
    d_state = jnp.einsum('bnjhd,bnjhe->bnhde', k_to_end, v)

    def step(state, xs):
        q_c, decay_c, ds_c = xs
        o_c = jnp.einsum('bihd,bhde->bihe', q_c, state)
        return decay_c[..., None] * state + ds_c, o_c

    state0 = jnp.zeros((bsz, heads, dk, dv), jnp.float32)
    _, o_inter = lax.scan(step, state0, (jnp.moveaxis(q_dec, 1, 0),
                                         jnp.moveaxis(jnp.exp(b_last), 1, 0),
                                         jnp.moveaxis(d_state, 1, 0)))
    o = o_intra + jnp.moveaxis(o_inter, 0, 1)
    return o.reshape(bsz, seq, heads, dv)


def head_group_norm(o, gain):
    mu = jnp.mean(o, axis=-1, keepdims=True)
    var = jnp.mean(jnp.square(o - mu), axis=-1, keepdims=True)
    y = (o - mu) * lax.rsqrt(var + GROUP_NORM_EPS)
    return y * gain.astype(jnp.float32).reshape(o.shape[-2], o.shape[-1])


def setup_inputs(seed: int = 0) -> dict:
    key = jax.random.key(seed)
    ks = jax.random.split(key, 12)
    f = jnp.float32
    return {
        'x': jax.random.normal(ks[0], (BATCH, SEQ, D_MODEL), f),
        'norm_gain': 1.0 + 0.02 * jax.random.normal(ks[1], (DEPTH, D_MODEL), f),
        'w_in': jax.random.normal(ks[2], (DEPTH, D_MODEL, IN_COLS), f) * D_MODEL ** -0.5,
        'b_gate': 0.01 * jax.random.normal(ks[3], (DEPTH, N_BRANCHES * D_MODEL), f),
        'w_alpha': jax.random.normal(ks[4], (DEPTH, GLA_GATE_RANK, GLA_KEY_WIDTH), f) * GLA_GATE_RANK ** -0.5,
        'b_alpha': 0.1 * jax.random.normal(ks[5], (DEPTH, GLA_KEY_WIDTH), f),
        'gla_norm_gain': 1.0 + 0.02 * jax.random.normal(ks[6], (DEPTH, GLA_VALUE_WIDTH), f),
        'w_out_attn': jax.random.normal(ks[7], (DEPTH, ATTN_WIDTH, D_MODEL), f) * ATTN_WIDTH ** -0.5,
        'w_out_gla': jax.random.normal(ks[8], (DEPTH, GLA_VALUE_WIDTH, D_MODEL), f) * GLA_VALUE_WIDTH ** -0.5,
        'w_out': jax.random.normal(ks[9], (DEPTH, D_MODEL, D_MODEL), f) * D_MODEL ** -0.5,
        'final_norm_gain': 1.0 + 0.02 * jax.random.normal(ks[10], (D_MODEL,), f),
    }


def reference(x, norm_gain, w_in, b_gate, w_alpha, b_alpha, gla_norm_gain,
              w_out_attn, w_out_gla, w_out, final_norm_gain):
    bsz, seq, _ = x.shape
    h = x
    for layer in range(DEPTH):
        u = rmsnorm(h, norm_gain[layer])
        proj = u @ w_in[layer]
        (q_a, k_a, v_a, z_a, q_b, k_b, v_b, z_b, a_code, g_a, g_b) = jnp.split(proj, IN_OFFSETS, axis=-1)

        o_a = dilated_mixture_attention(q_a.reshape(bsz, seq, ATTN_HEADS, ATTN_HEAD_DIM),
                                        k_a.reshape(bsz, seq, ATTN_HEADS, ATTN_HEAD_DIM),
                                        v_a.reshape(bsz, seq, ATTN_HEADS, ATTN_HEAD_DIM))
        o_a = o_a.reshape(bsz, seq, ATTN_WIDTH).astype(x.dtype) * jax.nn.silu(z_a)
        y_a = o_a @ w_out_attn[layer]

        log_a = jax.nn.log_sigmoid((a_code @ w_alpha[layer] + b_alpha[layer]).astype(jnp.float32)) / GLA_GATE_TAU
        o_b = gla_chunked(q_b.reshape(bsz, seq, GLA_HEADS, GLA_KEY_DIM),
                          k_b.reshape(bsz, seq, GLA_HEADS, GLA_KEY_DIM),
                          v_b.reshape(bsz, seq, GLA_HEADS, GLA_VALUE_DIM),
                          log_a.reshape(bsz, seq, GLA_HEADS, GLA_KEY_DIM))
        o_b = head_group_norm(o_b, gla_norm_gain[layer]).reshape(bsz, seq, GLA_VALUE_WIDTH)
        o_b = o_b.astype(x.dtype) * jax.nn.silu(z_b)
        y_b = o_b @ w_out_gla[layer]

        gate_bias_a, gate_bias_b = jnp.split(b_gate[layer], N_BRANCHES)
        merged = jax.nn.sigmoid(g_a + gate_bias_a) * y_a + jax.nn.sigmoid(g_b + gate_bias_b) * y_b
        h = h + merged @ w_out[layer]
    return rmsnorm(h, final_norm_gain)
```

```python
import numpy as np
from contextlib import ExitStack
import concourse.bass as bass
import concourse.mybir as mybir
from concourse.bass_utils import run_bass_kernel_spmd

F32 = mybir.dt.float32
BF16 = mybir.dt.bfloat16
ALU = mybir.AluOpType
AF = mybir.ActivationFunctionType

SEQ = 4096
DM = 1024
NT = 32
PATTERNS = (1, 4, 16)
DEBUG = False
PHASES = 4


class Buf:
    def __init__(self, name):
        self.name = name
        self.w = None
        self.r = {}


class Sched:
    ENG = ('pe', 'act', 'dve', 'pool', 'sp')

    def __init__(self):
        self.q = {e: [] for e in self.ENG}
        self.cnt = {'E_' + e: 0 for e in self.ENG}
        self.known = {e: {} for e in self.ENG}
        self.pend_r = {e: [] for e in self.ENG}
        self.pend_w = {e: [] for e in self.ENG}

    def _deps(self, reads, writes):
        evs = []
        for b in reads:
            if b.w is not None:
                evs.append(b.w)
        for b in writes:
            if b.w is not None:
                evs.append(b.w)
            evs.extend(b.r.items())
        return evs

    def _waits(self, eng, evs):
        need = {}
        for (k, v) in evs:
            if v > need.get(k, 0):
                need[k] = v
        out = []
        for k, v in need.items():
            if self.known[eng].get(k, 0) >= v:
                continue
            self.known[eng][k] = v
            out.append((k, v))
        return out

    def _record(self, ev, reads, writes):
        for b in writes:
            b.w = ev
            b.r = {}
        for b in reads:
            if ev[1] > b.r.get(ev[0], 0):
                b.r[ev[0]] = ev[1]

    def op(self, eng, fn, reads=(), writes=(), sig=True):
        writes = list(writes) + [b for b in reads if b.name.startswith("ps") and b not in writes]
        waits = self._waits(eng, self._deps(reads, writes))
        key = 'E_' + eng
        if not sig:
            self.pend_r[eng].extend(reads)
            self.pend_w[eng].extend(writes)
            self.q[eng].append((fn, waits, None))
            return None
        self.cnt[key] += 1
        ev = (key, self.cnt[key])
        self.q[eng].append((fn, waits, (key, 1)))
        self._record(ev, list(reads) + self.pend_r[eng], list(writes) + self.pend_w[eng])
        self.pend_r[eng] = []
        self.pend_w[eng] = []
        return ev

    def dma(self, eng, fn, semkey, reads=(), writes=()):
        waits = self._waits(eng, self._deps(reads, writes))
        self.cnt[semkey] = self.cnt.get(semkey, 0) + 16
        ev = (semkey, self.cnt[semkey])
        self.q[eng].append((fn, waits, (semkey, 16)))
        self._record(ev, reads, writes)
        return ev

    def barrier(self):
        evs = [(k, v) for k, v in self.cnt.items() if v > 0]
        for e in self.ENG:
            w = self._waits(e, evs)
            if w:
                self.q[e].append((None, w, None))

    def emit(self, nc, sems):
        q = self.q
        self.q = {e: [] for e in self.ENG}
        with nc.Block() as block:
            def run(engname):
                def body(e):
                    for fn, waits, inc in q[engname]:
                        for (k, v) in waits:
                            e.wait_ge(sems[k], v)
                        if fn is None:
                            continue
                        ins = fn(e)
                        if inc is not None:
                            ins.then_inc(sems[inc[0]], inc[1])
                return body
            block.tensor(run('pe'))
            block.scalar(run('act'))
            block.vector(run('dve'))
            block.gpsimd(run('pool'))
            block.sync(run('sp'))


def MM(out, lhsT, rhs, start=True, stop=True):
    return lambda e: e.matmul(out, lhsT=lhsT, rhs=rhs, start=start, stop=stop)


def ACTV(out, in_, func, **kw):
    return lambda e: e.activation(out=out, in_=in_, func=func, **kw)


def TT(out, in0, in1, op):
    return lambda e: e.tensor_tensor(out=out, in0=in0, in1=in1, op=op)


def TS(out, in0, s1, s2, op0, op1=None):
    if op1 is None:
        return lambda e: e.tensor_scalar(out=out, in0=in0, scalar1=s1, scalar2=None, op0=op0)
    return lambda e: e.tensor_scalar(out=out, in0=in0, scalar1=s1, scalar2=s2, op0=op0, op1=op1)


def STT(out, in0, scalar, in1, op0, op1):
    return lambda e: e.scalar_tensor_tensor(out=out, in0=in0, scalar=scalar, in1=in1, op0=op0, op1=op1)


def CP(out, in_):
    return lambda e: e.tensor_copy(out=out, in_=in_)


def DMA(out, in_):
    return lambda e: e.dma_start(out=out, in_=in_)


SEMKEYS = ['E_pe', 'E_act', 'E_dve', 'E_pool', 'E_sp',
           'D_c0', 'D_c1', 'D_c2', 'D_c3', 'D_c4', 'D_c5', 'D_x0', 'D_x1', 'D_x2', 'D_x3', 'D_x4', 'D_x5', 'D_w', 'D_w0', 'D_w1', 'D_sp', 'D_sp0', 'D_sp1',
           'D_o0', 'D_o1', 'D_ld0', 'D_ld1', 'D_xr0', 'D_xr1']


def build():
    nc = bass.Bass("TRN2", target_bir_lowering=False)

    def din(name, shape, dt=F32):
        return nc.dram_tensor(name, list(shape), dt, kind="ExternalInput").ap()

    x_d = din("x", [SEQ, DM])
    gainT_d = din("gainT", [128, 8])
    watt_d = din("watt", [8, 128, 8, 512])
    wgla_d = din("wgla", [128, 8, 3088])
    wg_d = din("wg", [128, 8, 2048])
    woa_d = din("woa", [128, 8, 1024])
    wob_d = din("wob", [128, 8, 1024])
    wo_d = din("wo", [128, 8, 1024])
    bgT_d = din("bgT", [128, 16])
    wal_d = din("wal", [128, 512])
    ggT_d = din("ggT", [128, 8])
    fg_d = din("fg", [128, 1024])
    out_d = nc.dram_tensor("out", [SEQ, DM], F32, kind="ExternalOutput").ap()
    skind = "ExternalOutput" if DEBUG else "Internal"
    oa_scr = nc.dram_tensor("oa_scr", [DM, SEQ], BF16, kind=skind).ap()
    ob_scr = nc.dram_tensor("ob_scr", [DM, SEQ], BF16, kind=skind).ap()
    uT_dbg = nc.dram_tensor("uT_dbg", [128, 8, SEQ], BF16, kind="ExternalOutput").ap() if DEBUG else None
    oa_v = oa_scr.rearrange("(c p) t -> p c t", p=128)
    ob_v = ob_scr.rearrange("(c p) t -> p c t", p=128)

    S = Sched()
    final_evs = []

    with ExitStack() as G:
        def sb(es, name, shape, dt):
            return es.enter_context(nc.sbuf_tensor("s_" + name, list(shape), dt))

        sems = {k: G.enter_context(nc.semaphore(k)) for k in SEMKEYS}
        ps = [G.enter_context(nc.psum_tensor("ps%d" % i, [128, 512], F32)) for i in range(8)]
        b_ps = [Buf("ps%d" % i) for i in range(8)]
        rot = {'i': 0}

        def nextbank(pool=(0, 1, 2, 3, 4, 5, 6, 7)):
            k = rot.get(pool, 0)
            rot[pool] = k + 1
            return pool[k % len(pool)]

        uT = sb(G, "uT", [128, 8, SEQ], BF16)
        b_uT = Buf("uT")
        ident = sb(G, "ident", [128, 128], BF16)
        dmat = sb(G, "dmat", [128, 2, 128], F32)
        m01 = sb(G, "m01", [128, 2, 128], F32)
        dcl = sb(G, "dcl", [128, 2, 128], F32)
        tri = sb(G, "tri", [128, 128], F32)
        onec = sb(G, "onec", [128, 1], F32)
        gainT = sb(G, "gainT_s", [128, 8], F32)
        bgT = sb(G, "bgT_s", [128, 16], F32)
        ggT = sb(G, "ggT_s", [128, 8], F32)
        b_const3 = Buf("const3")
        b_const = Buf("const")
        b_const2 = Buf("const2")

        S.op('pool', lambda e: e.iota(dmat[:], pattern=[[-128, 2], [1, 128]], base=128, channel_multiplier=-1,
                                      allow_small_or_imprecise_dtypes=True), writes=[b_const])
        S.op('dve', TS(m01[:], dmat[:], 0.0, None, ALU.is_ge), reads=[b_const], writes=[b_const])
        S.op('dve', TS(dcl[:], dmat[:], 128.0, None, ALU.is_le), reads=[b_const], writes=[b_const])
        S.op('dve', TT(m01[:], m01[:], dcl[:], ALU.mult), reads=[b_const], writes=[b_const])
        S.op('dve', TS(dcl[:], dmat[:], 0.0, 128.0, ALU.max, ALU.min), reads=[b_const], writes=[b_const])
        S.op('dve', TS(tri[:], dmat[:, 1, :], 0.0, -1.0 / 16.0, ALU.is_ge, ALU.mult), reads=[b_const], writes=[b_const])
        S.op('dve', TS(ident[:], dmat[:, 1, :], 0.0, None, ALU.is_equal), reads=[b_const], writes=[b_const])
        S.op('dve', lambda e: e.memset(onec[:], 1.0), writes=[b_const])
        S.dma('sp', DMA(gainT[:], gainT_d[:, :]), 'D_c0', writes=[b_const])
        S.dma('sp', DMA(bgT[:], bgT_d[:, :]), 'D_c1', writes=[b_const2])
        S.dma('sp', DMA(ggT[:], ggT_d[:, :]), 'D_c3', writes=[b_const3])

        P12 = ExitStack()
        stage = sb(P12, "stage2", [128, 8, 512], F32)
        b_stage = Buf("stage2")
        Wb = sb(P12, "Wb2", [128, 8, 512], BF16)
        b_Wb = Buf("Wb2")

        with ExitStack() as P:
            S.dma('sp', DMA(stage[:], watt_d[0]), 'D_w', writes=[b_stage])
            xt = [sb(P, "xt%d" % i, [128, DM], F32) for i in range(6)]
            b_xt = [Buf("xt%d" % i) for i in range(6)]
            xn = [sb(P, "xn%d" % i, [128, DM], BF16) for i in range(2)]
            b_xn = [Buf("xn%d" % i) for i in range(2)]
            junk = sb(P, "junk1", [128, DM], BF16)
            b_junk = Buf("junk1")
            ss = sb(P, "ss", [128, NT], F32)
            rstd = sb(P, "rstd", [128, NT], F32)
            b_ss = [Buf("ss%d" % i) for i in range(NT)]
            def p1_dma(t):
                s3 = t % 6
                S.dma('sp', DMA(xt[s3][:], x_d[t * 128:(t + 1) * 128, :]), 'D_x%d' % s3, writes=[b_xt[s3]])

            epsc = sb(P, "epsc", [128, 1], F32)
            b_epsc = Buf("epsc")
            S.op('dve', lambda e: e.memset(epsc[:], 1e-6), writes=[b_epsc])
            xn3 = [sb(P, "xn3_%d" % i, [128, DM], BF16) for i in range(3)]
            b_xn3 = [Buf("xn3_%d" % i) for i in range(3)]
            pbank = {}

            def p1_A(t):
                s3 = t % 6
                S.op('act', ACTV(junk[:], xt[s3][:], AF.Square, accum_out=ss[:, t:t + 1]),
                     reads=[b_xt[s3]], writes=[b_junk, b_ss[t]])
                S.op('act', ACTV(rstd[:, t:t + 1], ss[:, t:t + 1], AF.Sqrt, scale=1.0 / DM, bias=epsc[:, 0:1]),
                     reads=[b_ss[t], b_epsc], writes=[b_ss[t]])

            def p1_D(t):
                s3 = t % 6
                s2 = t % 3
                S.op('dve', lambda e, t=t: e.reciprocal(out=rstd[:, t:t + 1], in_=rstd[:, t:t + 1]),
                     reads=[b_ss[t]], writes=[b_ss[t]])
                S.op('dve', TS(xn3[s2][:], xt[s3][:], rstd[:, t:t + 1], None, ALU.mult),
                     reads=[b_xt[s3], b_ss[t]], writes=[b_xn3[s2]])

            def p1_P(t):
                s2 = t % 3
                bks = []
                for half in range(2):
                    bk = nextbank()
                    bks.append(bk)
                    for cc in range(4):
                        c = half * 4 + cc
                        S.op('pe', MM(ps[bk][:, cc * 128:(cc + 1) * 128], lhsT=xn3[s2][:, c * 128:(c + 1) * 128],
                                      rhs=ident[:, :]), reads=[b_xn3[s2], b_const], writes=[b_ps[bk]], sig=(cc == 3))
                pbank[t] = bks

            def p1_E(t):
                bks = pbank.pop(t)
                for half in range(2):
                    bk = bks[half]
                    psv = ps[bk][:].rearrange("p (a b) -> p a b", b=128)
                    eng = 'act' if half == 0 else 'dve'
                    fn = (ACTV(uT[:, half * 4:half * 4 + 4, t * 128:(t + 1) * 128], psv, AF.Copy) if half == 0
                          else CP(uT[:, half * 4:half * 4 + 4, t * 128:(t + 1) * 128], psv))
                    S.op(eng, fn, reads=[b_ps[bk]], writes=[b_uT])

            for t in range(6):
                p1_dma(t)
            p1_A(0)
            p1_A(1)
            p1_D(0)
            for t in range(NT):
                if t + 2 < NT:
                    p1_A(t + 2)
                if t + 1 < NT:
                    p1_D(t + 1)
                p1_P(t)
                if t > 0:
                    p1_E(t - 1)
                if t + 6 < NT:
                    p1_dma(t + 6)
                if 8 <= t < 16:
                    c = t - 8
                    S.op('act', ACTV(Wb[:, c, :], stage[:, c, :], AF.Copy, scale=gainT[:, c:c + 1]),
                         reads=[b_stage, b_const], writes=[b_Wb])
                if t == 16:
                    S.dma('sp', DMA(stage[:], watt_d[1]), 'D_w', writes=[b_stage])
            p1_E(NT - 1)
            if DEBUG:
                S.dma('sp', DMA(uT_dbg[:, :, :], uT[:]), 'D_c5', reads=[b_uT])
            S.barrier()
            S.emit(nc, sems)

        with ExitStack() as P:
          if PHASES >= 2:
              qA0 = sb(P, "qA0", [128, SEQ], BF16)
              qB0 = sb(P, "qB0", [128, SEQ], BF16)
              kT = sb(P, "kT", [128, SEQ], BF16)
              vT = sb(P, "vT", [128, SEQ], BF16)
              zs = sb(P, "zs", [128, SEQ], BF16)
              oaT = sb(P, "oaT", [128, SEQ], BF16)
              b_oaT = Buf("oaT")
              b_q, b_k, b_v, b_z = Buf("q"), Buf("k"), Buf("v"), Buf("z")
              vaug = sb(P, "vaug", [128, NT, 192], BF16)
              b_vaug = Buf("vaug")
              acc = [sb(P, "acc%d" % i, [128, SEQ], F32) for i in range(2)]
              b_acc = [[Buf("acc%d_%d" % (i, j)) for j in range(8)] for i in range(2)]
              NSL = 6
              P0 = [sb(P, "P0_%d" % i, [128, 512], BF16) for i in range(NSL)]
              PT = [sb(P, "PT_%d" % i, [128, 512], BF16) for i in range(NSL)]
              b_P0 = [Buf("P0_%d" % i) for i in range(NSL)]
              b_PT = [Buf("PT_%d" % i) for i in range(NSL)]
              etab = [sb(P, "etab%d" % i, [128, 2, 256], BF16) for i in range(3)]
              b_etab = [Buf("etab%d" % i) for i in range(3)]
              tmpE = sb(P, "tmpE", [128, 256], F32)
              b_tmpE = Buf("tmpE")
              ntmp = [sb(P, "ntmp%d" % i, [128, 512], F32) for i in range(2)]
              b_ntmp = [Buf("ntmp%d" % i) for i in range(2)]

              S.op('pool', lambda e: e.memset(qA0[:], 0.0), writes=[b_q])
              S.op('pool', lambda e: e.memset(qB0[:], 0.0), writes=[b_q])
              S.op('pool', lambda e: e.memset(vaug[:], 1.0), writes=[b_vaug])

              def tokset(tl, r, c, jj):
                  return tl[:].rearrange("p (j i r) -> p j i r", i=128, r=r)[:, jj, :, c]

              def load_w(hp):
                  S.dma('sp', DMA(stage[:], watt_d[hp]), 'D_w', writes=[b_stage])

              def cast_pieces():
                  def mk(c):
                      def f():
                          S.op('act', ACTV(Wb[:, c, :], stage[:, c, :], AF.Copy, scale=gainT[:, c:c + 1]),
                               reads=[b_stage, b_const], writes=[b_Wb])
                      return f
                  return [mk(c) for c in range(8)]

              def cast_w():
                  for f in cast_pieces():
                      f()

              cnt = {'st': 0, 'et': 0, 'nt': 0}
              APOOL = (0, 1, 2, 3, 4, 5)
              OPOOL = (6, 7)

              def emit_proj(hp, gs, filler=None):
                  for g in gs:
                      for tb in range(8):
                          if filler:
                              filler.pop(0)()
                          bk = nextbank(APOOL)
                          for c in range(8):
                              S.op('pe', MM(ps[bk][:, :], lhsT=Wb[:, c, g * 128:(g + 1) * 128],
                                            rhs=uT[:, c, tb * 512:(tb + 1) * 512], start=(c == 0), stop=(c == 7)),
                                   reads=[b_Wb, b_uT], writes=[b_ps[bk]], sig=(c == 7))
                          tsl = slice(tb * 512, (tb + 1) * 512)
                          if g == 0:
                              S.op('act', ACTV(qA0[0:64, tsl], ps[bk][0:64, :], AF.Copy, scale=0.125),
                                   reads=[b_ps[bk]], writes=[b_q])
                              S.op('act', ACTV(qB0[64:128, tsl], ps[bk][64:128, :], AF.Copy, scale=0.125),
                                   reads=[b_ps[bk]], writes=[b_q])
                          elif g == 1:
                              S.op('act', ACTV(kT[:, tsl], ps[bk][:, :], AF.Copy), reads=[b_ps[bk]], writes=[b_k])
                          elif g == 2:
                              S.op('act', ACTV(vT[:, tsl], ps[bk][:, :], AF.Copy), reads=[b_ps[bk]], writes=[b_v])
                          else:
                              S.op('act', ACTV(zs[:, tsl], ps[bk][:, :], AF.Silu), reads=[b_ps[bk]], writes=[b_z])

              def emit_attn(hp, fillers=None):
                  tick = {'n': 0}
                  for pi, r in enumerate(PATTERNS):
                      nseg = NT // r
                      def build_vaug(r=r, nseg=nseg):
                          for T0 in range(0, NT, 4):
                              bk = nextbank(APOOL)
                              for u in range(4):
                                  c, jj = divmod(T0 + u, nseg)
                                  S.op('pe', MM(ps[bk][:, u * 128:(u + 1) * 128], lhsT=tokset(vT, r, c, jj), rhs=ident[:, :]),
                                       reads=[b_v, b_const], writes=[b_ps[bk]], sig=(u == 3))
                              psv = ps[bk][:].rearrange("p (u b f) -> p u b f", b=2, f=64)
                              dst = vaug[:, T0:T0 + 4, :].rearrange("p t (b f) -> p t b f", f=64)[:, :, 0:3:2, :]
                              if (T0 // 4) % 2 == 0:
                                  S.op('act', ACTV(dst, psv, AF.Copy), reads=[b_ps[bk]], writes=[b_vaug])
                              else:
                                  S.op('dve', CP(dst, psv), reads=[b_ps[bk]], writes=[b_vaug])
                      if r == 1:
                          groups = [[(0, jj) for jj in range(g * 4, g * 4 + 4)] for g in range(8)]
                      elif r == 4:
                          groups = [[(c, jj) for c in range(4)] for jj in range(8)]
                      else:
                          groups = [[(c, jj) for c in range(c0, c0 + 4)] for jj in range(2) for c0 in range(0, 16, 4)]
                      tasks = [(hh, gi) for hh in range(2) for gi in range(len(groups))]
                      etslot = {}
                      for hh in range(2):
                          h = hp * 2 + hh
                          slope = 2.0 ** (-8.0 * (h + 1) / 16.0)
                          es_ = cnt['et'] % 3
                          cnt['et'] += 1
                          etslot[hh] = es_
                          S.op('act', ACTV(tmpE[:], dcl[:].rearrange("p a b -> p (a b)"), AF.Exp, scale=-slope * r),
                               reads=[b_const], writes=[b_tmpE])
                          for u2 in range(2):
                              S.op('dve', TT(etab[es_][:, u2, :], tmpE[:], m01[:].rearrange("p a b -> p (a b)"), ALU.mult),
                                   reads=[b_tmpE, b_const], writes=[b_etab[es_]])

                      def emit_st(task):
                          hh, gi = task
                          grp = groups[gi]
                          qh = qA0 if hh == 0 else qB0
                          slots = []
                          for pair2 in range(2):
                              bk = nextbank(APOOL)
                              v4 = ps[bk][:].rearrange("p (u h i) -> p u h i", h=2, i=128)
                              mms = []
                              for u2 in range(2):
                                  c, jj = grp[pair2 * 2 + u2]
                                  qs = tokset(qh, r, c, jj)
                                  if jj > 0:
                                      mms.append(MM(v4[:, u2, 0, :], lhsT=tokset(kT, r, c, jj - 1), rhs=qs))
                                  mms.append(MM(v4[:, u2, 1, :], lhsT=tokset(kT, r, c, jj), rhs=qs))
                              for i, m in enumerate(mms):
                                  S.op('pe', m, reads=[b_k, b_q], writes=[b_ps[bk]], sig=(i == len(mms) - 1))
                              sl = cnt['st'] % NSL
                              cnt['st'] += 1
                              S.op('act', ACTV(P0[sl][:], ps[bk][:, :], AF.Exp), reads=[b_ps[bk]], writes=[b_P0[sl]])
                              S.op('pool' if cnt['st'] % 3 == 0 else 'dve',
                                   TT(PT[sl][:], P0[sl][:], etab[etslot[hh]][:].rearrange("p a b -> p (a b)"), ALU.mult),
                                   reads=[b_P0[sl], b_etab[etslot[hh]]], writes=[b_PT[sl]])
                              slots.append(sl)
                          return slots

                      def emit_pv(task, slots):
                          hh, gi = task
                          grp = groups[gi]
                          ob = nextbank(OPOOL)
                          cols = slice(0, 128) if hh == 0 else slice(64, 192)
                          mms = []
                          for u in range(4):
                              c, jj = grp[u]
                              T = c * nseg + jj
                              ptv = PT[slots[u // 2]][:].rearrange("p (u h i) -> p u h i", h=2, i=128)
                              o_ap = ps[ob][:, u * 128:(u + 1) * 128]
                              if jj > 0:
                                  mms.append(MM(o_ap, lhsT=vaug[:, T - 1, cols], rhs=ptv[:, u % 2, 0, :], start=True, stop=False))
                                  mms.append(MM(o_ap, lhsT=vaug[:, T, cols], rhs=ptv[:, u % 2, 1, :], start=False, stop=True))
                              else:
                                  mms.append(MM(o_ap, lhsT=vaug[:, T, cols], rhs=ptv[:, u % 2, 1, :], start=True, stop=True))
                          for i, m in enumerate(mms):
                              S.op('pe', m, reads=[b_vaug, b_PT[slots[0]], b_PT[slots[1]]], writes=[b_ps[ob]],
                                   sig=(i == len(mms) - 1))
                          psv = ps[ob][:].rearrange("p (u i) -> p u i", i=128)
                          a = acc[hh]
                          if r == 1:
                              accv = a[:, gi * 512:(gi + 1) * 512].rearrange("p (u i) -> p u i", i=128)
                              blks = [gi]
                          elif r == 4:
                              accv = a[:, gi * 512:(gi + 1) * 512].rearrange("p (i c) -> p c i", c=4)
                              blks = [gi]
                          else:
                              jj = grp[0][1]
                              c0 = grp[0][0]
                              accv = a[:, jj * 2048:(jj + 1) * 2048].rearrange("p (i c) -> p c i", c=16)[:, c0:c0 + 4, :]
                              blks = [jj * 4 + i for i in range(4)]
                          bb = [b_acc[hh][i] for i in blks]
                          if pi == 0:
                              S.op('dve', CP(accv, psv), reads=[b_ps[ob]], writes=bb)
                          else:
                              S.op('dve', TT(accv, psv, accv, ALU.add), reads=[b_ps[ob]] + bb, writes=bb)

                      pend = [emit_st(tasks[0])]
                      build_vaug()
                      pend.append(emit_st(tasks[1]))
                      for ti in range(len(tasks)):
                          if ti + 2 < len(tasks):
                              pend.append(emit_st(tasks[ti + 2]))
                          emit_pv(tasks[ti], pend.pop(0))
                          tick['n'] += 1
                          if fillers and tick['n'] % 5 == 0:
                              fillers.pop(0)()

              def norm_pieces(hp):
                  pieces = []
                  for hh in range(2):
                      npart = slice(0, 64) if hh == 0 else slice(64, 128)
                      dpart = slice(64, 128) if hh == 0 else slice(0, 64)

                      def ln_piece(hh=hh, dpart=dpart):
                          S.op('act', ACTV(acc[hh][dpart, :], acc[hh][dpart, :], AF.Ln), reads=b_acc[hh], writes=b_acc[hh])
                      pieces.append(ln_piece)
                      for tb in range(8):
                          def blk_piece(hh=hh, tb=tb, npart=npart, dpart=dpart):
                              tsl = slice(tb * 512, (tb + 1) * 512)
                              ns = cnt['nt'] % 2
                              cnt['nt'] += 1
                              S.op('act', ACTV(ntmp[ns][npart, :], acc[hh][dpart, tsl], AF.Exp, scale=-1.0),
                                   reads=[b_acc[hh][tb]], writes=[b_ntmp[ns]])
                              S.op('dve', TT(ntmp[ns][npart, :], ntmp[ns][npart, :], acc[hh][npart, tsl], ALU.mult),
                                   reads=[b_acc[hh][tb], b_ntmp[ns]], writes=[b_ntmp[ns]])
                              S.op('dve', TT(oaT[npart, tsl], ntmp[ns][npart, :], zs[npart, tsl], ALU.mult),
                                   reads=[b_ntmp[ns], b_z], writes=[b_oaT])
                          pieces.append(blk_piece)

                  def spill_piece(hp=hp):
                      S.dma('pool', DMA(oa_scr[hp * 128:(hp + 1) * 128, :], oaT[:, :]), 'D_sp', reads=[b_oaT])
                  pieces.append(spill_piece)
                  return pieces

              emit_proj(0, (0, 1, 2, 3))
              cast_w()
              load_w(2)
              castq = []
              for hp in range(8):
                  emit_attn(hp, fillers=castq)
                  while castq:
                      castq.pop(0)()
                  if hp >= 1 and hp + 2 < 8:
                      load_w(hp + 2)
                  pieces = norm_pieces(hp)
                  if hp + 1 < 8:
                      emit_proj(hp + 1, (0, 1, 2), filler=pieces)
                  while pieces:
                      pieces.pop(0)()
                  if hp + 1 < 8:
                      emit_proj(hp + 1, (3,))
                      if hp + 2 < 8:
                          castq = cast_pieces()
              S.barrier()
              S.emit(nc, sems)

        P12.close()

        with ExitStack() as P:
          if PHASES >= 3:
              Wg = sb(P, "Wg3", [128, 8, 3072], BF16)
              b_WgS = [Buf("Wg3_%d" % i) for i in range(12)]
              Wa = sb(P, "Wa3", [128, 8, 128], BF16)
              b_Wa = Buf("Wa3")
              stage3 = [sb(P, "stage3_%d" % i, [128, 8, 256], F32) for i in range(2)]
              b_stage3 = [Buf("stage3_%d" % i) for i in range(2)]
              wal = sb(P, "wal", [128, 512], F32)
              decs = sb(P, "decs", [128, 4, NT], F32)
              b_decs = [Buf("decs%d" % i) for i in range(NT)]
              b_c3 = Buf("c3")
              b_c3g = Buf("c3g")
              qk = sb(P, "qk_sb", [128, 8, 512], BF16)
              b_qk = Buf("qk")
              acT = sb(P, "acT", [128, 512], F32)
              b_acT = Buf("acT")
              Lsb = [sb(P, "Lsb%d" % i, [128, 512], F32) for i in range(2)]
              b_L = [Buf("L%d" % i) for i in range(2)]
              epos = [sb(P, "epos%d" % i, [128, 4, 128], F32) for i in range(2)]
              eneg = [sb(P, "eneg%d" % i, [128, 4, 128], F32) for i in range(2)]
              b_ep = [Buf("ep%d" % i) for i in range(2)]
              b_en = [Buf("en%d" % i) for i in range(2)]
              qd = [sb(P, "qd%d" % i, [128, 128], BF16) for i in range(8)]
              ki = [sb(P, "ki%d" % i, [128, 128], BF16) for i in range(8)]
              atm = [sb(P, "atm%d" % i, [128, 128], BF16) for i in range(4)]
              kit = [sb(P, "kit%d" % i, [128, 128], BF16) for i in range(4)]
              vsb = [sb(P, "vsb%d" % i, [128, 256], BF16) for i in range(8)]
              zsb = [sb(P, "zsb%d" % i, [128, 256], BF16) for i in range(12)]
              b_qd = [Buf("qd%d" % i) for i in range(8)]
              b_ki = [Buf("ki%d" % i) for i in range(8)]
              b_atm = [Buf("atm%d" % i) for i in range(4)]
              b_kit = [Buf("kit%d" % i) for i in range(4)]
              b_vsb = [Buf("vsb%d" % i) for i in range(8)]
              b_zsb = [Buf("zsb%d" % i) for i in range(12)]
              state = sb(P, "state", [128, 4, 256], F32)
              stbf = sb(P, "stbf", [128, 4, 256], BF16)
              b_state = [Buf("state%d" % i) for i in range(4)]
              b_stbf = [Buf("stbf%d" % i) for i in range(4)]
              bst = [sb(P, "bst%d" % i, [128, 6], F32) for i in range(4)]
              mv = [sb(P, "mv%d" % i, [128, 4], F32) for i in range(4)]
              b_mv = [Buf("mv%d" % i) for i in range(4)]
              onf = [sb(P, "onf%d" % i, [128, 256], BF16) for i in range(4)]
              b_onf = [Buf("onf%d" % i) for i in range(4)]
              obt = [sb(P, "obt%d" % i, [128, 1024], BF16) for i in range(2)]
              b_obt = [Buf("obt%d" % i) for i in range(2)]
              obT = [sb(P, "obT%d" % i, [128, 8, 512], BF16) for i in range(2)]
              b_obT = [Buf("obT%d" % i) for i in range(2)]

              S.dma('sp', DMA(wal[:], wal_d[:, :]), 'D_c2', writes=[b_c3])
              S.op('pool', lambda e: e.memset(Wa[:], 0.0), writes=[b_Wa])
              S.op('pool', lambda e: e.memset(acT[:], 0.0), writes=[b_acT])
              S.op('pool', lambda e: e.memset(acT[0:32, :], 1.0), writes=[b_acT])
              S.dma('sp', DMA(stage3[0][:, :, 0:16], wgla_d[:, :, 3072:3088]), 'D_w0', writes=[b_stage3[0]])
              for c in range(8):
                  S.op('dve', TS(Wa[:, c, 0:16], stage3[0][:, c, 0:16], gainT[:, c:c + 1], None, ALU.mult),
                       reads=[b_stage3[0], b_const], writes=[b_Wa])
              for sl in range(12):
                  k2 = sl % 2
                  S.dma('sp', DMA(stage3[k2][:], wgla_d[:, :, sl * 256:(sl + 1) * 256]), 'D_w%d' % k2, writes=[b_stage3[k2]])
                  for c in range(8):
                      if c % 2 == 0:
                          S.op('act', ACTV(Wg[:, c, sl * 256:(sl + 1) * 256], stage3[k2][:, c, :], AF.Copy, scale=gainT[:, c:c + 1]),
                               reads=[b_stage3[k2], b_const], writes=[b_WgS[sl]])
                      else:
                          S.op('dve', TS(Wg[:, c, sl * 256:(sl + 1) * 256], stage3[k2][:, c, :], gainT[:, c:c + 1], None, ALU.mult),
                               reads=[b_stage3[k2], b_const], writes=[b_WgS[sl]])

              def emit_B(tb):
                  tsl = slice(tb * 512, (tb + 1) * 512)
                  bk = nextbank((0, 1, 2, 3))
                  for c in range(8):
                      S.op('pe', MM(ps[bk][:, :], lhsT=Wa[:, c, :], rhs=uT[:, c, tsl], start=(c == 0), stop=(c == 7)),
                           reads=[b_Wa, b_uT], writes=[b_ps[bk]], sig=(c == 7))
                  S.op('dve', CP(acT[0:16, :], ps[bk][0:16, :]), reads=[b_ps[bk]], writes=[b_acT])
                  for j in range(8):
                      bk = nextbank((0, 1, 2, 3))
                      for c in range(8):
                          S.op('pe', MM(ps[bk][:, :], lhsT=Wg[:, c, j * 128:(j + 1) * 128], rhs=uT[:, c, tsl],
                                        start=(c == 0), stop=(c == 7)),
                               reads=[b_WgS[j // 2], b_uT], writes=[b_ps[bk]], sig=(c == 7))
                      if j % 2 == 0:
                          S.op('act', ACTV(qk[:, j, :], ps[bk][:, :], AF.Copy), reads=[b_ps[bk]], writes=[b_qk])
                      else:
                          S.op('dve', CP(qk[:, j, :], ps[bk][:, :]), reads=[b_ps[bk]], writes=[b_qk])

              GP = (0, 1, 2, 3)
              OP = (4, 5, 6, 7)

              def emit_G1(t):
                  tt = t % 4
                  s2 = t % 2
                  csl = slice(tt * 128, (tt + 1) * 128)
                  bk = nextbank(GP)
                  S.op('pe', MM(ps[bk][:, :], lhsT=acT[:, csl], rhs=wal[:, :]), reads=[b_acT, b_c3], writes=[b_ps[bk]])
                  S.op('act', ACTV(Lsb[s2][:], ps[bk][:, :], AF.Exp, scale=-1.0), reads=[b_ps[bk]], writes=[b_L[s2]])
                  S.op('act', ACTV(Lsb[s2][:], Lsb[s2][:], AF.Ln, bias=onec[:, 0:1]), reads=[b_L[s2], b_const], writes=[b_L[s2]])

              def emit_G3(t):
                  s2 = t % 2
                  bk = nextbank(GP)
                  for h in range(4):
                      S.op('pe', MM(ps[bk][:, h * 128:(h + 1) * 128], lhsT=Lsb[s2][:, h * 128:(h + 1) * 128], rhs=tri[:, :]),
                           reads=[b_L[s2], b_const], writes=[b_ps[bk]], sig=(h == 3))
                  psb = ps[bk][:].rearrange("p (h i) -> p h i", i=128)
                  S.op('act', ACTV(epos[s2][:], psb, AF.Exp), reads=[b_ps[bk]], writes=[b_ep[s2]])
                  S.op('act', ACTV(eneg[s2][:], psb, AF.Exp, scale=-1.0), reads=[b_ps[bk]], writes=[b_en[s2]])
                  S.op('act', ACTV(decs[:, :, t], psb[:, :, 127], AF.Exp), reads=[b_ps[bk]], writes=[b_decs[t]])

              def emit_G5(t):
                  tt = t % 4
                  s2 = t % 2
                  csl = slice(tt * 128, (tt + 1) * 128)
                  for h in range(4):
                      x = s2 * 4 + h
                      S.op('dve', STT(qd[x][:], qk[:, h, csl], 128.0 ** -0.5, epos[s2][:, h, :], ALU.mult, ALU.mult),
                           reads=[b_qk, b_ep[s2]], writes=[b_qd[x]])
                      S.op('dve', TT(ki[x][:], qk[:, 4 + h, csl], eneg[s2][:, h, :], ALU.mult),
                           reads=[b_qk, b_en[s2]], writes=[b_ki[x]])

              vz_pending = []

              def emit_VZ_silu():
                  while vz_pending:
                      x, bk = vz_pending.pop(0)
                      S.op('act', ACTV(zsb[x][:], ps[bk][:, 256:512], AF.Silu), reads=[b_ps[bk]], writes=[b_zsb[x]])

              def emit_VZ(t, heads, defer_silu=False):
                  s2 = t % 2
                  for h in heads:
                      x = s2 * 4 + h
                      bk = nextbank(GP)
                      for c in range(8):
                          S.op('pe', MM(ps[bk][:, :], lhsT=uT[:, c, t * 128:(t + 1) * 128],
                                        rhs=Wg[:, c, 1024 + h * 512:1024 + (h + 1) * 512], start=(c == 0), stop=(c == 7)),
                               reads=[b_WgS[4 + 2 * h], b_WgS[5 + 2 * h], b_uT], writes=[b_ps[bk]], sig=(c == 7))
                      if h < 2:
                          S.op('dve', CP(vsb[x][:], ps[bk][:, 0:256]), reads=[b_ps[bk]], writes=[b_vsb[x]])
                          vz_pending.append(((t % 3) * 4 + h, bk))
                      else:
                          zx = (t % 3) * 4 + h
                          S.op('act', ACTV(zsb[zx][:], ps[bk][:, 256:512], AF.Silu), reads=[b_ps[bk]], writes=[b_zsb[zx]])
                          S.op('act', ACTV(vsb[x][:], ps[bk][:, 0:256], AF.Copy), reads=[b_ps[bk]], writes=[b_vsb[x]])
                  if not defer_silu:
                      emit_VZ_silu()

              def emit_H1(t):
                  s2 = t % 2
                  abk = []
                  for h in range(4):
                      x = s2 * 4 + h
                      bk = nextbank(GP)
                      abk.append(bk)
                      S.op('pe', MM(ps[bk][:, 0:128], lhsT=ki[x][:], rhs=qd[x][:]), reads=[b_ki[x], b_qd[x]],
                           writes=[b_ps[bk]], sig=False)
                      S.op('pe', MM(ps[bk][:, 128:256], lhsT=ki[x][:], rhs=ident[:, :]), reads=[b_ki[x], b_const],
                           writes=[b_ps[bk]])
                  for h in range(4):
                      bk = abk[h]
                      S.op('dve', TT(atm[h][:], ps[bk][:, 0:128], m01[:, 1, :], ALU.mult), reads=[b_ps[bk], b_const],
                           writes=[b_atm[h]])
                      S.op('act', ACTV(kit[h][:], ps[bk][:, 128:256], AF.Copy), reads=[b_ps[bk]], writes=[b_kit[h]])

              obk_of = {}

              def emit_H2a(t):
                  s2 = t % 2
                  obk = []
                  obk_of[t] = obk
                  for h in range(4):
                      x = s2 * 4 + h
                      bk = nextbank(OP)
                      obk.append(bk)
                      if t > 0:
                          S.op('pe', MM(ps[bk][:, 0:256], lhsT=atm[h][:], rhs=vsb[x][:], start=True, stop=False),
                               reads=[b_atm[h], b_vsb[x]], writes=[b_ps[bk]], sig=False)
                          S.op('pe', MM(ps[bk][:, 0:256], lhsT=qd[x][:], rhs=stbf[:, h, :], start=False, stop=True),
                               reads=[b_qd[x], b_stbf[h]], writes=[b_ps[bk]], sig=False)
                      else:
                          S.op('pe', MM(ps[bk][:, 0:256], lhsT=atm[h][:], rhs=vsb[x][:]),
                               reads=[b_atm[h], b_vsb[x]], writes=[b_ps[bk]], sig=False)
                      S.op('pe', MM(ps[bk][:, 256:512], lhsT=kit[h][:], rhs=vsb[x][:]),
                           reads=[b_kit[h], b_vsb[x]], writes=[b_ps[bk]])

              def emit_ST(t):
                  obk = obk_of[t]
                  for h in range(4):
                      bk = obk[h]
                      if t > 0:
                          S.op('dve', STT(state[:, h, :], state[:, h, :], decs[:, h, t - 1:t], ps[bk][:, 256:512], ALU.mult, ALU.add),
                               reads=[b_ps[bk], b_decs[t - 1], b_state[h]], writes=[b_state[h]])
                      else:
                          S.op('dve', CP(state[:, h, :], ps[bk][:, 256:512]), reads=[b_ps[bk]], writes=[b_state[h]])
                      S.op('dve', TS(stbf[:, h, :], state[:, h, :], decs[:, h, t:t + 1], None, ALU.mult),
                           reads=[b_state[h], b_decs[t]], writes=[b_stbf[h]])

              def emit_gnA(t):
                  obk = obk_of[t]
                  for h in range(4):
                      bk = obk[h]
                      S.op('dve', lambda e, h=h, bk=bk: e.bn_stats(out=bst[h][:], in_=ps[bk][:, 0:256]),
                           reads=[b_ps[bk]], writes=[b_mv[h]])
                      S.op('dve', lambda e, h=h: e.bn_aggr(out=mv[h][:, 0:2], in_=bst[h][:]), reads=[b_mv[h]], writes=[b_mv[h]])
                      S.op('dve', TS(mv[h][:, 1:2], mv[h][:, 1:2], 1e-5, None, ALU.add),
                           reads=[b_mv[h]], writes=[b_mv[h]])

              def emit_gnB(t):
                  s2 = t % 2
                  obk = obk_of.pop(t)
                  for h in range(4):
                      S.op('act', ACTV(mv[h][:, 1:2], mv[h][:, 1:2], AF.Sqrt), reads=[b_mv[h]], writes=[b_mv[h]])
                  for h in range(4):
                      S.op('dve', lambda e, h=h: e.reciprocal(out=mv[h][:, 1:2], in_=mv[h][:, 1:2]),
                           reads=[b_mv[h]], writes=[b_mv[h]])
                      S.op('dve', TS(mv[h][:, 2:3], mv[h][:, 0:1], mv[h][:, 1:2], -1.0, ALU.mult, ALU.mult),
                           reads=[b_mv[h]], writes=[b_mv[h]])
                  for h in range(4):
                      x = (t % 3) * 4 + h
                      bk = obk[h]
                      S.op('dve', TS(onf[h][:], ps[bk][:, 0:256], mv[h][:, 0:1], mv[h][:, 1:2], ALU.subtract, ALU.mult),
                           reads=[b_ps[bk], b_mv[h]], writes=[b_onf[h]])
                      S.op('dve', TT(obt[s2][:, h * 256:(h + 1) * 256], onf[h][:], zsb[x][:], ALU.mult),
                           reads=[b_onf[h], b_zsb[x]], writes=[b_obt[s2]])

              def emit_T(t):
                  tb, tt = divmod(t, 4)
                  s2 = t % 2
                  ot = tb % 2
                  csl = slice(tt * 128, (tt + 1) * 128)
                  for half in range(2):
                      bk = nextbank(GP)
                      for cc in range(4):
                          c = half * 4 + cc
                          S.op('pe', MM(ps[bk][:, cc * 128:(cc + 1) * 128], lhsT=obt[s2][:, c * 128:(c + 1) * 128],
                                        rhs=ident[:, :]), reads=[b_obt[s2], b_const], writes=[b_ps[bk]], sig=(cc == 3))
                      psv = ps[bk][:].rearrange("p (a b) -> p a b", b=128)
                      if half == 0:
                          S.op('act', ACTV(obT[ot][:, half * 4:half * 4 + 4, csl], psv, AF.Copy),
                               reads=[b_ps[bk]], writes=[b_obT[ot]])
                      else:
                          S.op('dve', CP(obT[ot][:, half * 4:half * 4 + 4, csl], psv), reads=[b_ps[bk]], writes=[b_obT[ot]])
                  if tt == 3:
                      tsl = slice(tb * 512, (tb + 1) * 512)
                      S.dma('pool', DMA(ob_v[:, :, tsl], obT[ot][:]), 'D_sp%d' % ot, reads=[b_obT[ot]])

              emit_B(0)
              emit_G1(0)
              emit_VZ(0, (0, 1))
              emit_G3(0)
              emit_VZ(0, (2, 3))
              emit_G5(0)
              for t in range(NT):
                  nx = t + 1 < NT
                  emit_H1(t)
                  if t > 0:
                      emit_gnA(t - 1)
                      emit_gnB(t - 1)
                  if nx:
                      if (t + 1) % 4 == 0:
                          emit_B((t + 1) // 4)
                      emit_G1(t + 1)
                      emit_VZ(t + 1, (0, 1))
                      emit_G3(t + 1)
                  emit_H2a(t)
                  emit_ST(t)
                  if nx:
                      emit_G5(t + 1)
                      emit_VZ(t + 1, (2, 3))
                  if t > 0:
                      emit_T(t - 1)
              emit_gnA(NT - 1)
              emit_gnB(NT - 1)
              emit_T(NT - 1)
              S.barrier()
              S.emit(nc, sems)

        with ExitStack() as P:
          if PHASES >= 4:
              Wga = sb(P, "Wga", [128, 8, 1024], BF16)
              Wgb = sb(P, "Wgb", [128, 8, 1024], BF16)
              Woa = sb(P, "Woa", [128, 8, 1024], BF16)
              Wob = sb(P, "Wob", [128, 8, 1024], BF16)
              Wo = sb(P, "Wo", [128, 8, 1024], BF16)
              b_W4 = Buf("W4")
              stage4 = [sb(P, "stage4_%d" % i, [128, 512], F32) for i in range(2)]
              b_stage4 = [Buf("stage4_%d" % i) for i in range(2)]
              fg = sb(P, "fg", [128, 1024], F32)
              b_c4 = Buf("c4")
              oab = sb(P, "oab", [128, 8, 512], BF16)
              obb = sb(P, "obb", [128, 8, 512], BF16)
              b_oab, b_obb = Buf("oab"), Buf("obb")
              mT = sb(P, "mT", [128, 8, 512], BF16)
              b_mT = Buf("mT")
              sg = [sb(P, "sg%d" % i, [128, 512], F32) for i in range(4)]
              b_sg = [Buf("sg%d" % i) for i in range(4)]
              xr = [sb(P, "xr%d" % i, [128, 1024], F32) for i in range(2)]
              b_xr = [Buf("xr%d" % i) for i in range(2)]
              hs = [sb(P, "hs%d" % i, [128, 1024], F32) for i in range(2)]
              b_hs = [Buf("hs%d" % i) for i in range(2)]
              junk = sb(P, "junk4", [128, 1024], BF16)
              b_junk = Buf("junk4")
              s4 = sb(P, "s4", [128, NT], F32)
              b_s4 = [Buf("s4_%d" % i) for i in range(NT)]

              S.dma('sp', DMA(fg[:], fg_d[:, :]), 'D_c4', writes=[b_c4])
              b_Wd = {id(Woa): Buf("W4oa"), id(Wob): Buf("W4ob"), id(Wga): Buf("W4ga"), id(Wgb): Buf("W4gb"), id(Wo): Buf("W4o")}
              W4spec = ((Woa, woa_d, 0, None), (Wob, wob_d, 0, ggT), (Wga, wg_d, 0, gainT), (Wgb, wg_d, 1024, gainT))
              b_Wh = {}
              for (wt, _s, _n, _g) in W4spec + ((Wo, wo_d, 0, None),):
                  for k2 in range(2):
                      b_Wh[(id(wt), k2)] = Buf("W4_%d_%d" % (len(b_Wh), k2))
              ldcnt = {'n': 0}

              def load_half(wt, src, ncol, use_gain, k2):
                  for c in range(8):
                      sl = ldcnt['n'] % 2
                      ldcnt['n'] += 1
                      S.dma('sp', DMA(stage4[sl][:], src[:, c, ncol + k2 * 512:ncol + (k2 + 1) * 512]), 'D_w%d' % sl,
                            writes=[b_stage4[sl]])
                      dst = wt[:, c, k2 * 512:(k2 + 1) * 512]
                      bw = b_Wh[(id(wt), k2)]
                      if sl == 0:
                          if use_gain is not None:
                              S.op('act', ACTV(dst, stage4[sl][:], AF.Copy, scale=use_gain[:, c:c + 1]),
                                   reads=[b_stage4[sl], b_const, b_const3], writes=[bw])
                          else:
                              S.op('act', ACTV(dst, stage4[sl][:], AF.Copy), reads=[b_stage4[sl]], writes=[bw])
                      else:
                          if use_gain is not None:
                              S.op('dve', TS(dst, stage4[sl][:], use_gain[:, c:c + 1], None, ALU.mult),
                                   reads=[b_stage4[sl], b_const, b_const3], writes=[bw])
                          else:
                              S.op('dve', CP(dst, stage4[sl][:]), reads=[b_stage4[sl]], writes=[bw])

              S.dma('sp', DMA(oab[:], oa_v[:, :, 0:512]), 'D_ld0', writes=[b_oab])
              S.dma('sp', DMA(obb[:], ob_v[:, :, 0:512]), 'D_ld1', writes=[b_obb])
              for k2 in range(2):
                  for spec in W4spec:
                      load_half(*spec, k2)
              for k2 in range(2):
                  load_half(Wo, wo_d, 0, None, k2)
              for tb in range(8):
                  tsl = slice(tb * 512, (tb + 1) * 512)
                  for m in range(8):
                      msl = slice(m * 128, (m + 1) * 128)
                      banks = []
                      for (wt, rhs_t, rb) in ((Woa, oab, b_oab), (Wob, obb, b_obb), (Wga, None, b_uT), (Wgb, None, b_uT)):
                          bk = nextbank()
                          for c in range(8):
                              rhs = rhs_t[:, c, :] if rhs_t is not None else uT[:, c, tsl]
                              S.op('pe', MM(ps[bk][:, :], lhsT=wt[:, c, msl], rhs=rhs, start=(c == 0), stop=(c == 7)),
                                   reads=[b_Wh[(id(wt), m // 4)], rb], writes=[b_ps[bk]], sig=(c == 7))
                          banks.append(bk)
                      S.op('act', ACTV(sg[0][:], ps[banks[2]][:, :], AF.Sigmoid, bias=bgT[:, m:m + 1]),
                           reads=[b_ps[banks[2]], b_const2], writes=[b_sg[0]])
                      S.op('act', ACTV(sg[1][:], ps[banks[3]][:, :], AF.Sigmoid, bias=bgT[:, 8 + m:9 + m]),
                           reads=[b_ps[banks[3]], b_const2], writes=[b_sg[1]])
                      S.op('dve', TT(sg[2][:], ps[banks[0]][:, :], sg[0][:], ALU.mult), reads=[b_ps[banks[0]], b_sg[0]],
                           writes=[b_sg[2]])
                      S.op('dve', TT(sg[3][:], ps[banks[1]][:, :], sg[1][:], ALU.mult), reads=[b_ps[banks[1]], b_sg[1]],
                           writes=[b_sg[3]])
                      S.op('pool', TT(mT[:, m, :], sg[2][:], sg[3][:], ALU.add), reads=[b_sg[2], b_sg[3]], writes=[b_mT])
                  if tb + 1 < 8:
                      nsl = slice((tb + 1) * 512, (tb + 2) * 512)
                      S.dma('sp', DMA(oab[:], oa_v[:, :, nsl]), 'D_ld0', writes=[b_oab])
                      S.dma('sp', DMA(obb[:], ob_v[:, :, nsl]), 'D_ld1', writes=[b_obb])
                  for tt in range(4):
                      t = tb * 4 + tt
                      s2 = t % 2
                      S.dma('sp', DMA(xr[s2][:], x_d[t * 128:(t + 1) * 128, :]), 'D_xr%d' % s2, writes=[b_xr[s2]])
                      for half in range(2):
                          bk = nextbank()
                          for m in range(8):
                              S.op('pe', MM(ps[bk][:, :], lhsT=mT[:, m, tt * 128:(tt + 1) * 128],
                                            rhs=Wo[:, m, half * 512:(half + 1) * 512], start=(m == 0), stop=(m == 7)),
                                   reads=[b_mT, b_Wh[(id(Wo), half)]], writes=[b_ps[bk]], sig=(m == 7))
                          S.op('dve', TT(hs[s2][:, half * 512:(half + 1) * 512], ps[bk][:, :], xr[s2][:, half * 512:(half + 1) * 512],
                                         ALU.add), reads=[b_ps[bk], b_xr[s2]], writes=[b_hs[s2]])
                      S.op('act', ACTV(junk[:], hs[s2][:], AF.Square, accum_out=s4[:, t:t + 1]),
                           reads=[b_hs[s2]], writes=[b_junk, b_s4[t]])
                      S.op('dve', TS(s4[:, t:t + 1], s4[:, t:t + 1], 1.0 / DM, 1e-6, ALU.mult, ALU.add),
                           reads=[b_s4[t]], writes=[b_s4[t]])
                      S.op('act', ACTV(s4[:, t:t + 1], s4[:, t:t + 1], AF.Sqrt), reads=[b_s4[t]], writes=[b_s4[t]])
                      S.op('dve', lambda e, t=t: e.reciprocal(out=s4[:, t:t + 1], in_=s4[:, t:t + 1]),
                           reads=[b_s4[t]], writes=[b_s4[t]])
                      S.op('dve', STT(hs[s2][:], hs[s2][:], s4[:, t:t + 1], fg[:], ALU.mult, ALU.mult),
                           reads=[b_hs[s2], b_s4[t], b_c4], writes=[b_hs[s2]])
                      ev = S.dma('pool', DMA(out_d[t * 128:(t + 1) * 128, :], hs[s2][:]), 'D_o%d' % s2, reads=[b_hs[s2]])
                      final_evs.append(ev)
              S.barrier()
              S.emit(nc, sems)
    return nc


def _lay8(w):
    n = w.shape[1]
    return np.ascontiguousarray(w.reshape(8, 128, n).transpose(1, 0, 2))


_NC_CACHE = {}


def kernel(x, norm_gain, w_in, b_gate, w_alpha, b_alpha, gla_norm_gain,
           w_out_attn, w_out_gla, w_out, final_norm_gain):
    f = np.float32
    x = np.asarray(x, f)
    W = np.asarray(w_in, f)[0]
    gainT = np.ascontiguousarray(np.asarray(norm_gain, f)[0].reshape(8, 128).T)
    watt = np.stack([
        _lay8(np.concatenate([W[:, hp * 128:(hp + 1) * 128], W[:, 1024 + hp * 128:1024 + (hp + 1) * 128],
                              W[:, 2048 + hp * 128:2048 + (hp + 1) * 128], W[:, 3072 + hp * 128:3072 + (hp + 1) * 128]], axis=1))
        for hp in range(8)])
    cols = [W[:, 4096:4608], W[:, 4608:5120]]
    for h in range(4):
        cols.append(W[:, 5120 + h * 256:5120 + (h + 1) * 256])
        cols.append(W[:, 6144 + h * 256:6144 + (h + 1) * 256])
    cols.append(W[:, 7168:7184])
    wgla = _lay8(np.concatenate(cols, axis=1))
    wg = _lay8(W[:, 7184:9232])
    woa = _lay8(np.asarray(w_out_attn, f)[0])
    wob = _lay8(np.asarray(w_out_gla, f)[0])
    wo = _lay8(np.asarray(w_out, f)[0])
    bgT = np.ascontiguousarray(np.asarray(b_gate, f)[0].reshape(16, 128).T)
    wal = np.zeros((128, 512), f)
    wal[0:16] = np.asarray(w_alpha, f)[0]
    wal[16] = np.asarray(b_alpha, f)[0]
    ggT = np.ascontiguousarray(np.asarray(gla_norm_gain, f)[0].reshape(8, 128).T)
    fg = np.ascontiguousarray(np.broadcast_to(np.asarray(final_norm_gain, f)[None, :], (128, 1024)))

    if 'nc' not in _NC_CACHE:
        _NC_CACHE['nc'] = build()
    nc = _NC_CACHE['nc']
    shared = dict(gainT=gainT, watt=watt, wgla=wgla, wg=wg, woa=woa, wob=wob, wo=wo, bgT=bgT, wal=wal, ggT=ggT, fg=fg)
    in_maps = [dict(shared, x=np.ascontiguousarray(x[b])) for b in range(8)]
    res = run_bass_kernel_spmd(nc, in_maps, core_ids=list(range(8)))
    if DEBUG:
        kernel.dbg = res.results
    return np.stack([np.asarray(res.results[b]["out"], f) for b in range(8)], axis=0)
```

```python
import numpy as np
from contextlib import ExitStack
import concourse.bass as bass
import concourse.mybir as mybir
from concourse.bass_utils import run_bass_kernel_spmd

F32 = mybir.dt.float32
BF16 = mybir.dt.bfloat16
ALU = mybir.AluOpType
AF = mybir.ActivationFunctionType

SEQ = 4096
DM = 1024
NT = 32
PATTERNS = (1, 4, 16)
DEBUG = False
PHASES = 4


class Buf:
    def __init__(self, name):
        self.name = name
        self.w = None
        self.r = {}


class Sched:
    ENG = ('pe', 'act', 'dve', 'pool', 'sp')

    def __init__(self):
        self.q = {e: [] for e in self.ENG}
        self.cnt = {'E_' + e: 0 for e in self.ENG}
        self.known = {e: {} for e in self.ENG}
        self.pend_r = {e: [] for e in self.ENG}
        self.pend_w = {e: [] for e in self.ENG}

    def _deps(self, reads, writes):
        evs = []
        for b in reads:
            if b.w is not None:
                evs.append(b.w)
        for b in writes:
            if b.w is not None:
                evs.append(b.w)
            evs.extend(b.r.items())
        return evs

    def _waits(self, eng, evs):
        need = {}
        for (k, v) in evs:
            if v > need.get(k, 0):
                need[k] = v
        out = []
        for k, v in need.items():
            if self.known[eng].get(k, 0) >= v:
                continue
            self.known[eng][k] = v
            out.append((k, v))
        return out

    def _record(self, ev, reads, writes):
        for b in writes:
            b.w = ev
            b.r = {}
        for b in reads:
            if ev[1] > b.r.get(ev[0], 0):
                b.r[ev[0]] = ev[1]

    def op(self, eng, fn, reads=(), writes=(), sig=True):
        writes = list(writes) + [b for b in reads if b.name.startswith("ps") and b not in writes]
        waits = self._waits(eng, self._deps(reads, writes))
        key = 'E_' + eng
        if not sig:
            self.pend_r[eng].extend(reads)
            self.pend_w[eng].extend(writes)
            self.q[eng].append((fn, waits, None))
            return None
        self.cnt[key] += 1
        ev = (key, self.cnt[key])
        self.q[eng].append((fn, waits, (key, 1)))
        self._record(ev, list(reads) + self.pend_r[eng], list(writes) + self.pend_w[eng])
        self.pend_r[eng] = []
        self.pend_w[eng] = []
        return ev

    def dma(self, eng, fn, semkey, reads=(), writes=()):
        waits = self._waits(eng, self._deps(reads, writes))
        self.cnt[semkey] = self.cnt.get(semkey, 0) + 16
        ev = (semkey, self.cnt[semkey])
        self.q[eng].append((fn, waits, (semkey, 16)))
        self._record(ev, reads, writes)
        return ev

    def barrier(self):
        evs = [(k, v) for k, v in self.cnt.items() if v > 0]
        for e in self.ENG:
            w = self._waits(e, evs)
            if w:
                self.q[e].append((None, w, None))

    def emit(self, nc, sems):
        q = self.q
        self.q = {e: [] for e in self.ENG}
        with nc.Block() as block:
            def run(engname):
                def body(e):
                    for fn, waits, inc in q[engname]:
                        for (k, v) in waits:
                            e.wait_ge(sems[k], v)
                        if fn is None:
                            continue
                        ins = fn(e)
                        if inc is not None:
                            ins.then_inc(sems[inc[0]], inc[1])
                return body
            block.tensor(run('pe'))
            block.scalar(run('act'))
            block.vector(run('dve'))
            block.gpsimd(run('pool'))
            block.sync(run('sp'))


def MM(out, lhsT, rhs, start=True, stop=True):
    return lambda e: e.matmul(out, lhsT=lhsT, rhs=rhs, start=start, stop=stop)


def ACTV(out, in_, func, **kw):
    return lambda e: e.activation(out=out, in_=in_, func=func, **kw)


def TT(out, in0, in1, op):
    return lambda e: e.tensor_tensor(out=out, in0=in0, in1=in1, op=op)


def TS(out, in0, s1, s2, op0, op1=None):
    if op1 is None:
        return lambda e: e.tensor_scalar(out=out, in0=in0, scalar1=s1, scalar2=None, op0=op0)
    return lambda e: e.tensor_scalar(out=out, in0=in0, scalar1=s1, scalar2=s2, op0=op0, op1=op1)


def STT(out, in0, scalar, in1, op0, op1):
    return lambda e: e.scalar_tensor_tensor(out=out, in0=in0, scalar=scalar, in1=in1, op0=op0, op1=op1)


def CP(out, in_):
    return lambda e: e.tensor_copy(out=out, in_=in_)


def DMA(out, in_):
    return lambda e: e.dma_start(out=out, in_=in_)


SEMKEYS = ['E_pe', 'E_act', 'E_dve', 'E_pool', 'E_sp',
           'D_c0', 'D_c1', 'D_c2', 'D_c3', 'D_c4', 'D_c5', 'D_x0', 'D_x1', 'D_x2', 'D_x3', 'D_x4', 'D_x5', 'D_w', 'D_w0', 'D_w1', 'D_sp', 'D_sp0', 'D_sp1',
           'D_o0', 'D_o1', 'D_ld0', 'D_ld1', 'D_xr0', 'D_xr1']


def build():
    nc = bass.Bass("TRN2", target_bir_lowering=False)

    def din(name, shape, dt=F32):
        return nc.dram_tensor(name, list(shape), dt, kind="ExternalInput").ap()

    x_d = din("x", [SEQ, DM])
    gainT_d = din("gainT", [128, 8])
    watt_d = din("watt", [8, 128, 8, 512])
    wgla_d = din("wgla", [128, 8, 3088])
    wg_d = din("wg", [128, 8, 2048])
    woa_d = din("woa", [128, 8, 1024])
    wob_d = din("wob", [128, 8, 1024])
    wo_d = din("wo", [128, 8, 1024])
    bgT_d = din("bgT", [128, 16])
    wal_d = din("wal", [128, 512])
    ggT_d = din("ggT", [128, 8])
    fg_d = din("fg", [128, 1024])
    out_d = nc.dram_tensor("out", [SEQ, DM], F32, kind="ExternalOutput").ap()
    skind = "ExternalOutput" if DEBUG else "Internal"
    oa_scr = nc.dram_tensor("oa_scr", [DM, SEQ], BF16, kind=skind).ap()
    ob_scr = nc.dram_tensor("ob_scr", [DM, SEQ], BF16, kind=skind).ap()
    uT_dbg = nc.dram_tensor("uT_dbg", [128, 8, SEQ], BF16, kind="ExternalOutput").ap() if DEBUG else None
    oa_v = oa_scr.rearrange("(c p) t -> p c t", p=128)
    ob_v = ob_scr.rearrange("(c p) t -> p c t", p=128)

    S = Sched()
    final_evs = []

    with ExitStack() as G:
        def sb(es, name, shape, dt):
            return es.enter_context(nc.sbuf_tensor("s_" + name, list(shape), dt))

        sems = {k: G.enter_context(nc.semaphore(k)) for k in SEMKEYS}
        ps = [G.enter_context(nc.psum_tensor("ps%d" % i, [128, 512], F32)) for i in range(8)]
        b_ps = [Buf("ps%d" % i) for i in range(8)]
        rot = {'i': 0}

        def nextbank(pool=(0, 1, 2, 3, 4, 5, 6, 7)):
            k = rot.get(pool, 0)
            rot[pool] = k + 1
            return pool[k % len(pool)]

        uT = sb(G, "uT", [128, 8, SEQ], BF16)
        b_uT = Buf("uT")
        ident = sb(G, "ident", [128, 128], BF16)
        dmat = sb(G, "dmat", [128, 2, 128], F32)
        m01 = sb(G, "m01", [128, 2, 128], F32)
        dcl = sb(G, "dcl", [128, 2, 128], F32)
        tri = sb(G, "tri", [128, 128], F32)
        onec = sb(G, "onec", [128, 1], F32)
        gainT = sb(G, "gainT_s", [128, 8], F32)
        bgT = sb(G, "bgT_s", [128, 16], F32)
        ggT = sb(G, "ggT_s", [128, 8], F32)
        b_const3 = Buf("const3")
        b_const = Buf("const")
        b_const2 = Buf("const2")

        S.op('pool', lambda e: e.iota(dmat[:], pattern=[[-128, 2], [1, 128]], base=128, channel_multiplier=-1,
                                      allow_small_or_imprecise_dtypes=True), writes=[b_const])
        S.op('dve', TS(m01[:], dmat[:], 0.0, None, ALU.is_ge), reads=[b_const], writes=[b_const])
        S.op('dve', TS(dcl[:], dmat[:], 128.0, None, ALU.is_le), reads=[b_const], writes=[b_const])
        S.op('dve', TT(m01[:], m01[:], dcl[:], ALU.mult), reads=[b_const], writes=[b_const])
        S.op('dve', TS(dcl[:], dmat[:], 0.0, 128.0, ALU.max, ALU.min), reads=[b_const], writes=[b_const])
        S.op('dve', TS(tri[:], dmat[:, 1, :], 0.0, -1.0 / 16.0, ALU.is_ge, ALU.mult), reads=[b_const], writes=[b_const])
        S.op('dve', TS(ident[:], dmat[:, 1, :], 0.0, None, ALU.is_equal), reads=[b_const], writes=[b_const])
        S.op('dve', lambda e: e.memset(onec[:], 1.0), writes=[b_const])
        S.dma('sp', DMA(gainT[:], gainT_d[:, :]), 'D_c0', writes=[b_const])
        S.dma('sp', DMA(bgT[:], bgT_d[:, :]), 'D_c1', writes=[b_const2])
        S.dma('sp', DMA(ggT[:], ggT_d[:, :]), 'D_c3', writes=[b_const3])

        P12 = ExitStack()
        stage = sb(P12, "stage2", [128, 8, 512], F32)
        b_stage = Buf("stage2")
        Wb = sb(P12, "Wb2", [128, 8, 512], BF16)
        b_Wb = Buf("Wb2")

        with ExitStack() as P:
            S.dma('sp', DMA(stage[:], watt_d[0]), 'D_w', writes=[b_stage])
            xt = [sb(P, "xt%d" % i, [128, DM], F32) for i in range(6)]
            b_xt = [Buf("xt%d" % i) for i in range(6)]
            xn = [sb(P, "xn%d" % i, [128, DM], BF16) for i in range(2)]
            b_xn = [Buf("xn%d" % i) for i in range(2)]
            junk = sb(P, "junk1", [128, DM], BF16)
            b_junk = Buf("junk1")
            ss = sb(P, "ss", [128, NT], F32)
            rstd = sb(P, "rstd", [128, NT], F32)
            b_ss = [Buf("ss%d" % i) for i in range(NT)]
            def p1_dma(t):
                s3 = t % 6
                S.dma('sp', DMA(xt[s3][:], x_d[t * 128:(t + 1) * 128, :]), 'D_x%d' % s3, writes=[b_xt[s3]])

            epsc = sb(P, "epsc", [128, 1], F32)
            b_epsc = Buf("epsc")
            S.op('dve', lambda e: e.memset(epsc[:], 1e-6), writes=[b_epsc])
            xn3 = [sb(P, "xn3_%d" % i, [128, DM], BF16) for i in range(3)]
            b_xn3 = [Buf("xn3_%d" % i) for i in range(3)]
            pbank = {}

            def p1_A(t):
                s3 = t % 6
                S.op('act', ACTV(junk[:], xt[s3][:], AF.Square, accum_out=ss[:, t:t + 1]),
                     reads=[b_xt[s3]], writes=[b_junk, b_ss[t]])
                S.op('act', ACTV(rstd[:, t:t + 1], ss[:, t:t + 1], AF.Sqrt, scale=1.0 / DM, bias=epsc[:, 0:1]),
                     reads=[b_ss[t], b_epsc], writes=[b_ss[t]])

            def p1_D(t):
                s3 = t % 6
                s2 = t % 3
                S.op('dve', lambda e, t=t: e.reciprocal(out=rstd[:, t:t + 1], in_=rstd[:, t:t + 1]),
                     reads=[b_ss[t]], writes=[b_ss[t]])
                S.op('dve', TS(xn3[s2][:], xt[s3][:], rstd[:, t:t + 1], None, ALU.mult),
                     reads=[b_xt[s3], b_ss[t]], writes=[b_xn3[s2]])

            def p1_P(t):
                s2 = t % 3
                bks = []
                for half in range(2):
                    bk = nextbank()
                    bks.append(bk)
                    for cc in range(4):
                        c = half * 4 + cc
                        S.op('pe', MM(ps[bk][:, cc * 128:(cc + 1) * 128], lhsT=xn3[s2][:, c * 128:(c + 1) * 128],
                                      rhs=ident[:, :]), reads=[b_xn3[s2], b_const], writes=[b_ps[bk]], sig=(cc == 3))
                pbank[t] = bks

            def p1_E(t):
                bks = pbank.pop(t)
                for half in range(2):
                    bk = bks[half]
                    psv = ps[bk][:].rearrange("p (a b) -> p a b", b=128)
                    eng = 'act' if half == 0 else 'dve'
                    fn = (ACTV(uT[:, half * 4:half * 4 + 4, t * 128:(t + 1) * 128], psv, AF.Copy) if half == 0
                          else CP(uT[:, half * 4:half * 4 + 4, t * 128:(t + 1) * 128], psv))
                    S.op(eng, fn, reads=[b_ps[bk]], writes=[b_uT])

            for t in range(6):
                p1_dma(t)
            p1_A(0)
            p1_A(1)
            p1_D(0)
            for t in range(NT):
                if t + 2 < NT:
                    p1_A(t + 2)
                if t + 1 < NT:
                    p1_D(t + 1)
                p1_P(t)
                if t > 0:
                    p1_E(t - 1)
                if t + 6 < NT:
                    p1_dma(t + 6)
                if 8 <= t < 16:
                    c = t - 8
                    S.op('act', ACTV(Wb[:, c, :], stage[:, c, :], AF.Copy, scale=gainT[:, c:c + 1]),
                         reads=[b_stage, b_const], writes=[b_Wb])
                if t == 16:
                    S.dma('sp', DMA(stage[:], watt_d[1]), 'D_w', writes=[b_stage])
            p1_E(NT - 1)
            if DEBUG:
                S.dma('sp', DMA(uT_dbg[:, :, :], uT[:]), 'D_c5', reads=[b_uT])
            S.barrier()
            S.emit(nc, sems)

        with ExitStack() as P:
          if PHASES >= 2:
              qA0 = sb(P, "qA0", [128, SEQ], BF16)
              qB0 = sb(P, "qB0", [128, SEQ], BF16)
              kT = sb(P, "kT", [128, SEQ], BF16)
              vT = sb(P, "vT", [128, SEQ], BF16)
              zs = sb(P, "zs", [128, SEQ], BF16)
              oaT = sb(P, "oaT", [128, SEQ], BF16)
              b_oaT = Buf("oaT")
              b_q, b_k, b_v, b_z = Buf("q"), Buf("k"), Buf("v"), Buf("z")
              vaug = sb(P, "vaug", [128, NT, 192], BF16)
              b_vaug = Buf("vaug")
              acc = [sb(P, "acc%d" % i, [128, SEQ], F32) for i in range(2)]
              b_acc = [[Buf("acc%d_%d" % (i, j)) for j in range(8)] for i in range(2)]
              NSL = 6
              P0 = [sb(P, "P0_%d" % i, [128, 512], BF16) for i in range(NSL)]
              PT = [sb(P, "PT_%d" % i, [128, 512], BF16) for i in range(NSL)]
              b_P0 = [Buf("P0_%d" % i) for i in range(NSL)]
              b_PT = [Buf("PT_%d" % i) for i in range(NSL)]
              etab = [sb(P, "etab%d" % i, [128, 2, 256], BF16) for i in range(3)]
              b_etab = [Buf("etab%d" % i) for i in range(3)]
              tmpE = sb(P, "tmpE", [128, 256], F32)
              b_tmpE = Buf("tmpE")
              ntmp = [sb(P, "ntmp%d" % i, [128, 512], F32) for i in range(2)]
              b_ntmp = [Buf("ntmp%d" % i) for i in range(2)]

              S.op('pool', lambda e: e.memset(qA0[:], 0.0), writes=[b_q])
              S.op('pool', lambda e: e.memset(qB0[:], 0.0), writes=[b_q])
              S.op('pool', lambda e: e.memset(vaug[:], 1.0), writes=[b_vaug])

              def tokset(tl, r, c, jj):
                  return tl[:].rearrange("p (j i r) -> p j i r", i=128, r=r)[:, jj, :, c]

              def load_w(hp):
                  S.dma('sp', DMA(stage[:], watt_d[hp]), 'D_w', writes=[b_stage])

              def cast_pieces():
                  def mk(c):
                      def f():
                          S.op('act', ACTV(Wb[:, c, :], stage[:, c, :], AF.Copy, scale=gainT[:, c:c + 1]),
                               reads=[b_stage, b_const], writes=[b_Wb])
                      return f
                  return [mk(c) for c in range(8)]

              def cast_w():
                  for f in cast_pieces():
                      f()

              cnt = {'st': 0, 'et': 0, 'nt': 0}
              APOOL = (0, 1, 2, 3, 4, 5)
              OPOOL = (6, 7)

              def emit_proj(hp, gs, filler=None):
                  for g in gs:
                      for tb in range(8):
                          if filler:
                              filler.pop(0)()
                          bk = nextbank(APOOL)
                          for c in range(8):
                              S.op('pe', MM(ps[bk][:, :], lhsT=Wb[:, c, g * 128:(g + 1) * 128],
                                            rhs=uT[:, c, tb * 512:(tb + 1) * 512], start=(c == 0), stop=(c == 7)),
                                   reads=[b_Wb, b_uT], writes=[b_ps[bk]], sig=(c == 7))
                          tsl = slice(tb * 512, (tb + 1) * 512)
                          if g == 0:
                              S.op('act', ACTV(qA0[0:64, tsl], ps[bk][0:64, :], AF.Copy, scale=0.125),
                                   reads=[b_ps[bk]], writes=[b_q])
                              S.op('act', ACTV(qB0[64:128, tsl], ps[bk][64:128, :], AF.Copy, scale=0.125),
                                   reads=[b_ps[bk]], writes=[b_q])
                          elif g == 1:
                              S.op('act', ACTV(kT[:, tsl], ps[bk][:, :], AF.Copy), reads=[b_ps[bk]], writes=[b_k])
                          elif g == 2:
                              S.op('act', ACTV(vT[:, tsl], ps[bk][:, :], AF.Copy), reads=[b_ps[bk]], writes=[b_v])
                          else:
                              S.op('act', ACTV(zs[:, tsl], ps[bk][:, :], AF.Silu), reads=[b_ps[bk]], writes=[b_z])

              def emit_attn(hp, fillers=None):
                  tick = {'n': 0}
                  for pi, r in enumerate(PATTERNS):
                      nseg = NT // r
                      def build_vaug(r=r, nseg=nseg):
                          for T0 in range(0, NT, 4):
                              bk = nextbank(APOOL)
                              for u in range(4):
                                  c, jj = divmod(T0 + u, nseg)
                                  S.op('pe', MM(ps[bk][:, u * 128:(u + 1) * 128], lhsT=tokset(vT, r, c, jj), rhs=ident[:, :]),
                                       reads=[b_v, b_const], writes=[b_ps[bk]], sig=(u == 3))
                              psv = ps[bk][:].rearrange("p (u b f) -> p u b f", b=2, f=64)
                              dst = vaug[:, T0:T0 + 4, :].rearrange("p t (b f) -> p t b f", f=64)[:, :, 0:3:2, :]
                              if (T0 // 4) % 2 == 0:
                                  S.op('act', ACTV(dst, psv, AF.Copy), reads=[b_ps[bk]], writes=[b_vaug])
                              else:
                                  S.op('dve', CP(dst, psv), reads=[b_ps[bk]], writes=[b_vaug])
                      if r == 1:
                          groups = [[(0, jj) for jj in range(g * 4, g * 4 + 4)] for g in range(8)]
                      elif r == 4:
                          groups = [[(c, jj) for c in range(4)] for jj in range(8)]
                      else:
                          groups = [[(c, jj) for c in range(c0, c0 + 4)] for jj in range(2) for c0 in range(0, 16, 4)]
                      tasks = [(hh, gi) for hh in range(2) for gi in range(len(groups))]
                      etslot = {}
                      for hh in range(2):
                          h = hp * 2 + hh
                          slope = 2.0 ** (-8.0 * (h + 1) / 16.0)
                          es_ = cnt['et'] % 3
                          cnt['et'] += 1
                          etslot[hh] = es_
                          S.op('act', ACTV(tmpE[:], dcl[:].rearrange("p a b -> p (a b)"), AF.Exp, scale=-slope * r),
                               reads=[b_const], writes=[b_tmpE])
                          for u2 in range(2):
                              S.op('dve', TT(etab[es_][:, u2, :], tmpE[:], m01[:].rearrange("p a b -> p (a b)"), ALU.mult),
                                   reads=[b_tmpE, b_const], writes=[b_etab[es_]])

                      def emit_st(task):
                          hh, gi = task
                          grp = groups[gi]
                          qh = qA0 if hh == 0 else qB0
                          slots = []
                          for pair2 in range(2):
                              bk = nextbank(APOOL)
                              v4 = ps[bk][:].rearrange("p (u h i) -> p u h i", h=2, i=128)
                              mms = []
                              for u2 in range(2):
                                  c, jj = grp[pair2 * 2 + u2]
                                  qs = tokset(qh, r, c, jj)
                                  if jj > 0:
                                      mms.append(MM(v4[:, u2, 0, :], lhsT=tokset(kT, r, c, jj - 1), rhs=qs))
                                  mms.append(MM(v4[:, u2, 1, :], lhsT=tokset(kT, r, c, jj), rhs=qs))
                              for i, m in enumerate(mms):
                                  S.op('pe', m, reads=[b_k, b_q], writes=[b_ps[bk]], sig=(i == len(mms) - 1))
                              sl = cnt['st'] % NSL
                              cnt['st'] += 1
                              S.op('act', ACTV(P0[sl][:], ps[bk][:, :], AF.Exp), reads=[b_ps[bk]], writes=[b_P0[sl]])
                              S.op('pool' if cnt['st'] % 3 == 0 else 'dve',
                                   TT(PT[sl][:], P0[sl][:], etab[etslot[hh]][:].rearrange("p a b -> p (a b)"), ALU.mult),
                                   reads=[b_P0[sl], b_etab[etslot[hh]]], writes=[b_PT[sl]])
                              slots.append(sl)
                          return slots

                      def emit_pv(task, slots):
                          hh, gi = task
                          grp = groups[gi]
                          ob = nextbank(OPOOL)
                          cols = slice(0, 128) if hh == 0 else slice(64, 192)
                          mms = []
                          for u in range(4):
                              c, jj = grp[u]
                              T = c * nseg + jj
                              ptv = PT[slots[u // 2]][:].rearrange("p (u h i) -> p u h i", h=2, i=128)
                              o_ap = ps[ob][:, u * 128:(u + 1) * 128]
                              if jj > 0:
                                  mms.append(MM(o_ap, lhsT=vaug[:, T - 1, cols], rhs=ptv[:, u % 2, 0, :], start=True, stop=False))
                                  mms.append(MM(o_ap, lhsT=vaug[:, T, cols], rhs=ptv[:, u % 2, 1, :], start=False, stop=True))
                              else:
                                  mms.append(MM(o_ap, lhsT=vaug[:, T, cols], rhs=ptv[:, u % 2, 1, :], start=True, stop=True))
                          for i, m in enumerate(mms):
                              S.op('pe', m, reads=[b_vaug, b_PT[slots[0]], b_PT[slots[1]]], writes=[b_ps[ob]],
                                   sig=(i == len(mms) - 1))
                          psv = ps[ob][:].rearrange("p (u i) -> p u i", i=128)
                          a = acc[hh]
                          if r == 1:
                              accv = a[:, gi * 512:(gi + 1) * 512].rearrange("p (u i) -> p u i", i=128)
                              blks = [gi]
                          elif r == 4:
                              accv = a[:, gi * 512:(gi + 1) * 512].rearrange("p (i c) -> p c i", c=4)
                              blks = [gi]
                          else:
                              jj = grp[0][1]
                              c0 = grp[0][0]
                              accv = a[:, jj * 2048:(jj + 1) * 2048].rearrange("p (i c) -> p c i", c=16)[:, c0:c0 + 4, :]
                              blks = [jj * 4 + i for i in range(4)]
                          bb = [b_acc[hh][i] for i in blks]
                          if pi == 0:
                              S.op('dve', CP(accv, psv), reads=[b_ps[ob]], writes=bb)
                          else:
                              S.op('dve', TT(accv, psv, accv, ALU.add), reads=[b_ps[ob]] + bb, writes=bb)

                      pend = [emit_st(tasks[0])]
                      build_vaug()
                      pend.append(emit_st(tasks[1]))
                      for ti in range(len(tasks)):
                          if ti + 2 < len(tasks):
                              pend.append(emit_st(tasks[ti + 2]))
                          emit_pv(tasks[ti], pend.pop(0))
                          tick['n'] += 1
                          if fillers and tick['n'] % 5 == 0:
                              fillers.pop(0)()

              def norm_pieces(hp):
                  pieces = []
                  for hh in range(2):
                      npart = slice(0, 64) if hh == 0 else slice(64, 128)
                      dpart = slice(64, 128) if hh == 0 else slice(0, 64)

                      def ln_piece(hh=hh, dpart=dpart):
                          S.op('act', ACTV(acc[hh][dpart, :], acc[hh][dpart, :], AF.Ln), reads=b_acc[hh], writes=b_acc[hh])
                      pieces.append(ln_piece)
                      for tb in range(8):
                          def blk_piece(hh=hh, tb=tb, npart=npart, dpart=dpart):
                              tsl = slice(tb * 512, (tb + 1) * 512)
                              ns = cnt['nt'] % 2
                              cnt['nt'] += 1
                              S.op('act', ACTV(ntmp[ns][npart, :], acc[hh][dpart, tsl], AF.Exp, scale=-1.0),
                                   reads=[b_acc[hh][tb]], writes=[b_ntmp[ns]])
                              S.op('dve', TT(ntmp[ns][npart, :], ntmp[ns][npart, :], acc[hh][npart, tsl], ALU.mult),
                                   reads=[b_acc[hh][tb], b_ntmp[ns]], writes=[b_ntmp[ns]])
                              S.op('dve', TT(oaT[npart, tsl], ntmp[ns][npart, :], zs[npart, tsl], ALU.mult),
                                   reads=[b_ntmp[ns], b_z], writes=[b_oaT])
                          pieces.append(blk_piece)

                  def spill_piece(hp=hp):
                      S.dma('pool', DMA(oa_scr[hp * 128:(hp + 1) * 128, :], oaT[:, :]), 'D_sp', reads=[b_oaT])
                  pieces.append(spill_piece)
                  return pieces

              emit_proj(0, (0, 1, 2, 3))
              cast_w()
              load_w(2)
              castq = []
              for hp in range(8):
                  emit_attn(hp, fillers=castq)
                  while castq:
                      castq.pop(0)()
                  if hp >= 1 and hp + 2 < 8:
                      load_w(hp + 2)
                  pieces = norm_pieces(hp)
                  if hp + 1 < 8:
                      emit_proj(hp + 1, (0, 1, 2), filler=pieces)
                  while pieces:
                      pieces.pop(0)()
                  if hp + 1 < 8:
                      emit_proj(hp + 1, (3,))
                      if hp + 2 < 8:
                          castq = cast_pieces()
              S.barrier()
              S.emit(nc, sems)

        P12.close()

        with ExitStack() as P:
          if PHASES >= 3:
              Wg = sb(P, "Wg3", [128, 8, 3072], BF16)
              b_WgS = [Buf("Wg3_%d" % i) for i in range(12)]
              Wa = sb(P, "Wa3", [128, 8, 128], BF16)
              b_Wa = Buf("Wa3")
              stage3 = [sb(P, "stage3_%d" % i, [128, 8, 256], F32) for i in range(2)]
              b_stage3 = [Buf("stage3_%d" % i) for i in range(2)]
              wal = sb(P, "wal", [128, 512], F32)
              decs = sb(P, "decs", [128, 4, NT], F32)
              b_decs = [Buf("decs%d" % i) for i in range(NT)]
              b_c3 = Buf("c3")
              b_c3g = Buf("c3g")
              qk = sb(P, "qk_sb", [128, 8, 512], BF16)
              b_qk = Buf("qk")
              acT = sb(P, "acT", [128, 512], F32)
              b_acT = Buf("acT")
              Lsb = [sb(P, "Lsb%d" % i, [128, 512], F32) for i in range(2)]
              b_L = [Buf("L%d" % i) for i in range(2)]
              epos = [sb(P, "epos%d" % i, [128, 4, 128], F32) for i in range(2)]
              eneg = [sb(P, "eneg%d" % i, [128, 4, 128], F32) for i in range(2)]
              b_ep = [Buf("ep%d" % i) for i in range(2)]
              b_en = [Buf("en%d" % i) for i in range(2)]
              qd = [sb(P, "qd%d" % i, [128, 128], BF16) for i in range(8)]
              ki = [sb(P, "ki%d" % i, [128, 128], BF16) for i in range(8)]
              atm = [sb(P, "atm%d" % i, [128, 128], BF16) for i in range(4)]
              kit = [sb(P, "kit%d" % i, [128, 128], BF16) for i in range(4)]
              vsb = [sb(P, "vsb%d" % i, [128, 256], BF16) for i in range(8)]
              zsb = [sb(P, "zsb%d" % i, [128, 256], BF16) for i in range(12)]
              b_qd = [Buf("qd%d" % i) for i in range(8)]
              b_ki = [Buf("ki%d" % i) for i in range(8)]
              b_atm = [Buf("atm%d" % i) for i in range(4)]
              b_kit = [Buf("kit%d" % i) for i in range(4)]
              b_vsb = [Buf("vsb%d" % i) for i in range(8)]
              b_zsb = [Buf("zsb%d" % i) for i in range(12)]
              state = sb(P, "state", [128, 4, 256], F32)
              stbf = sb(P, "stbf", [128, 4, 256], BF16)
              b_state = [Buf("state%d" % i) for i in range(4)]
              b_stbf = [Buf("stbf%d" % i) for i in range(4)]
              bst = [sb(P, "bst%d" % i, [128, 6], F32) for i in range(4)]
              mv = [sb(P, "mv%d" % i, [128, 4], F32) for i in range(4)]
              b_mv = [Buf("mv%d" % i) for i in range(4)]
              onf = [sb(P, "onf%d" % i, [128, 256], BF16) for i in range(4)]
              b_onf = [Buf("onf%d" % i) for i in range(4)]
              obt = [sb(P, "obt%d" % i, [128, 1024], BF16) for i in range(2)]
              b_obt = [Buf("obt%d" % i) for i in range(2)]
              obT = [sb(P, "obT%d" % i, [128, 8, 512], BF16) for i in range(2)]
              b_obT = [Buf("obT%d" % i) for i in range(2)]

              S.dma('sp', DMA(wal[:], wal_d[:, :]), 'D_c2', writes=[b_c3])
              S.op('pool', lambda e: e.memset(Wa[:], 0.0), writes=[b_Wa])
              S.op('pool', lambda e: e.memset(acT[:], 0.0), writes=[b_acT])
              S.op('pool', lambda e: e.memset(acT[0:32, :], 1.0), writes=[b_acT])
              S.dma('sp', DMA(stage3[0][:, :, 0:16], wgla_d[:, :, 3072:3088]), 'D_w0', writes=[b_stage3[0]])
              for c in range(8):
                  S.op('dve', TS(Wa[:, c, 0:16], stage3[0][:, c, 0:16], gainT[:, c:c + 1], None, ALU.mult),
                       reads=[b_stage3[0], b_const], writes=[b_Wa])
              for sl in range(12):
                  k2 = sl % 2
                  S.dma('sp', DMA(stage3[k2][:], wgla_d[:, :, sl * 256:(sl + 1) * 256]), 'D_w%d' % k2, writes=[b_stage3[k2]])
                  for c in range(8):
                      if c % 2 == 0:
                          S.op('act', ACTV(Wg[:, c, sl * 256:(sl + 1) * 256], stage3[k2][:, c, :], AF.Copy, scale=gainT[:, c:c + 1]),
                               reads=[b_stage3[k2], b_const], writes=[b_WgS[sl]])
                      else:
                          S.op('dve', TS(Wg[:, c, sl * 256:(sl + 1) * 256], stage3[k2][:, c, :], gainT[:, c:c + 1], None, ALU.mult),
                               reads=[b_stage3[k2], b_const], writes=[b_WgS[sl]])

              def emit_B(tb):
                  tsl = slice(tb * 512, (tb + 1) * 512)
                  bk = nextbank((0, 1, 2, 3))
                  for c in range(8):
                      S.op('pe', MM(ps[bk][:, :], lhsT=Wa[:, c, :], rhs=uT[:, c, tsl], start=(c == 0), stop=(c == 7)),
                           reads=[b_Wa, b_uT], writes=[b_ps[bk]], sig=(c == 7))
                  S.op('dve', CP(acT[0:16, :], ps[bk][0:16, :]), reads=[b_ps[bk]], writes=[b_acT])
                  for j in range(8):
                      bk = nextbank((0, 1, 2, 3))
                      for c in range(8):
                          S.op('pe', MM(ps[bk][:, :], lhsT=Wg[:, c, j * 128:(j + 1) * 128], rhs=uT[:, c, tsl],
                                        start=(c == 0), stop=(c == 7)),
                               reads=[b_WgS[j // 2], b_uT], writes=[b_ps[bk]], sig=(c == 7))
                      if j % 2 == 0:
                          S.op('act', ACTV(qk[:, j, :], ps[bk][:, :], AF.Copy), reads=[b_ps[bk]], writes=[b_qk])
                      else:
                          S.op('dve', CP(qk[:, j, :], ps[bk][:, :]), reads=[b_ps[bk]], writes=[b_qk])

              GP = (0, 1, 2, 3)
              OP = (4, 5, 6, 7)

              def emit_G1(t):
                  tt = t % 4
                  s2 = t % 2
                  csl = slice(tt * 128, (tt + 1) * 128)
                  bk = nextbank(GP)
                  S.op('pe', MM(ps[bk][:, :], lhsT=acT[:, csl], rhs=wal[:, :]), reads=[b_acT, b_c3], writes=[b_ps[bk]])
                  S.op('act', ACTV(Lsb[s2][:], ps[bk][:, :], AF.Exp, scale=-1.0), reads=[b_ps[bk]], writes=[b_L[s2]])
                  S.op('act', ACTV(Lsb[s2][:], Lsb[s2][:], AF.Ln, bias=onec[:, 0:1]), reads=[b_L[s2], b_const], writes=[b_L[s2]])

              def emit_G3(t):
                  s2 = t % 2
                  bk = nextbank(GP)
                  for h in range(4):
                      S.op('pe', MM(ps[bk][:, h * 128:(h + 1) * 128], lhsT=Lsb[s2][:, h * 128:(h + 1) * 128], rhs=tri[:, :]),
                           reads=[b_L[s2], b_const], writes=[b_ps[bk]], sig=(h == 3))
                  psb = ps[bk][:].rearrange("p (h i) -> p h i", i=128)
                  S.op('act', ACTV(epos[s2][:], psb, AF.Exp), reads=[b_ps[bk]], writes=[b_ep[s2]])
                  S.op('act', ACTV(eneg[s2][:], psb, AF.Exp, scale=-1.0), reads=[b_ps[bk]], writes=[b_en[s2]])
                  S.op('act', ACTV(decs[:, :, t], psb[:, :, 127], AF.Exp), reads=[b_ps[bk]], writes=[b_decs[t]])

              def emit_G5(t):
                  tt = t % 4
                  s2 = t % 2
                  csl = slice(tt * 128, (tt + 1) * 128)
                  for h in range(4):
                      x = s2 * 4 + h
                      S.op('dve', STT(qd[x][:], qk[:, h, csl], 128.0 ** -0.5, epos[s2][:, h, :], ALU.mult, ALU.mult),
                           reads=[b_qk, b_ep[s2]], writes=[b_qd[x]])
                      S.op('dve', TT(ki[x][:], qk[:, 4 + h, csl], eneg[s2][:, h, :], ALU.mult),
                           reads=[b_qk, b_en[s2]], writes=[b_ki[x]])

              vz_pending = []

              def emit_VZ_silu():
                  while vz_pending:
                      x, bk = vz_pending.pop(0)
                      S.op('act', ACTV(zsb[x][:], ps[bk][:, 256:512], AF.Silu), reads=[b_ps[bk]], writes=[b_zsb[x]])

              def emit_VZ(t, heads, defer_silu=False):
                  s2 = t % 2
                  for h in heads:
                      x = s2 * 4 + h
                      bk = nextbank(GP)
                      for c in range(8):
                          S.op('pe', MM(ps[bk][:, :], lhsT=uT[:, c, t * 128:(t + 1) * 128],
                                        rhs=Wg[:, c, 1024 + h * 512:1024 + (h + 1) * 512], start=(c == 0), stop=(c == 7)),
                               reads=[b_WgS[4 + 2 * h], b_WgS[5 + 2 * h], b_uT], writes=[b_ps[bk]], sig=(c == 7))
                      if h < 2:
                          S.op('dve', CP(vsb[x][:], ps[bk][:, 0:256]), reads=[b_ps[bk]], writes=[b_vsb[x]])
                          vz_pending.append(((t % 3) * 4 + h, bk))
                      else:
                          zx = (t % 3) * 4 + h
                          S.op('act', ACTV(zsb[zx][:], ps[bk][:, 256:512], AF.Silu), reads=[b_ps[bk]], writes=[b_zsb[zx]])
                          S.op('act', ACTV(vsb[x][:], ps[bk][:, 0:256], AF.Copy), reads=[b_ps[bk]], writes=[b_vsb[x]])
                  if not defer_silu:
                      emit_VZ_silu()

              def emit_H1(t):
                  s2 = t % 2
                  abk = []
                  for h in range(4):
                      x = s2 * 4 + h
                      bk = nextbank(GP)
                      abk.append(bk)
                      S.op('pe', MM(ps[bk][:, 0:128], lhsT=ki[x][:], rhs=qd[x][:]), reads=[b_ki[x], b_qd[x]],
                           writes=[b_ps[bk]], sig=False)
                      S.op('pe', MM(ps[bk][:, 128:256], lhsT=ki[x][:], rhs=ident[:, :]), reads=[b_ki[x], b_const],
                           writes=[b_ps[bk]])
                  for h in range(4):
                      bk = abk[h]
                      S.op('dve', TT(atm[h][:], ps[bk][:, 0:128], m01[:, 1, :], ALU.mult), reads=[b_ps[bk], b_const],
                           writes=[b_atm[h]])
                      S.op('act', ACTV(kit[h][:], ps[bk][:, 128:256], AF.Copy), reads=[b_ps[bk]], writes=[b_kit[h]])

              obk_of = {}

              def emit_H2a(t):
                  s2 = t % 2
                  obk = []
                  obk_of[t] = obk
                  for h in range(4):
                      x = s2 * 4 + h
                      bk = nextbank(OP)
                      obk.append(bk)
                      if t > 0:
                          S.op('pe', MM(ps[bk][:, 0:256], lhsT=atm[h][:], rhs=vsb[x][:], start=True, stop=False),
                               reads=[b_atm[h], b_vsb[x]], writes=[b_ps[bk]], sig=False)
                          S.op('pe', MM(ps[bk][:, 0:256], lhsT=qd[x][:], rhs=stbf[:, h, :], start=False, stop=True),
                               reads=[b_qd[x], b_stbf[h]], writes=[b_ps[bk]], sig=False)
                      else:
                          S.op('pe', MM(ps[bk][:, 0:256], lhsT=atm[h][:], rhs=vsb[x][:]),
                               reads=[b_atm[h], b_vsb[x]], writes=[b_ps[bk]], sig=False)
                      S.op('pe', MM(ps[bk][:, 256:512], lhsT=kit[h][:], rhs=vsb[x][:]),
                           reads=[b_kit[h], b_vsb[x]], writes=[b_ps[bk]])

              def emit_ST(t):
                  obk = obk_of[t]
                  for h in range(4):
                      bk = obk[h]
                      if t > 0:
                          S.op('dve', STT(state[:, h, :], state[:, h, :], decs[:, h, t - 1:t], ps[bk][:, 256:512], ALU.mult, ALU.add),
                               reads=[b_ps[bk], b_decs[t - 1], b_state[h]], writes=[b_state[h]])
                      else:
                          S.op('dve', CP(state[:, h, :], ps[bk][:, 256:512]), reads=[b_ps[bk]], writes=[b_state[h]])
                      S.op('dve', TS(stbf[:, h, :], state[:, h, :], decs[:, h, t:t + 1], None, ALU.mult),
                           reads=[b_state[h], b_decs[t]], writes=[b_stbf[h]])

              def emit_gnA(t):
                  obk = obk_of[t]
                  for h in range(4):
                      bk = obk[h]
                      S.op('dve', lambda e, h=h, bk=bk: e.bn_stats(out=bst[h][:], in_=ps[bk][:, 0:256]),
                           reads=[b_ps[bk]], writes=[b_mv[h]])
                      S.op('dve', lambda e, h=h: e.bn_aggr(out=mv[h][:, 0:2], in_=bst[h][:]), reads=[b_mv[h]], writes=[b_mv[h]])
                      S.op('dve', TS(mv[h][:, 1:2], mv[h][:, 1:2], 1e-5, None, ALU.add),
                           reads=[b_mv[h]], writes=[b_mv[h]])

              def emit_gnB(t):
                  s2 = t % 2
                  obk = obk_of.pop(t)
                  for h in range(4):
                      S.op('act', ACTV(mv[h][:, 1:2], mv[h][:, 1:2], AF.Sqrt), reads=[b_mv[h]], writes=[b_mv[h]])
                  for h in range(4):
                      S.op('dve', lambda e, h=h: e.reciprocal(out=mv[h][:, 1:2], in_=mv[h][:, 1:2]),
                           reads=[b_mv[h]], writes=[b_mv[h]])
                      S.op('dve', TS(mv[h][:, 2:3], mv[h][:, 0:1], mv[h][:, 1:2], -1.0, ALU.mult, ALU.mult),
                           reads=[b_mv[h]], writes=[b_mv[h]])
                  for h in range(4):
                      x = (t % 3) * 4 + h
                      bk = obk[h]
                      S.op('dve', TS(onf[h][:], ps[bk][:, 0:256], mv[h][:, 0:1], mv[h][:, 1:2], ALU.subtract, ALU.mult),
                           reads=[b_ps[bk], b_mv[h]], writes=[b_onf[h]])
                      S.op('dve', TT(obt[s2][:, h * 256:(h + 1) * 256], onf[h][:], zsb[x][:], ALU.mult),
                           reads=[b_onf[h], b_zsb[x]], writes=[b_obt[s2]])

              def emit_T(t):
                  tb, tt = divmod(t, 4)
                  s2 = t % 2
                  ot = tb % 2
                  csl = slice(tt * 128, (tt + 1) * 128)
                  for half in range(2):
                      bk = nextbank(GP)
                      for cc in range(4):
                          c = half * 4 + cc
                          S.op('pe', MM(ps[bk][:, cc * 128:(cc + 1) * 128], lhsT=obt[s2][:, c * 128:(c + 1) * 128],
                                        rhs=ident[:, :]), reads=[b_obt[s2], b_const], writes=[b_ps[bk]], sig=(cc == 3))
                      psv = ps[bk][:].rearrange("p (a b) -> p a b", b=128)
                      if half == 0:
                          S.op('act', ACTV(obT[ot][:, half * 4:half * 4 + 4, csl], psv, AF.Copy),
                               reads=[b_ps[bk]], writes=[b_obT[ot]])
                      else:
                          S.op('dve', CP(obT[ot][:, half * 4:half * 4 + 4, csl], psv), reads=[b_ps[bk]], writes=[b_obT[ot]])
                  if tt == 3:
                      tsl = slice(tb * 512, (tb + 1) * 512)
                      S.dma('pool', DMA(ob_v[:, :, tsl], obT[ot][:]), 'D_sp%d' % ot, reads=[b_obT[ot]])

              emit_B(0)
              emit_G1(0)
              emit_VZ(0, (0, 1))
              emit_G3(0)
              emit_VZ(0, (2, 3))
              emit_G5(0)
              for t in range(NT):
                  nx = t + 1 < NT
                  emit_H1(t)
                  if t > 0:
                      emit_gnA(t - 1)
                      emit_gnB(t - 1)
                  if nx:
                      if (t + 1) % 4 == 0:
                          emit_B((t + 1) // 4)
                      emit_G1(t + 1)
                      emit_VZ(t + 1, (0, 1))
                      emit_G3(t + 1)
                  emit_H2a(t)
                  emit_ST(t)
                  if nx:
                      emit_G5(t + 1)
                      emit_VZ(t + 1, (2, 3))
                  if t > 0:
                      emit_T(t - 1)
              emit_gnA(NT - 1)
              emit_gnB(NT - 1)
              emit_T(NT - 1)
              S.barrier()
              S.emit(nc, sems)

        with ExitStack() as P:
          if PHASES >= 4:
              Wga = sb(P, "Wga", [128, 8, 1024], BF16)
              Wgb = sb(P, "Wgb", [128, 8, 1024], BF16)
              Woa = sb(P, "Woa", [128, 8, 1024], BF16)
              Wob = sb(P, "Wob", [128, 8, 1024], BF16)
              Wo = sb(P, "Wo", [128, 8, 1024], BF16)
              b_W4 = Buf("W4")
              stage4 = [sb(P, "stage4_%d" % i, [128, 512], F32) for i in range(2)]
              b_stage4 = [Buf("stage4_%d" % i) for i in range(2)]
              fg = sb(P, "fg", [128, 1024], F32)
              b_c4 = Buf("c4")
              oab = sb(P, "oab", [128, 8, 512], BF16)
              obb = sb(P, "obb", [128, 8, 512], BF16)
              b_oab, b_obb = Buf("oab"), Buf("obb")
              mT = sb(P, "mT", [128, 8, 512], BF16)
              b_mT = Buf("mT")
              sg = [sb(P, "sg%d" % i, [128, 512], F32) for i in range(4)]
              b_sg = [Buf("sg%d" % i) for i in range(4)]
              xr = [sb(P, "xr%d" % i, [128, 1024], F32) for i in range(2)]
              b_xr = [Buf("xr%d" % i) for i in range(2)]
              hs = [sb(P, "hs%d" % i, [128, 1024], F32) for i in range(2)]
              b_hs = [Buf("hs%d" % i) for i in range(2)]
              junk = sb(P, "junk4", [128, 1024], BF16)
              b_junk = Buf("junk4")
              s4 = sb(P, "s4", [128, NT], F32)
              b_s4 = [Buf("s4_%d" % i) for i in range(NT)]

              S.dma('sp', DMA(fg[:], fg_d[:, :]), 'D_c4', writes=[b_c4])
              b_Wd = {id(Woa): Buf("W4oa"), id(Wob): Buf("W4ob"), id(Wga): Buf("W4ga"), id(Wgb): Buf("W4gb"), id(Wo): Buf("W4o")}
              W4spec = ((Woa, woa_d, 0, None), (Wob, wob_d, 0, ggT), (Wga, wg_d, 0, gainT), (Wgb, wg_d, 1024, gainT))
              b_Wh = {}
              for (wt, _s, _n, _g) in W4spec + ((Wo, wo_d, 0, None),):
                  for k2 in range(4):
                      b_Wh[(id(wt), k2)] = Buf("W4_%d_%d" % (len(b_Wh), k2))
              ldcnt = {'n': 0}

              def load_half(wt, src, ncol, use_gain, k2):
                  for c in range(8):
                      sl = ldcnt['n'] % 2
                      ldcnt['n'] += 1
                      S.dma('sp', DMA(stage4[sl][:], src[:, c, ncol + k2 * 512:ncol + (k2 + 1) * 512]), 'D_w%d' % sl,
                            writes=[b_stage4[sl]])
                      dst = wt[:, c, k2 * 512:(k2 + 1) * 512]
                      bw = b_Wh[(id(wt), k2)]
                      if sl == 0:
                          if use_gain is not None:
                              S.op('act', ACTV(dst, stage4[sl][:], AF.Copy, scale=use_gain[:, c:c + 1]),
                                   reads=[b_stage4[sl], b_const, b_const3], writes=[bw])
                          else:
                              S.op('act', ACTV(dst, stage4[sl][:], AF.Copy), reads=[b_stage4[sl]], writes=[bw])
                      else:
                          if use_gain is not None:
                              S.op('dve', TS(dst, stage4[sl][:], use_gain[:, c:c + 1], None, ALU.mult),
                                   reads=[b_stage4[sl], b_const, b_const3], writes=[bw])
                          else:
                              S.op('dve', CP(dst, stage4[sl][:]), reads=[b_stage4[sl]], writes=[bw])

              S.dma('sp', DMA(oab[:], oa_v[:, :, 0:512]), 'D_ld0', writes=[b_oab])
              S.dma('sp', DMA(obb[:], ob_v[:, :, 0:512]), 'D_ld1', writes=[b_obb])
              def load_quarter(wt, src, ncol, use_gain, q):
                  for c0 in range(0, 8, 2):
                      sl = ldcnt['n'] % 2
                      ldcnt['n'] += 1
                      stv = stage4[sl][:].rearrange("p (a b) -> p a b", b=256)
                      S.dma('sp', DMA(stv, src[:, c0:c0 + 2, ncol + q * 256:ncol + (q + 1) * 256]), 'D_w%d' % sl,
                            writes=[b_stage4[sl]])
                      bw = b_Wh[(id(wt), q)]
                      for a in range(2):
                          c = c0 + a
                          dst = wt[:, c, q * 256:(q + 1) * 256]
                          eng = 'act' if a == 0 else 'dve'
                          if use_gain is not None:
                              fn = (ACTV(dst, stv[:, a, :], AF.Copy, scale=use_gain[:, c:c + 1]) if a == 0
                                    else TS(dst, stv[:, a, :], use_gain[:, c:c + 1], None, ALU.mult))
                              S.op(eng, fn, reads=[b_stage4[sl], b_const, b_const3], writes=[bw])
                          else:
                              fn = ACTV(dst, stv[:, a, :], AF.Copy) if a == 0 else CP(dst, stv[:, a, :])
                              S.op(eng, fn, reads=[b_stage4[sl]], writes=[bw])

              for q in range(4):
                  for spec in W4spec:
                      load_quarter(*spec, q)
              for k2 in range(2):
                  load_half(Wo, wo_d, 0, None, k2)
              for tb in range(8):
                  tsl = slice(tb * 512, (tb + 1) * 512)
                  for m in range(8):
                      msl = slice(m * 128, (m + 1) * 128)
                      banks = []
                      for (wt, rhs_t, rb) in ((Woa, oab, b_oab), (Wob, obb, b_obb), (Wga, None, b_uT), (Wgb, None, b_uT)):
                          bk = nextbank()
                          for c in range(8):
                              rhs = rhs_t[:, c, :] if rhs_t is not None else uT[:, c, tsl]
                              S.op('pe', MM(ps[bk][:, :], lhsT=wt[:, c, msl], rhs=rhs, start=(c == 0), stop=(c == 7)),
                                   reads=[b_Wh[(id(wt), m // 2)], rb], writes=[b_ps[bk]], sig=(c == 7))
                          banks.append(bk)
                      S.op('act', ACTV(sg[0][:], ps[banks[2]][:, :], AF.Sigmoid, bias=bgT[:, m:m + 1]),
                           reads=[b_ps[banks[2]], b_const2], writes=[b_sg[0]])
                      S.op('act', ACTV(sg[1][:], ps[banks[3]][:, :], AF.Sigmoid, bias=bgT[:, 8 + m:9 + m]),
                           reads=[b_ps[banks[3]], b_const2], writes=[b_sg[1]])
                      S.op('dve', TT(sg[2][:], ps[banks[0]][:, :], sg[0][:], ALU.mult), reads=[b_ps[banks[0]], b_sg[0]],
                           writes=[b_sg[2]])
                      S.op('dve', TT(sg[3][:], ps[banks[1]][:, :], sg[1][:], ALU.mult), reads=[b_ps[banks[1]], b_sg[1]],
                           writes=[b_sg[3]])
                      S.op('pool', TT(mT[:, m, :], sg[2][:], sg[3][:], ALU.add), reads=[b_sg[2], b_sg[3]], writes=[b_mT])
                  if tb + 1 < 8:
                      nsl = slice((tb + 1) * 512, (tb + 2) * 512)
                      S.dma('sp', DMA(oab[:], oa_v[:, :, nsl]), 'D_ld0', writes=[b_oab])
                      S.dma('sp', DMA(obb[:], ob_v[:, :, nsl]), 'D_ld1', writes=[b_obb])
                  for tt in range(4):
                      t = tb * 4 + tt
                      s2 = t % 2
                      S.dma('sp', DMA(xr[s2][:], x_d[t * 128:(t + 1) * 128, :]), 'D_xr%d' % s2, writes=[b_xr[s2]])
                      for half in range(2):
                          bk = nextbank()
                          for m in range(8):
                              S.op('pe', MM(ps[bk][:, :], lhsT=mT[:, m, tt * 128:(tt + 1) * 128],
                                            rhs=Wo[:, m, half * 512:(half + 1) * 512], start=(m == 0), stop=(m == 7)),
                                   reads=[b_mT, b_Wh[(id(Wo), half)]], writes=[b_ps[bk]], sig=(m == 7))
                          S.op('dve', TT(hs[s2][:, half * 512:(half + 1) * 512], ps[bk][:, :], xr[s2][:, half * 512:(half + 1) * 512],
                                         ALU.add), reads=[b_ps[bk], b_xr[s2]], writes=[b_hs[s2]])
                      S.op('act', ACTV(junk[:], hs[s2][:], AF.Square, accum_out=s4[:, t:t + 1]),
                           reads=[b_hs[s2]], writes=[b_junk, b_s4[t]])
                      S.op('dve', TS(s4[:, t:t + 1], s4[:, t:t + 1], 1.0 / DM, 1e-6, ALU.mult, ALU.add),
                           reads=[b_s4[t]], writes=[b_s4[t]])
                      S.op('act', ACTV(s4[:, t:t + 1], s4[:, t:t + 1], AF.Sqrt), reads=[b_s4[t]], writes=[b_s4[t]])
                      S.op('dve', lambda e, t=t: e.reciprocal(out=s4[:, t:t + 1], in_=s4[:, t:t + 1]),
                           reads=[b_s4[t]], writes=[b_s4[t]])
                      S.op('dve', STT(hs[s2][:], hs[s2][:], s4[:, t:t + 1], fg[:], ALU.mult, ALU.mult),
                           reads=[b_hs[s2], b_s4[t], b_c4], writes=[b_hs[s2]])
                      ev = S.dma('pool', DMA(out_d[t * 128:(t + 1) * 128, :], hs[s2][:]), 'D_o%d' % s2, reads=[b_hs[s2]])
                      final_evs.append(ev)
              S.barrier()
              S.emit(nc, sems)
    return nc


def _lay8(w):
    n = w.shape[1]
    return np.ascontiguousarray(w.reshape(8, 128, n).transpose(1, 0, 2))


_NC_CACHE = {}


def kernel(x, norm_gain, w_in, b_gate, w_alpha, b_alpha, gla_norm_gain,
           w_out_attn, w_out_gla, w_out, final_norm_gain):
    f = np.float32
    x = np.asarray(x, f)
    W = np.asarray(w_in, f)[0]
    gainT = np.ascontiguousarray(np.asarray(norm_gain, f)[0].reshape(8, 128).T)
    watt = np.stack([
        _lay8(np.concatenate([W[:, hp * 128:(hp + 1) * 128], W[:, 1024 + hp * 128:1024 + (hp + 1) * 128],
                              W[:, 2048 + hp * 128:2048 + (hp + 1) * 128], W[:, 3072 + hp * 128:3072 + (hp + 1) * 128]], axis=1))
        for hp in range(8)])
    cols = [W[:, 4096:4608], W[:, 4608:5120]]
    for h in range(4):
        cols.append(W[:, 5120 + h * 256:5120 + (h + 1) * 256])
        cols.append(W[:, 6144 + h * 256:6144 + (h + 1) * 256])
    cols.append(W[:, 7168:7184])
    wgla = _lay8(np.concatenate(cols, axis=1))
    wg = _lay8(W[:, 7184:9232])
    woa = _lay8(np.asarray(w_out_attn, f)[0])
    wob = _lay8(np.asarray(w_out_gla, f)[0])
    wo = _lay8(np.asarray(w_out, f)[0])
    bgT = np.ascontiguousarray(np.asarray(b_gate, f)[0].reshape(16, 128).T)
    wal = np.zeros((128, 512), f)
    wal[0:16] = np.asarray(w_alpha, f)[0]
    wal[16] = np.asarray(b_alpha, f)[0]
    ggT = np.ascontiguousarray(np.asarray(gla_norm_gain, f)[0].reshape(8, 128).T)
    fg = np.ascontiguousarray(np.broadcast_to(np.asarray(final_norm_gain, f)[None, :], (128, 1024)))

    if 'nc' not in _NC_CACHE:
        _NC_CACHE['nc'] = build()
    nc = _NC_CACHE['nc']
    shared = dict(gainT=gainT, watt=watt, wgla=wgla, wg=wg, woa=woa, wob=wob, wo=wo, bgT=bgT, wal=wal, ggT=ggT, fg=fg)
    in_maps = [dict(shared, x=np.ascontiguousarray(x[b])) for b in range(8)]
    res = run_bass_kernel_spmd(nc, in_maps, core_ids=list(range(8)))
    if DEBUG:
        kernel.dbg = res.results
    return np.stack([np.asarray(res.results[b]["out"], f) for b in range(8)], axis=0)
```

```python
import numpy as np
from contextlib import ExitStack
import concourse.bass as bass
import concourse.mybir as mybir
from concourse.bass_utils import run_bass_kernel_spmd

F32 = mybir.dt.float32
BF16 = mybir.dt.bfloat16
ALU = mybir.AluOpType
AF = mybir.ActivationFunctionType

SEQ = 4096
DM = 1024
NT = 32
PATTERNS = (1, 4, 16)
DEBUG = False
PHASES = 4


class Buf:
    def __init__(self, name):
        self.name = name
        self.w = None
        self.r = {}


class Sched:
    ENG = ('pe', 'act', 'dve', 'pool', 'sp')

    def __init__(self):
        self.q = {e: [] for e in self.ENG}
        self.cnt = {'E_' + e: 0 for e in self.ENG}
        self.known = {e: {} for e in self.ENG}
        self.pend_r = {e: [] for e in self.ENG}
        self.pend_w = {e: [] for e in self.ENG}

    def _deps(self, reads, writes):
        evs = []
        for b in reads:
            if b.w is not None:
                evs.append(b.w)
        for b in writes:
            if b.w is not None:
                evs.append(b.w)
            evs.extend(b.r.items())
        return evs

    def _waits(self, eng, evs):
        need = {}
        for (k, v) in evs:
            if v > need.get(k, 0):
                need[k] = v
        out = []
        for k, v in need.items():
            if self.known[eng].get(k, 0) >= v:
                continue
            self.known[eng][k] = v
            out.append((k, v))
        return out

    def _record(self, ev, reads, writes):
        for b in writes:
            b.w = ev
            b.r = {}
        for b in reads:
            if ev[1] > b.r.get(ev[0], 0):
                b.r[ev[0]] = ev[1]

    def op(self, eng, fn, reads=(), writes=(), sig=True):
        writes = list(writes) + [b for b in reads if b.name.startswith("ps") and b not in writes]
        waits = self._waits(eng, self._deps(reads, writes))
        key = 'E_' + eng
        if not sig:
            self.pend_r[eng].extend(reads)
            self.pend_w[eng].extend(writes)
            self.q[eng].append((fn, waits, None))
            return None
        self.cnt[key] += 1
        ev = (key, self.cnt[key])
        self.q[eng].append((fn, waits, (key, 1)))
        self._record(ev, list(reads) + self.pend_r[eng], list(writes) + self.pend_w[eng])
        self.pend_r[eng] = []
        self.pend_w[eng] = []
        return ev

    def dma(self, eng, fn, semkey, reads=(), writes=()):
        waits = self._waits(eng, self._deps(reads, writes))
        self.cnt[semkey] = self.cnt.get(semkey, 0) + 16
        ev = (semkey, self.cnt[semkey])
        self.q[eng].append((fn, waits, (semkey, 16)))
        self._record(ev, reads, writes)
        return ev

    def barrier(self):
        evs = [(k, v) for k, v in self.cnt.items() if v > 0]
        for e in self.ENG:
            w = self._waits(e, evs)
            if w:
                self.q[e].append((None, w, None))

    def emit(self, nc, sems):
        q = self.q
        self.q = {e: [] for e in self.ENG}
        with nc.Block() as block:
            def run(engname):
                def body(e):
                    for fn, waits, inc in q[engname]:
                        for (k, v) in waits:
                            e.wait_ge(sems[k], v)
                        if fn is None:
                            continue
                        ins = fn(e)
                        if inc is not None:
                            ins.then_inc(sems[inc[0]], inc[1])
                return body
            block.tensor(run('pe'))
            block.scalar(run('act'))
            block.vector(run('dve'))
            block.gpsimd(run('pool'))
            block.sync(run('sp'))


def MM(out, lhsT, rhs, start=True, stop=True):
    return lambda e: e.matmul(out, lhsT=lhsT, rhs=rhs, start=start, stop=stop)


def ACTV(out, in_, func, **kw):
    return lambda e: e.activation(out=out, in_=in_, func=func, **kw)


def TT(out, in0, in1, op):
    return lambda e: e.tensor_tensor(out=out, in0=in0, in1=in1, op=op)


def TS(out, in0, s1, s2, op0, op1=None):
    if op1 is None:
        return lambda e: e.tensor_scalar(out=out, in0=in0, scalar1=s1, scalar2=None, op0=op0)
    return lambda e: e.tensor_scalar(out=out, in0=in0, scalar1=s1, scalar2=s2, op0=op0, op1=op1)


def STT(out, in0, scalar, in1, op0, op1):
    return lambda e: e.scalar_tensor_tensor(out=out, in0=in0, scalar=scalar, in1=in1, op0=op0, op1=op1)


def CP(out, in_):
    return lambda e: e.tensor_copy(out=out, in_=in_)


def DMA(out, in_):
    return lambda e: e.dma_start(out=out, in_=in_)


SEMKEYS = ['E_pe', 'E_act', 'E_dve', 'E_pool', 'E_sp',
           'D_c0', 'D_c1', 'D_c2', 'D_c3', 'D_c4', 'D_c5', 'D_x0', 'D_x1', 'D_x2', 'D_x3', 'D_x4', 'D_x5', 'D_w', 'D_w0', 'D_w1', 'D_w2', 'D_w3', 'D_sp', 'D_sp0', 'D_sp1',
           'D_o0', 'D_o1', 'D_ld0', 'D_ld1', 'D_xr0', 'D_xr1']


def build():
    nc = bass.Bass("TRN2", target_bir_lowering=False)

    def din(name, shape, dt=F32):
        return nc.dram_tensor(name, list(shape), dt, kind="ExternalInput").ap()

    x_d = din("x", [SEQ, DM])
    gainT_d = din("gainT", [128, 8])
    watt_d = din("watt", [8, 128, 8, 512])
    wgla_d = din("wgla", [128, 8, 3088])
    wg_d = din("wg", [128, 8, 2048])
    woa_d = din("woa", [128, 8, 1024])
    wob_d = din("wob", [128, 8, 1024])
    wo_d = din("wo", [128, 8, 1024])
    bgT_d = din("bgT", [128, 16])
    wal_d = din("wal", [128, 512])
    ggT_d = din("ggT", [128, 8])
    fg_d = din("fg", [128, 1024])
    out_d = nc.dram_tensor("out", [SEQ, DM], F32, kind="ExternalOutput").ap()
    skind = "ExternalOutput" if DEBUG else "Internal"
    oa_scr = nc.dram_tensor("oa_scr", [DM, SEQ], BF16, kind=skind).ap()
    ob_scr = nc.dram_tensor("ob_scr", [DM, SEQ], BF16, kind=skind).ap()
    uT_dbg = nc.dram_tensor("uT_dbg", [128, 8, SEQ], BF16, kind="ExternalOutput").ap() if DEBUG else None
    oa_v = oa_scr.rearrange("(c p) t -> p c t", p=128)
    ob_v = ob_scr.rearrange("(c p) t -> p c t", p=128)

    S = Sched()
    final_evs = []

    with ExitStack() as G:
        def sb(es, name, shape, dt):
            return es.enter_context(nc.sbuf_tensor("s_" + name, list(shape), dt))

        sems = {k: G.enter_context(nc.semaphore(k)) for k in SEMKEYS}
        ps = [G.enter_context(nc.psum_tensor("ps%d" % i, [128, 512], F32)) for i in range(8)]
        b_ps = [Buf("ps%d" % i) for i in range(8)]
        rot = {'i': 0}

        def nextbank(pool=(0, 1, 2, 3, 4, 5, 6, 7)):
            k = rot.get(pool, 0)
            rot[pool] = k + 1
            return pool[k % len(pool)]

        uT = sb(G, "uT", [128, 8, SEQ], BF16)
        b_uT = Buf("uT")
        ident = sb(G, "ident", [128, 128], BF16)
        dmat = sb(G, "dmat", [128, 2, 128], F32)
        m01 = sb(G, "m01", [128, 2, 128], F32)
        dcl = sb(G, "dcl", [128, 2, 128], F32)
        tri = sb(G, "tri", [128, 128], F32)
        onec = sb(G, "onec", [128, 1], F32)
        gainT = sb(G, "gainT_s", [128, 8], F32)
        bgT = sb(G, "bgT_s", [128, 16], F32)
        ggT = sb(G, "ggT_s", [128, 8], F32)
        b_const3 = Buf("const3")
        b_const = Buf("const")
        b_const2 = Buf("const2")

        S.op('pool', lambda e: e.iota(dmat[:], pattern=[[-128, 2], [1, 128]], base=128, channel_multiplier=-1,
                                      allow_small_or_imprecise_dtypes=True), writes=[b_const])
        S.op('dve', TS(m01[:], dmat[:], 0.0, None, ALU.is_ge), reads=[b_const], writes=[b_const])
        S.op('dve', TS(dcl[:], dmat[:], 128.0, None, ALU.is_le), reads=[b_const], writes=[b_const])
        S.op('dve', TT(m01[:], m01[:], dcl[:], ALU.mult), reads=[b_const], writes=[b_const])
        S.op('dve', TS(dcl[:], dmat[:], 0.0, 128.0, ALU.max, ALU.min), reads=[b_const], writes=[b_const])
        S.op('dve', TS(tri[:], dmat[:, 1, :], 0.0, -1.0 / 16.0, ALU.is_ge, ALU.mult), reads=[b_const], writes=[b_const])
        S.op('dve', TS(ident[:], dmat[:, 1, :], 0.0, None, ALU.is_equal), reads=[b_const], writes=[b_const])
        S.op('dve', lambda e: e.memset(onec[:], 1.0), writes=[b_const])
        S.dma('sp', DMA(gainT[:], gainT_d[:, :]), 'D_c0', writes=[b_const])
        S.dma('sp', DMA(bgT[:], bgT_d[:, :]), 'D_c1', writes=[b_const2])
        S.dma('sp', DMA(ggT[:], ggT_d[:, :]), 'D_c3', writes=[b_const3])

        P12 = ExitStack()
        stage = sb(P12, "stage2", [128, 8, 512], F32)
        b_stage = Buf("stage2")
        Wb = sb(P12, "Wb2", [128, 8, 512], BF16)
        b_Wb = Buf("Wb2")

        with ExitStack() as P:
            S.dma('sp', DMA(stage[:], watt_d[0]), 'D_w', writes=[b_stage])
            xt = [sb(P, "xt%d" % i, [128, DM], F32) for i in range(6)]
            b_xt = [Buf("xt%d" % i) for i in range(6)]
            xn = [sb(P, "xn%d" % i, [128, DM], BF16) for i in range(2)]
            b_xn = [Buf("xn%d" % i) for i in range(2)]
            junk = sb(P, "junk1", [128, DM], BF16)
            b_junk = Buf("junk1")
            ss = sb(P, "ss", [128, NT], F32)
            rstd = sb(P, "rstd", [128, NT], F32)
            b_ss = [Buf("ss%d" % i) for i in range(NT)]
            def p1_dma(t):
                s3 = t % 6
                S.dma('sp', DMA(xt[s3][:], x_d[t * 128:(t + 1) * 128, :]), 'D_x%d' % s3, writes=[b_xt[s3]])

            epsc = sb(P, "epsc", [128, 1], F32)
            b_epsc = Buf("epsc")
            S.op('dve', lambda e: e.memset(epsc[:], 1e-6), writes=[b_epsc])
            xn3 = [sb(P, "xn3_%d" % i, [128, DM], BF16) for i in range(3)]
            b_xn3 = [Buf("xn3_%d" % i) for i in range(3)]
            pbank = {}

            def p1_A(t):
                s3 = t % 6
                S.op('act', ACTV(junk[:], xt[s3][:], AF.Square, accum_out=ss[:, t:t + 1]),
                     reads=[b_xt[s3]], writes=[b_junk, b_ss[t]])
                S.op('act', ACTV(rstd[:, t:t + 1], ss[:, t:t + 1], AF.Sqrt, scale=1.0 / DM, bias=epsc[:, 0:1]),
                     reads=[b_ss[t], b_epsc], writes=[b_ss[t]])

            def p1_D(t):
                s3 = t % 6
                s2 = t % 3
                S.op('dve', lambda e, t=t: e.reciprocal(out=rstd[:, t:t + 1], in_=rstd[:, t:t + 1]),
                     reads=[b_ss[t]], writes=[b_ss[t]])
                S.op('dve', TS(xn3[s2][:], xt[s3][:], rstd[:, t:t + 1], None, ALU.mult),
                     reads=[b_xt[s3], b_ss[t]], writes=[b_xn3[s2]])

            def p1_P(t):
                s2 = t % 3
                bks = []
                for half in range(2):
                    bk = nextbank()
                    bks.append(bk)
                    for cc in range(4):
                        c = half * 4 + cc
                        S.op('pe', MM(ps[bk][:, cc * 128:(cc + 1) * 128], lhsT=xn3[s2][:, c * 128:(c + 1) * 128],
                                      rhs=ident[:, :]), reads=[b_xn3[s2], b_const], writes=[b_ps[bk]], sig=(cc == 3))
                pbank[t] = bks

            def p1_E(t):
                bks = pbank.pop(t)
                for half in range(2):
                    bk = bks[half]
                    psv = ps[bk][:].rearrange("p (a b) -> p a b", b=128)
                    eng = 'act' if half == 0 else 'dve'
                    fn = (ACTV(uT[:, half * 4:half * 4 + 4, t * 128:(t + 1) * 128], psv, AF.Copy) if half == 0
                          else CP(uT[:, half * 4:half * 4 + 4, t * 128:(t + 1) * 128], psv))
                    S.op(eng, fn, reads=[b_ps[bk]], writes=[b_uT])

            for t in range(6):
                p1_dma(t)
            p1_A(0)
            p1_A(1)
            p1_D(0)
            for t in range(NT):
                if t + 2 < NT:
                    p1_A(t + 2)
                if t + 1 < NT:
                    p1_D(t + 1)
                p1_P(t)
                if t > 0:
                    p1_E(t - 1)
                if t + 6 < NT:
                    p1_dma(t + 6)
                if 8 <= t < 16:
                    c = t - 8
                    S.op('act', ACTV(Wb[:, c, :], stage[:, c, :], AF.Copy, scale=gainT[:, c:c + 1]),
                         reads=[b_stage, b_const], writes=[b_Wb])
                if t == 16:
                    S.dma('sp', DMA(stage[:], watt_d[1]), 'D_w', writes=[b_stage])
            p1_E(NT - 1)
            if DEBUG:
                S.dma('sp', DMA(uT_dbg[:, :, :], uT[:]), 'D_c5', reads=[b_uT])
            S.barrier()
            S.emit(nc, sems)

        with ExitStack() as P:
          if PHASES >= 2:
              qA0 = sb(P, "qA0", [128, SEQ], BF16)
              qB0 = sb(P, "qB0", [128, SEQ], BF16)
              kT = sb(P, "kT", [128, SEQ], BF16)
              vT = sb(P, "vT", [128, SEQ], BF16)
              zs = sb(P, "zs", [128, SEQ], BF16)
              oaT = sb(P, "oaT", [128, SEQ], BF16)
              b_oaT = Buf("oaT")
              b_q, b_k, b_v, b_z = Buf("q"), Buf("k"), Buf("v"), Buf("z")
              vaug = sb(P, "vaug", [128, NT, 192], BF16)
              b_vaug = Buf("vaug")
              acc = [sb(P, "acc%d" % i, [128, SEQ], F32) for i in range(2)]
              b_acc = [[Buf("acc%d_%d" % (i, j)) for j in range(8)] for i in range(2)]
              NSL = 6
              P0 = [sb(P, "P0_%d" % i, [128, 512], BF16) for i in range(NSL)]
              PT = [sb(P, "PT_%d" % i, [128, 512], BF16) for i in range(NSL)]
              b_P0 = [Buf("P0_%d" % i) for i in range(NSL)]
              b_PT = [Buf("PT_%d" % i) for i in range(NSL)]
              etab = [sb(P, "etab%d" % i, [128, 2, 256], BF16) for i in range(3)]
              b_etab = [Buf("etab%d" % i) for i in range(3)]
              tmpE = sb(P, "tmpE", [128, 256], F32)
              b_tmpE = Buf("tmpE")
              ntmp = [sb(P, "ntmp%d" % i, [128, 512], F32) for i in range(2)]
              b_ntmp = [Buf("ntmp%d" % i) for i in range(2)]

              S.op('pool', lambda e: e.memset(qA0[:], 0.0), writes=[b_q])
              S.op('pool', lambda e: e.memset(qB0[:], 0.0), writes=[b_q])
              S.op('pool', lambda e: e.memset(vaug[:], 1.0), writes=[b_vaug])

              def tokset(tl, r, c, jj):
                  return tl[:].rearrange("p (j i r) -> p j i r", i=128, r=r)[:, jj, :, c]

              def load_w(hp):
                  S.dma('sp', DMA(stage[:], watt_d[hp]), 'D_w', writes=[b_stage])

              def cast_pieces():
                  def mk(c):
                      def f():
                          S.op('act', ACTV(Wb[:, c, :], stage[:, c, :], AF.Copy, scale=gainT[:, c:c + 1]),
                               reads=[b_stage, b_const], writes=[b_Wb])
                      return f
                  return [mk(c) for c in range(8)]

              def cast_w():
                  for f in cast_pieces():
                      f()

              cnt = {'st': 0, 'et': 0, 'nt': 0}
              APOOL = (0, 1, 2, 3, 4, 5)
              OPOOL = (6, 7)

              def emit_proj(hp, gs, filler=None):
                  for g in gs:
                      for tb in range(8):
                          if filler:
                              filler.pop(0)()
                          bk = nextbank(APOOL)
                          for c in range(8):
                              S.op('pe', MM(ps[bk][:, :], lhsT=Wb[:, c, g * 128:(g + 1) * 128],
                                            rhs=uT[:, c, tb * 512:(tb + 1) * 512], start=(c == 0), stop=(c == 7)),
                                   reads=[b_Wb, b_uT], writes=[b_ps[bk]], sig=(c == 7))
                          tsl = slice(tb * 512, (tb + 1) * 512)
                          if g == 0:
                              S.op('act', ACTV(qA0[0:64, tsl], ps[bk][0:64, :], AF.Copy, scale=0.125),
                                   reads=[b_ps[bk]], writes=[b_q])
                              S.op('act', ACTV(qB0[64:128, tsl], ps[bk][64:128, :], AF.Copy, scale=0.125),
                                   reads=[b_ps[bk]], writes=[b_q])
                          elif g == 1:
                              S.op('act', ACTV(kT[:, tsl], ps[bk][:, :], AF.Copy), reads=[b_ps[bk]], writes=[b_k])
                          elif g == 2:
                              S.op('act', ACTV(vT[:, tsl], ps[bk][:, :], AF.Copy), reads=[b_ps[bk]], writes=[b_v])
                          else:
                              S.op('act', ACTV(zs[:, tsl], ps[bk][:, :], AF.Silu), reads=[b_ps[bk]], writes=[b_z])

              def emit_attn(hp, fillers=None):
                  tick = {'n': 0}
                  for pi, r in enumerate(PATTERNS):
                      nseg = NT // r
                      def build_vaug(r=r, nseg=nseg):
                          for T0 in range(0, NT, 4):
                              bk = nextbank(APOOL)
                              for u in range(4):
                                  c, jj = divmod(T0 + u, nseg)
                                  S.op('pe', MM(ps[bk][:, u * 128:(u + 1) * 128], lhsT=tokset(vT, r, c, jj), rhs=ident[:, :]),
                                       reads=[b_v, b_const], writes=[b_ps[bk]], sig=(u == 3))
                              psv = ps[bk][:].rearrange("p (u b f) -> p u b f", b=2, f=64)
                              dst = vaug[:, T0:T0 + 4, :].rearrange("p t (b f) -> p t b f", f=64)[:, :, 0:3:2, :]
                              if (T0 // 4) % 2 == 0:
                                  S.op('act', ACTV(dst, psv, AF.Copy), reads=[b_ps[bk]], writes=[b_vaug])
                              else:
                                  S.op('dve', CP(dst, psv), reads=[b_ps[bk]], writes=[b_vaug])
                      if r == 1:
                          groups = [[(0, jj) for jj in range(g * 4, g * 4 + 4)] for g in range(8)]
                      elif r == 4:
                          groups = [[(c, jj) for c in range(4)] for jj in range(8)]
                      else:
                          groups = [[(c, jj) for c in range(c0, c0 + 4)] for jj in range(2) for c0 in range(0, 16, 4)]
                      tasks = [(hh, gi) for hh in range(2) for gi in range(len(groups))]
                      etslot = {}
                      for hh in range(2):
                          h = hp * 2 + hh
                          slope = 2.0 ** (-8.0 * (h + 1) / 16.0)
                          es_ = cnt['et'] % 3
                          cnt['et'] += 1
                          etslot[hh] = es_
                          S.op('act', ACTV(tmpE[:], dcl[:].rearrange("p a b -> p (a b)"), AF.Exp, scale=-slope * r),
                               reads=[b_const], writes=[b_tmpE])
                          for u2 in range(2):
                              S.op('dve', TT(etab[es_][:, u2, :], tmpE[:], m01[:].rearrange("p a b -> p (a b)"), ALU.mult),
                                   reads=[b_tmpE, b_const], writes=[b_etab[es_]])

                      def emit_st(task):
                          hh, gi = task
                          grp = groups[gi]
                          qh = qA0 if hh == 0 else qB0
                          slots = []
                          for pair2 in range(2):
                              bk = nextbank(APOOL)
                              v4 = ps[bk][:].rearrange("p (u h i) -> p u h i", h=2, i=128)
                              mms = []
                              for u2 in range(2):
                                  c, jj = grp[pair2 * 2 + u2]
                                  qs = tokset(qh, r, c, jj)
                                  if jj > 0:
                                      mms.append(MM(v4[:, u2, 0, :], lhsT=tokset(kT, r, c, jj - 1), rhs=qs))
                                  mms.append(MM(v4[:, u2, 1, :], lhsT=tokset(kT, r, c, jj), rhs=qs))
                              for i, m in enumerate(mms):
                                  S.op('pe', m, reads=[b_k, b_q], writes=[b_ps[bk]], sig=(i == len(mms) - 1))
                              sl = cnt['st'] % NSL
                              cnt['st'] += 1
                              S.op('act', ACTV(P0[sl][:], ps[bk][:, :], AF.Exp), reads=[b_ps[bk]], writes=[b_P0[sl]])
                              S.op('pool' if cnt['st'] % 3 == 0 else 'dve',
                                   TT(PT[sl][:], P0[sl][:], etab[etslot[hh]][:].rearrange("p a b -> p (a b)"), ALU.mult),
                                   reads=[b_P0[sl], b_etab[etslot[hh]]], writes=[b_PT[sl]])
                              slots.append(sl)
                          return slots

                      def emit_pv(task, slots):
                          hh, gi = task
                          grp = groups[gi]
                          ob = nextbank(OPOOL)
                          cols = slice(0, 128) if hh == 0 else slice(64, 192)
                          mms = []
                          for u in range(4):
                              c, jj = grp[u]
                              T = c * nseg + jj
                              ptv = PT[slots[u // 2]][:].rearrange("p (u h i) -> p u h i", h=2, i=128)
                              o_ap = ps[ob][:, u * 128:(u + 1) * 128]
                              if jj > 0:
                                  mms.append(MM(o_ap, lhsT=vaug[:, T - 1, cols], rhs=ptv[:, u % 2, 0, :], start=True, stop=False))
                                  mms.append(MM(o_ap, lhsT=vaug[:, T, cols], rhs=ptv[:, u % 2, 1, :], start=False, stop=True))
                              else:
                                  mms.append(MM(o_ap, lhsT=vaug[:, T, cols], rhs=ptv[:, u % 2, 1, :], start=True, stop=True))
                          for i, m in enumerate(mms):
                              S.op('pe', m, reads=[b_vaug, b_PT[slots[0]], b_PT[slots[1]]], writes=[b_ps[ob]],
                                   sig=(i == len(mms) - 1))
                          psv = ps[ob][:].rearrange("p (u i) -> p u i", i=128)
                          a = acc[hh]
                          if r == 1:
                              accv = a[:, gi * 512:(gi + 1) * 512].rearrange("p (u i) -> p u i", i=128)
                              blks = [gi]
                          elif r == 4:
                              accv = a[:, gi * 512:(gi + 1) * 512].rearrange("p (i c) -> p c i", c=4)
                              blks = [gi]
                          else:
                              jj = grp[0][1]
                              c0 = grp[0][0]
                              accv = a[:, jj * 2048:(jj + 1) * 2048].rearrange("p (i c) -> p c i", c=16)[:, c0:c0 + 4, :]
                              blks = [jj * 4 + i for i in range(4)]
                          bb = [b_acc[hh][i] for i in blks]
                          if pi == 0:
                              S.op('dve', CP(accv, psv), reads=[b_ps[ob]], writes=bb)
                          else:
                              S.op('dve', TT(accv, psv, accv, ALU.add), reads=[b_ps[ob]] + bb, writes=bb)

                      pend = [emit_st(tasks[0])]
                      build_vaug()
                      pend.append(emit_st(tasks[1]))
                      for ti in range(len(tasks)):
                          if ti + 2 < len(tasks):
                              pend.append(emit_st(tasks[ti + 2]))
                          emit_pv(tasks[ti], pend.pop(0))
                          tick['n'] += 1
                          if fillers and tick['n'] % 5 == 0:
                              fillers.pop(0)()

              def norm_pieces(hp):
                  pieces = []
                  for hh in range(2):
                      npart = slice(0, 64) if hh == 0 else slice(64, 128)
                      dpart = slice(64, 128) if hh == 0 else slice(0, 64)

                      def ln_piece(hh=hh, dpart=dpart):
                          S.op('act', ACTV(acc[hh][dpart, :], acc[hh][dpart, :], AF.Ln), reads=b_acc[hh], writes=b_acc[hh])
                      pieces.append(ln_piece)
                      for tb in range(8):
                          def blk_piece(hh=hh, tb=tb, npart=npart, dpart=dpart):
                              tsl = slice(tb * 512, (tb + 1) * 512)
                              ns = cnt['nt'] % 2
                              cnt['nt'] += 1
                              S.op('act', ACTV(ntmp[ns][npart, :], acc[hh][dpart, tsl], AF.Exp, scale=-1.0),
                                   reads=[b_acc[hh][tb]], writes=[b_ntmp[ns]])
                              S.op('dve', TT(ntmp[ns][npart, :], ntmp[ns][npart, :], acc[hh][npart, tsl], ALU.mult),
                                   reads=[b_acc[hh][tb], b_ntmp[ns]], writes=[b_ntmp[ns]])
                              S.op('dve', TT(oaT[npart, tsl], ntmp[ns][npart, :], zs[npart, tsl], ALU.mult),
                                   reads=[b_ntmp[ns], b_z], writes=[b_oaT])
                          pieces.append(blk_piece)

                  def spill_piece(hp=hp):
                      S.dma('pool', DMA(oa_scr[hp * 128:(hp + 1) * 128, :], oaT[:, :]), 'D_sp', reads=[b_oaT])
                  pieces.append(spill_piece)
                  return pieces

              emit_proj(0, (0, 1, 2, 3))
              cast_w()
              load_w(2)
              castq = []
              for hp in range(8):
                  emit_attn(hp, fillers=castq)
                  while castq:
                      castq.pop(0)()
                  if hp >= 1 and hp + 2 < 8:
                      load_w(hp + 2)
                  pieces = norm_pieces(hp)
                  if hp + 1 < 8:
                      emit_proj(hp + 1, (0, 1, 2), filler=pieces)
                  while pieces:
                      pieces.pop(0)()
                  if hp + 1 < 8:
                      emit_proj(hp + 1, (3,))
                      if hp + 2 < 8:
                          castq = cast_pieces()
              S.barrier()
              S.emit(nc, sems)

        P12.close()

        with ExitStack() as P:
          if PHASES >= 3:
              Wg = sb(P, "Wg3", [128, 8, 3072], BF16)
              b_WgS = [Buf("Wg3_%d" % i) for i in range(6)]
              Wa = sb(P, "Wa3", [128, 8, 128], BF16)
              b_Wa = Buf("Wa3")
              stage3 = [sb(P, "stage3_%d" % i, [128, 2, 512], F32) for i in range(4)]
              b_stage3 = [Buf("stage3_%d" % i) for i in range(4)]
              wal = sb(P, "wal", [128, 512], F32)
              decs = sb(P, "decs", [128, 4, NT], F32)
              b_decs = [Buf("decs%d" % i) for i in range(NT)]
              b_c3 = Buf("c3")
              b_c3g = Buf("c3g")
              qk = sb(P, "qk_sb", [128, 8, 512], BF16)
              b_qk = Buf("qk")
              acT = sb(P, "acT", [128, 512], F32)
              b_acT = Buf("acT")
              Lsb = [sb(P, "Lsb%d" % i, [128, 512], F32) for i in range(2)]
              b_L = [Buf("L%d" % i) for i in range(2)]
              epos = [sb(P, "epos%d" % i, [128, 4, 128], F32) for i in range(2)]
              eneg = [sb(P, "eneg%d" % i, [128, 4, 128], F32) for i in range(2)]
              b_ep = [Buf("ep%d" % i) for i in range(2)]
              b_en = [Buf("en%d" % i) for i in range(2)]
              qd = [sb(P, "qd%d" % i, [128, 128], BF16) for i in range(8)]
              ki = [sb(P, "ki%d" % i, [128, 128], BF16) for i in range(8)]
              atm = [sb(P, "atm%d" % i, [128, 128], BF16) for i in range(4)]
              kit = [sb(P, "kit%d" % i, [128, 128], BF16) for i in range(4)]
              vsb = [sb(P, "vsb%d" % i, [128, 256], BF16) for i in range(8)]
              zsb = [sb(P, "zsb%d" % i, [128, 256], BF16) for i in range(12)]
              b_qd = [Buf("qd%d" % i) for i in range(8)]
              b_ki = [Buf("ki%d" % i) for i in range(8)]
              b_atm = [Buf("atm%d" % i) for i in range(4)]
              b_kit = [Buf("kit%d" % i) for i in range(4)]
              b_vsb = [Buf("vsb%d" % i) for i in range(8)]
              b_zsb = [Buf("zsb%d" % i) for i in range(12)]
              state = sb(P, "state", [128, 4, 256], F32)
              stbf = sb(P, "stbf", [128, 4, 256], BF16)
              b_state = [Buf("state%d" % i) for i in range(4)]
              b_stbf = [Buf("stbf%d" % i) for i in range(4)]
              bst = [sb(P, "bst%d" % i, [128, 6], F32) for i in range(4)]
              mv = [sb(P, "mv%d" % i, [128, 4], F32) for i in range(4)]
              b_mv = [Buf("mv%d" % i) for i in range(4)]
              onf = [sb(P, "onf%d" % i, [128, 256], BF16) for i in range(4)]
              b_onf = [Buf("onf%d" % i) for i in range(4)]
              obt = [sb(P, "obt%d" % i, [128, 1024], BF16) for i in range(2)]
              b_obt = [Buf("obt%d" % i) for i in range(2)]
              obT = [sb(P, "obT%d" % i, [128, 8, 512], BF16) for i in range(2)]
              b_obT = [Buf("obT%d" % i) for i in range(2)]

              S.dma('sp', DMA(wal[:], wal_d[:, :]), 'D_c2', writes=[b_c3])
              S.op('pool', lambda e: e.memset(Wa[:], 0.0), writes=[b_Wa])
              S.op('pool', lambda e: e.memset(acT[:], 0.0), writes=[b_acT])
              S.op('pool', lambda e: e.memset(acT[0:32, :], 1.0), writes=[b_acT])
              st_a = stage3[0][:].rearrange("p a b -> p (a b)")[:, 0:128].rearrange("p (c k) -> p c k", k=16)
              S.dma('sp', DMA(st_a, wgla_d[:, :, 3072:3088]), 'D_w0', writes=[b_stage3[0]])
              for c in range(8):
                  S.op('dve', TS(Wa[:, c, 0:16], st_a[:, c, :], gainT[:, c:c + 1], None, ALU.mult),
                       reads=[b_stage3[0], b_const], writes=[b_Wa])
              n3 = 0
              for blk in range(6):
                  for cp in range(4):
                      k4 = n3 % 4
                      n3 += 1
                      S.dma('sp', DMA(stage3[k4][:], wgla_d[:, 2 * cp:2 * cp + 2, blk * 512:(blk + 1) * 512]), 'D_w%d' % k4,
                            writes=[b_stage3[k4]])
                      for a in range(2):
                          c = 2 * cp + a
                          dst = Wg[:, c, blk * 512:(blk + 1) * 512]
                          if a == 0:
                              S.op('act', ACTV(dst, stage3[k4][:, a, :], AF.Copy, scale=gainT[:, c:c + 1]),
                                   reads=[b_stage3[k4], b_const], writes=[b_WgS[blk]])
                          else:
                              S.op('dve', TS(dst, stage3[k4][:, a, :], gainT[:, c:c + 1], None, ALU.mult),
                                   reads=[b_stage3[k4], b_const], writes=[b_WgS[blk]])

              def emit_B(tb):
                  tsl = slice(tb * 512, (tb + 1) * 512)
                  bk = nextbank((0, 1, 2, 3))
                  for c in range(8):
                      S.op('pe', MM(ps[bk][:, :], lhsT=Wa[:, c, :], rhs=uT[:, c, tsl], start=(c == 0), stop=(c == 7)),
                           reads=[b_Wa, b_uT], writes=[b_ps[bk]], sig=(c == 7))
                  S.op('dve', CP(acT[0:16, :], ps[bk][0:16, :]), reads=[b_ps[bk]], writes=[b_acT])
                  for j in range(8):
                      bk = nextbank((0, 1, 2, 3))
                      for c in range(8):
                          S.op('pe', MM(ps[bk][:, :], lhsT=Wg[:, c, j * 128:(j + 1) * 128], rhs=uT[:, c, tsl],
                                        start=(c == 0), stop=(c == 7)),
                               reads=[b_WgS[j // 4], b_uT], writes=[b_ps[bk]], sig=(c == 7))
                      if j % 2 == 0:
                          S.op('act', ACTV(qk[:, j, :], ps[bk][:, :], AF.Copy), reads=[b_ps[bk]], writes=[b_qk])
                      else:
                          S.op('dve', CP(qk[:, j, :], ps[bk][:, :]), reads=[b_ps[bk]], writes=[b_qk])

              GP = (0, 1, 2, 3)
              OP = (4, 5, 6, 7)

              def emit_G1(t):
                  tt = t % 4
                  s2 = t % 2
                  csl = slice(tt * 128, (tt + 1) * 128)
                  bk = nextbank(GP)
                  S.op('pe', MM(ps[bk][:, :], lhsT=acT[:, csl], rhs=wal[:, :]), reads=[b_acT, b_c3], writes=[b_ps[bk]])
                  S.op('act', ACTV(Lsb[s2][:], ps[bk][:, :], AF.Exp, scale=-1.0), reads=[b_ps[bk]], writes=[b_L[s2]])
                  S.op('act', ACTV(Lsb[s2][:], Lsb[s2][:], AF.Ln, bias=onec[:, 0:1]), reads=[b_L[s2], b_const], writes=[b_L[s2]])

              def emit_G3(t):
                  s2 = t % 2
                  bk = nextbank(GP)
                  for h in range(4):
                      S.op('pe', MM(ps[bk][:, h * 128:(h + 1) * 128], lhsT=Lsb[s2][:, h * 128:(h + 1) * 128], rhs=tri[:, :]),
                           reads=[b_L[s2], b_const], writes=[b_ps[bk]], sig=(h == 3))
                  psb = ps[bk][:].rearrange("p (h i) -> p h i", i=128)
                  S.op('act', ACTV(epos[s2][:], psb, AF.Exp), reads=[b_ps[bk]], writes=[b_ep[s2]])
                  S.op('act', ACTV(eneg[s2][:], psb, AF.Exp, scale=-1.0), reads=[b_ps[bk]], writes=[b_en[s2]])
                  S.op('act', ACTV(decs[:, :, t], psb[:, :, 127], AF.Exp), reads=[b_ps[bk]], writes=[b_decs[t]])

              def emit_G5(t):
                  tt = t % 4
                  s2 = t % 2
                  csl = slice(tt * 128, (tt + 1) * 128)
                  for h in range(4):
                      x = s2 * 4 + h
                      S.op('dve', STT(qd[x][:], qk[:, h, csl], 128.0 ** -0.5, epos[s2][:, h, :], ALU.mult, ALU.mult),
                           reads=[b_qk, b_ep[s2]], writes=[b_qd[x]])
                      S.op('dve', TT(ki[x][:], qk[:, 4 + h, csl], eneg[s2][:, h, :], ALU.mult),
                           reads=[b_qk, b_en[s2]], writes=[b_ki[x]])

              vz_pending = []

              def emit_VZ_silu():
                  while vz_pending:
                      x, bk = vz_pending.pop(0)
                      S.op('act', ACTV(zsb[x][:], ps[bk][:, 256:512], AF.Silu), reads=[b_ps[bk]], writes=[b_zsb[x]])

              def emit_VZ(t, heads, defer_silu=False):
                  s2 = t % 2
                  for h in heads:
                      x = s2 * 4 + h
                      bk = nextbank(GP)
                      for c in range(8):
                          S.op('pe', MM(ps[bk][:, :], lhsT=uT[:, c, t * 128:(t + 1) * 128],
                                        rhs=Wg[:, c, 1024 + h * 512:1024 + (h + 1) * 512], start=(c == 0), stop=(c == 7)),
                               reads=[b_WgS[2 + h], b_uT], writes=[b_ps[bk]], sig=(c == 7))
                      if h < 2:
                          S.op('dve', CP(vsb[x][:], ps[bk][:, 0:256]), reads=[b_ps[bk]], writes=[b_vsb[x]])
                          vz_pending.append(((t % 3) * 4 + h, bk))
                      else:
                          zx = (t % 3) * 4 + h
                          S.op('act', ACTV(zsb[zx][:], ps[bk][:, 256:512], AF.Silu), reads=[b_ps[bk]], writes=[b_zsb[zx]])
                          S.op('act', ACTV(vsb[x][:], ps[bk][:, 0:256], AF.Copy), reads=[b_ps[bk]], writes=[b_vsb[x]])
                  if not defer_silu:
                      emit_VZ_silu()

              def emit_H1(t):
                  s2 = t % 2
                  abk = []
                  for h in range(4):
                      x = s2 * 4 + h
                      bk = nextbank(GP)
                      abk.append(bk)
                      S.op('pe', MM(ps[bk][:, 0:128], lhsT=ki[x][:], rhs=qd[x][:]), reads=[b_ki[x], b_qd[x]],
                           writes=[b_ps[bk]], sig=False)
                      S.op('pe', MM(ps[bk][:, 128:256], lhsT=ki[x][:], rhs=ident[:, :]), reads=[b_ki[x], b_const],
                           writes=[b_ps[bk]])
                  for h in range(4):
                      bk = abk[h]
                      S.op('dve', TT(atm[h][:], ps[bk][:, 0:128], m01[:, 1, :], ALU.mult), reads=[b_ps[bk], b_const],
                           writes=[b_atm[h]])
                      S.op('act', ACTV(kit[h][:], ps[bk][:, 128:256], AF.Copy), reads=[b_ps[bk]], writes=[b_kit[h]])

              obk_of = {}

              def emit_H2a(t):
                  s2 = t % 2
                  obk = []
                  obk_of[t] = obk
                  for h in range(4):
                      x = s2 * 4 + h
                      bk = nextbank(OP)
                      obk.append(bk)
                      if t > 0:
                          S.op('pe', MM(ps[bk][:, 0:256], lhsT=atm[h][:], rhs=vsb[x][:], start=True, stop=False),
                               reads=[b_atm[h], b_vsb[x]], writes=[b_ps[bk]], sig=False)
                          S.op('pe', MM(ps[bk][:, 0:256], lhsT=qd[x][:], rhs=stbf[:, h, :], start=False, stop=True),
                               reads=[b_qd[x], b_stbf[h]], writes=[b_ps[bk]], sig=False)
                      else:
                          S.op('pe', MM(ps[bk][:, 0:256], lhsT=atm[h][:], rhs=vsb[x][:]),
                               reads=[b_atm[h], b_vsb[x]], writes=[b_ps[bk]], sig=False)
                      S.op('pe', MM(ps[bk][:, 256:512], lhsT=kit[h][:], rhs=vsb[x][:]),
                           reads=[b_kit[h], b_vsb[x]], writes=[b_ps[bk]])

              def emit_ST(t):
                  obk = obk_of[t]
                  for h in range(4):
                      bk = obk[h]
                      if t > 0:
                          S.op('dve', STT(state[:, h, :], state[:, h, :], decs[:, h, t - 1:t], ps[bk][:, 256:512], ALU.mult, ALU.add),
                               reads=[b_ps[bk], b_decs[t - 1], b_state[h]], writes=[b_state[h]])
                      else:
                          S.op('dve', CP(state[:, h, :], ps[bk][:, 256:512]), reads=[b_ps[bk]], writes=[b_state[h]])
                      S.op('dve', TS(stbf[:, h, :], state[:, h, :], decs[:, h, t:t + 1], None, ALU.mult),
                           reads=[b_state[h], b_decs[t]], writes=[b_stbf[h]])

              def emit_gnA(t):
                  obk = obk_of[t]
                  for h in range(4):
                      bk = obk[h]
                      S.op('dve', lambda e, h=h, bk=bk: e.bn_stats(out=bst[h][:], in_=ps[bk][:, 0:256]),
                           reads=[b_ps[bk]], writes=[b_mv[h]])
                      S.op('dve', lambda e, h=h: e.bn_aggr(out=mv[h][:, 0:2], in_=bst[h][:]), reads=[b_mv[h]], writes=[b_mv[h]])
                      S.op('dve', TS(mv[h][:, 1:2], mv[h][:, 1:2], 1e-5, None, ALU.add),
                           reads=[b_mv[h]], writes=[b_mv[h]])

              def emit_gnB(t):
                  s2 = t % 2
                  obk = obk_of.pop(t)
                  for h in range(4):
                      S.op('act', ACTV(mv[h][:, 1:2], mv[h][:, 1:2], AF.Sqrt), reads=[b_mv[h]], writes=[b_mv[h]])
                  for h in range(4):
                      S.op('dve', lambda e, h=h: e.reciprocal(out=mv[h][:, 1:2], in_=mv[h][:, 1:2]),
                           reads=[b_mv[h]], writes=[b_mv[h]])
                      S.op('dve', TS(mv[h][:, 2:3], mv[h][:, 0:1], mv[h][:, 1:2], -1.0, ALU.mult, ALU.mult),
                           reads=[b_mv[h]], writes=[b_mv[h]])
                  for h in range(4):
                      x = (t % 3) * 4 + h
                      bk = obk[h]
                      S.op('dve', TS(onf[h][:], ps[bk][:, 0:256], mv[h][:, 0:1], mv[h][:, 1:2], ALU.subtract, ALU.mult),
                           reads=[b_ps[bk], b_mv[h]], writes=[b_onf[h]])
                      S.op('dve', TT(obt[s2][:, h * 256:(h + 1) * 256], onf[h][:], zsb[x][:], ALU.mult),
                           reads=[b_onf[h], b_zsb[x]], writes=[b_obt[s2]])

              def emit_T(t):
                  tb, tt = divmod(t, 4)
                  s2 = t % 2
                  ot = tb % 2
                  csl = slice(tt * 128, (tt + 1) * 128)
                  for half in range(2):
                      bk = nextbank(GP)
                      for cc in range(4):
                          c = half * 4 + cc
                          S.op('pe', MM(ps[bk][:, cc * 128:(cc + 1) * 128], lhsT=obt[s2][:, c * 128:(c + 1) * 128],
                                        rhs=ident[:, :]), reads=[b_obt[s2], b_const], writes=[b_ps[bk]], sig=(cc == 3))
                      psv = ps[bk][:].rearrange("p (a b) -> p a b", b=128)
                      if half == 0:
                          S.op('act', ACTV(obT[ot][:, half * 4:half * 4 + 4, csl], psv, AF.Copy),
                               reads=[b_ps[bk]], writes=[b_obT[ot]])
                      else:
                          S.op('dve', CP(obT[ot][:, half * 4:half * 4 + 4, csl], psv), reads=[b_ps[bk]], writes=[b_obT[ot]])
                  if tt == 3:
                      tsl = slice(tb * 512, (tb + 1) * 512)
                      S.dma('pool', DMA(ob_v[:, :, tsl], obT[ot][:]), 'D_sp%d' % ot, reads=[b_obT[ot]])

              emit_B(0)
              emit_G1(0)
              emit_VZ(0, (0, 1))
              emit_G3(0)
              emit_VZ(0, (2, 3))
              emit_G5(0)
              for t in range(NT):
                  nx = t + 1 < NT
                  emit_H1(t)
                  if t > 0:
                      emit_gnA(t - 1)
                      emit_gnB(t - 1)
                  if nx:
                      if (t + 1) % 4 == 0:
                          emit_B((t + 1) // 4)
                      emit_G1(t + 1)
                      emit_VZ(t + 1, (0, 1))
                      emit_G3(t + 1)
                  emit_H2a(t)
                  emit_ST(t)
                  if nx:
                      emit_G5(t + 1)
                      emit_VZ(t + 1, (2, 3))
                  if t > 0:
                      emit_T(t - 1)
              emit_gnA(NT - 1)
              emit_gnB(NT - 1)
              emit_T(NT - 1)
              S.barrier()
              S.emit(nc, sems)

        with ExitStack() as P:
          if PHASES >= 4:
              Wga = sb(P, "Wga", [128, 8, 1024], BF16)
              Wgb = sb(P, "Wgb", [128, 8, 1024], BF16)
              Woa = sb(P, "Woa", [128, 8, 1024], BF16)
              Wob = sb(P, "Wob", [128, 8, 1024], BF16)
              Wo = sb(P, "Wo", [128, 8, 1024], BF16)
              b_W4 = Buf("W4")
              stage4 = [sb(P, "stage4_%d" % i, [128, 512], F32) for i in range(2)]
              b_stage4 = [Buf("stage4_%d" % i) for i in range(2)]
              fg = sb(P, "fg", [128, 1024], F32)
              b_c4 = Buf("c4")
              oab = sb(P, "oab", [128, 8, 512], BF16)
              obb = sb(P, "obb", [128, 8, 512], BF16)
              b_oab, b_obb = Buf("oab"), Buf("obb")
              mT = sb(P, "mT", [128, 8, 512], BF16)
              b_mT = Buf("mT")
              sg = [sb(P, "sg%d" % i, [128, 512], F32) for i in range(4)]
              b_sg = [Buf("sg%d" % i) for i in range(4)]
              xr = [sb(P, "xr%d" % i, [128, 1024], F32) for i in range(2)]
              b_xr = [Buf("xr%d" % i) for i in range(2)]
              hs = [sb(P, "hs%d" % i, [128, 1024], F32) for i in range(2)]
              b_hs = [Buf("hs%d" % i) for i in range(2)]
              junk = sb(P, "junk4", [128, 1024], BF16)
              b_junk = Buf("junk4")
              s4 = sb(P, "s4", [128, NT], F32)
              b_s4 = [Buf("s4_%d" % i) for i in range(NT)]

              S.dma('sp', DMA(fg[:], fg_d[:, :]), 'D_c4', writes=[b_c4])
              b_Wd = {id(Woa): Buf("W4oa"), id(Wob): Buf("W4ob"), id(Wga): Buf("W4ga"), id(Wgb): Buf("W4gb"), id(Wo): Buf("W4o")}
              W4spec = ((Woa, woa_d, 0, None), (Wob, wob_d, 0, ggT), (Wga, wg_d, 0, gainT), (Wgb, wg_d, 1024, gainT))
              b_Wh = {}
              for (wt, _s, _n, _g) in W4spec + ((Wo, wo_d, 0, None),):
                  for k2 in range(4):
                      b_Wh[(id(wt), k2)] = Buf("W4_%d_%d" % (len(b_Wh), k2))
              ldcnt = {'n': 0}

              def load_half(wt, src, ncol, use_gain, k2):
                  for c in range(8):
                      sl = ldcnt['n'] % 2
                      ldcnt['n'] += 1
                      S.dma('sp', DMA(stage4[sl][:], src[:, c, ncol + k2 * 512:ncol + (k2 + 1) * 512]), 'D_w%d' % sl,
                            writes=[b_stage4[sl]])
                      dst = wt[:, c, k2 * 512:(k2 + 1) * 512]
                      bw = b_Wh[(id(wt), k2)]
                      if sl == 0:
                          if use_gain is not None:
                              S.op('act', ACTV(dst, stage4[sl][:], AF.Copy, scale=use_gain[:, c:c + 1]),
                                   reads=[b_stage4[sl], b_const, b_const3], writes=[bw])
                          else:
                              S.op('act', ACTV(dst, stage4[sl][:], AF.Copy), reads=[b_stage4[sl]], writes=[bw])
                      else:
                          if use_gain is not None:
                              S.op('dve', TS(dst, stage4[sl][:], use_gain[:, c:c + 1], None, ALU.mult),
                                   reads=[b_stage4[sl], b_const, b_const3], writes=[bw])
                          else:
                              S.op('dve', CP(dst, stage4[sl][:]), reads=[b_stage4[sl]], writes=[bw])

              S.dma('sp', DMA(oab[:], oa_v[:, :, 0:512]), 'D_ld0', writes=[b_oab])
              S.dma('sp', DMA(obb[:], ob_v[:, :, 0:512]), 'D_ld1', writes=[b_obb])
              def load_quarter(wt, src, ncol, use_gain, q):
                  for c0 in range(0, 8, 2):
                      sl = ldcnt['n'] % 2
                      ldcnt['n'] += 1
                      stv = stage4[sl][:].rearrange("p (a b) -> p a b", b=256)
                      S.dma('sp', DMA(stv, src[:, c0:c0 + 2, ncol + q * 256:ncol + (q + 1) * 256]), 'D_w%d' % sl,
                            writes=[b_stage4[sl]])
                      bw = b_Wh[(id(wt), q)]
                      for a in range(2):
                          c = c0 + a
                          dst = wt[:, c, q * 256:(q + 1) * 256]
                          eng = 'act' if a == 0 else 'dve'
                          if use_gain is not None:
                              fn = (ACTV(dst, stv[:, a, :], AF.Copy, scale=use_gain[:, c:c + 1]) if a == 0
                                    else TS(dst, stv[:, a, :], use_gain[:, c:c + 1], None, ALU.mult))
                              S.op(eng, fn, reads=[b_stage4[sl], b_const, b_const3], writes=[bw])
                          else:
                              fn = ACTV(dst, stv[:, a, :], AF.Copy) if a == 0 else CP(dst, stv[:, a, :])
                              S.op(eng, fn, reads=[b_stage4[sl]], writes=[bw])

              for q in range(4):
                  for spec in W4spec:
                      load_quarter(*spec, q)
              for k2 in range(2):
                  load_half(Wo, wo_d, 0, None, k2)
              for tb in range(8):
                  tsl = slice(tb * 512, (tb + 1) * 512)
                  for m in range(8):
                      msl = slice(m * 128, (m + 1) * 128)
                      banks = []
                      for (wt, rhs_t, rb) in ((Woa, oab, b_oab), (Wob, obb, b_obb), (Wga, None, b_uT), (Wgb, None, b_uT)):
                          bk = nextbank()
                          for c in range(8):
                              rhs = rhs_t[:, c, :] if rhs_t is not None else uT[:, c, tsl]
                              S.op('pe', MM(ps[bk][:, :], lhsT=wt[:, c, msl], rhs=rhs, start=(c == 0), stop=(c == 7)),
                                   reads=[b_Wh[(id(wt), m // 2)], rb], writes=[b_ps[bk]], sig=(c == 7))
                          banks.append(bk)
                      S.op('act', ACTV(sg[0][:], ps[banks[2]][:, :], AF.Sigmoid, bias=bgT[:, m:m + 1]),
                           reads=[b_ps[banks[2]], b_const2], writes=[b_sg[0]])
                      S.op('act', ACTV(sg[1][:], ps[banks[3]][:, :], AF.Sigmoid, bias=bgT[:, 8 + m:9 + m]),
                           reads=[b_ps[banks[3]], b_const2], writes=[b_sg[1]])
                      S.op('dve', TT(sg[2][:], ps[banks[0]][:, :], sg[0][:], ALU.mult), reads=[b_ps[banks[0]], b_sg[0]],
                           writes=[b_sg[2]])
                      S.op('dve', TT(sg[3][:], ps[banks[1]][:, :], sg[1][:], ALU.mult), reads=[b_ps[banks[1]], b_sg[1]],
                           writes=[b_sg[3]])
                      S.op('pool', TT(mT[:, m, :], sg[2][:], sg[3][:], ALU.add), reads=[b_sg[2], b_sg[3]], writes=[b_mT])
                  if tb + 1 < 8:
                      nsl = slice((tb + 1) * 512, (tb + 2) * 512)
                      S.dma('sp', DMA(oab[:], oa_v[:, :, nsl]), 'D_ld0', writes=[b_oab])
                      S.dma('sp', DMA(obb[:], ob_v[:, :, nsl]), 'D_ld1', writes=[b_obb])
                  for tt in range(4):
                      t = tb * 4 + tt
                      s2 = t % 2
                      S.dma('sp', DMA(xr[s2][:], x_d[t * 128:(t + 1) * 128, :]), 'D_xr%d' % s2, writes=[b_xr[s2]])
                      for half in range(2):
                          bk = nextbank()
                          for m in range(8):
                              S.op('pe', MM(ps[bk][:, :], lhsT=mT[:, m, tt * 128:(tt + 1) * 128],
                                            rhs=Wo[:, m, half * 512:(half + 1) * 512], start=(m == 0), stop=(m == 7)),
                                   reads=[b_mT, b_Wh[(id(Wo), half)]], writes=[b_ps[bk]], sig=(m == 7))
                          S.op('dve', TT(hs[s2][:, half * 512:(half + 1) * 512], ps[bk][:, :], xr[s2][:, half * 512:(half + 1) * 512],
                                         ALU.add), reads=[b_ps[bk], b_xr[s2]], writes=[b_hs[s2]])
                      S.op('act', ACTV(junk[:], hs[s2][:], AF.Square, accum_out=s4[:, t:t + 1]),
                           reads=[b_hs[s2]], writes=[b_junk, b_s4[t]])
                      S.op('dve', TS(s4[:, t:t + 1], s4[:, t:t + 1], 1.0 / DM, 1e-6, ALU.mult, ALU.add),
                           reads=[b_s4[t]], writes=[b_s4[t]])
                      S.op('act', ACTV(s4[:, t:t + 1], s4[:, t:t + 1], AF.Sqrt), reads=[b_s4[t]], writes=[b_s4[t]])
                      S.op('dve', lambda e, t=t: e.reciprocal(out=s4[:, t:t + 1], in_=s4[:, t:t + 1]),
                           reads=[b_s4[t]], writes=[b_s4[t]])
                      S.op('dve', STT(hs[s2][:], hs[s2][:], s4[:, t:t + 1], fg[:], ALU.mult, ALU.mult),
                           reads=[b_hs[s2], b_s4[t], b_c4], writes=[b_hs[s2]])
                      ev = S.dma('pool', DMA(out_d[t * 128:(t + 1) * 128, :], hs[s2][:]), 'D_o%d' % s2, reads=[b_hs[s2]])
                      final_evs.append(ev)
              S.barrier()
              S.emit(nc, sems)
    return nc


def _lay8(w):
    n = w.shape[1]
    return np.ascontiguousarray(w.reshape(8, 128, n).transpose(1, 0, 2))


_NC_CACHE = {}


def kernel(x, norm_gain, w_in, b_gate, w_alpha, b_alpha, gla_norm_gain,
           w_out_attn, w_out_gla, w_out, final_norm_gain):
    f = np.float32
    x = np.asarray(x, f)
    W = np.asarray(w_in, f)[0]
    gainT = np.ascontiguousarray(np.asarray(norm_gain, f)[0].reshape(8, 128).T)
    watt = np.stack([
        _lay8(np.concatenate([W[:, hp * 128:(hp + 1) * 128], W[:, 1024 + hp * 128:1024 + (hp + 1) * 128],
                              W[:, 2048 + hp * 128:2048 + (hp + 1) * 128], W[:, 3072 + hp * 128:3072 + (hp + 1) * 128]], axis=1))
        for hp in range(8)])
    cols = [W[:, 4096:4608], W[:, 4608:5120]]
    for h in range(4):
        cols.append(W[:, 5120 + h * 256:5120 + (h + 1) * 256])
        cols.append(W[:, 6144 + h * 256:6144 + (h + 1) * 256])
    cols.append(W[:, 7168:7184])
    wgla = _lay8(np.concatenate(cols, axis=1))
    wg = _lay8(W[:, 7184:9232])
    woa = _lay8(np.asarray(w_out_attn, f)[0])
    wob = _lay8(np.asarray(w_out_gla, f)[0])
    wo = _lay8(np.asarray(w_out, f)[0])
    bgT = np.ascontiguousarray(np.asarray(b_gate, f)[0].reshape(16, 128).T)
    wal = np.zeros((128, 512), f)
    wal[0:16] = np.asarray(w_alpha, f)[0]
    wal[16] = np.asarray(b_alpha, f)[0]
    ggT = np.ascontiguousarray(np.asarray(gla_norm_gain, f)[0].reshape(8, 128).T)
    fg = np.ascontiguousarray(np.broadcast_to(np.asarray(final_norm_gain, f)[None, :], (128, 1024)))

    if 'nc' not in _NC_CACHE:
        _NC_CACHE['nc'] = build()
    nc = _NC_CACHE['nc']
    shared = dict(gainT=gainT, watt=watt, wgla=wgla, wg=wg, woa=woa, wob=wob, wo=wo, bgT=bgT, wal=wal, ggT=ggT, fg=fg)
    in_maps = [dict(shared, x=np.ascontiguousarray(x[b])) for b in range(8)]
    res = run_bass_kernel_spmd(nc, in_maps, core_ids=list(range(8)))
    if DEBUG:
        kernel.dbg = res.results
    return np.stack([np.asarray(res.results[b]["out"], f) for b in range(8)], axis=0)
```

```python
import numpy as np
from contextlib import ExitStack
import concourse.bass as bass
import concourse.mybir as mybir
from concourse.bass_utils import run_bass_kernel_spmd

F32 = mybir.dt.float32
BF16 = mybir.dt.bfloat16
ALU = mybir.AluOpType
AF = mybir.ActivationFunctionType

SEQ = 4096
DM = 1024
NT = 32
PATTERNS = (1, 4, 16)
DEBUG = False
PHASES = 4


class Buf:
    def __init__(self, name):
        self.name = name
        self.w = None
        self.r = {}


class Sched:
    ENG = ('pe', 'act', 'dve', 'pool', 'sp')

    def __init__(self):
        self.q = {e: [] for e in self.ENG}
        self.cnt = {'E_' + e: 0 for e in self.ENG}
        self.known = {e: {} for e in self.ENG}
        self.pend_r = {e: [] for e in self.ENG}
        self.pend_w = {e: [] for e in self.ENG}

    def _deps(self, reads, writes):
        evs = []
        for b in reads:
            if b.w is not None:
                evs.append(b.w)
        for b in writes:
            if b.w is not None:
                evs.append(b.w)
            evs.extend(b.r.items())
        return evs

    def _waits(self, eng, evs):
        need = {}
        for (k, v) in evs:
            if v > need.get(k, 0):
                need[k] = v
        out = []
        for k, v in need.items():
            if self.known[eng].get(k, 0) >= v:
                continue
            self.known[eng][k] = v
            out.append((k, v))
        return out

    def _record(self, ev, reads, writes):
        for b in writes:
            b.w = ev
            b.r = {}
        for b in reads:
            if ev[1] > b.r.get(ev[0], 0):
                b.r[ev[0]] = ev[1]

    def op(self, eng, fn, reads=(), writes=(), sig=True):
        writes = list(writes) + [b for b in reads if b.name.startswith("ps") and b not in writes]
        waits = self._waits(eng, self._deps(reads, writes))
        key = 'E_' + eng
        if not sig:
            self.pend_r[eng].extend(reads)
            self.pend_w[eng].extend(writes)
            self.q[eng].append((fn, waits, None))
            return None
        self.cnt[key] += 1
        ev = (key, self.cnt[key])
        self.q[eng].append((fn, waits, (key, 1)))
        self._record(ev, list(reads) + self.pend_r[eng], list(writes) + self.pend_w[eng])
        self.pend_r[eng] = []
        self.pend_w[eng] = []
        return ev

    def dma(self, eng, fn, semkey, reads=(), writes=()):
        waits = self._waits(eng, self._deps(reads, writes))
        self.cnt[semkey] = self.cnt.get(semkey, 0) + 16
        ev = (semkey, self.cnt[semkey])
        self.q[eng].append((fn, waits, (semkey, 16)))
        self._record(ev, reads, writes)
        return ev

    def barrier(self):
        evs = [(k, v) for k, v in self.cnt.items() if v > 0]
        for e in self.ENG:
            w = self._waits(e, evs)
            if w:
                self.q[e].append((None, w, None))

    def emit(self, nc, sems):
        q = self.q
        self.q = {e: [] for e in self.ENG}
        with nc.Block() as block:
            def run(engname):
                def body(e):
                    for fn, waits, inc in q[engname]:
                        for (k, v) in waits:
                            e.wait_ge(sems[k], v)
                        if fn is None:
                            continue
                        ins = fn(e)
                        if inc is not None:
                            ins.then_inc(sems[inc[0]], inc[1])
                return body
            block.tensor(run('pe'))
            block.scalar(run('act'))
            block.vector(run('dve'))
            block.gpsimd(run('pool'))
            block.sync(run('sp'))


def MM(out, lhsT, rhs, start=True, stop=True):
    return lambda e: e.matmul(out, lhsT=lhsT, rhs=rhs, start=start, stop=stop)


def ACTV(out, in_, func, **kw):
    return lambda e: e.activation(out=out, in_=in_, func=func, **kw)


def TT(out, in0, in1, op):
    return lambda e: e.tensor_tensor(out=out, in0=in0, in1=in1, op=op)


def TS(out, in0, s1, s2, op0, op1=None):
    if op1 is None:
        return lambda e: e.tensor_scalar(out=out, in0=in0, scalar1=s1, scalar2=None, op0=op0)
    return lambda e: e.tensor_scalar(out=out, in0=in0, scalar1=s1, scalar2=s2, op0=op0, op1=op1)


def STT(out, in0, scalar, in1, op0, op1):
    return lambda e: e.scalar_tensor_tensor(out=out, in0=in0, scalar=scalar, in1=in1, op0=op0, op1=op1)


def CP(out, in_):
    return lambda e: e.tensor_copy(out=out, in_=in_)


def DMA(out, in_):
    return lambda e: e.dma_start(out=out, in_=in_)


SEMKEYS = ['E_pe', 'E_act', 'E_dve', 'E_pool', 'E_sp',
           'D_c0', 'D_c1', 'D_c2', 'D_c3', 'D_c4', 'D_c5', 'D_x0', 'D_x1', 'D_x2', 'D_x3', 'D_x4', 'D_x5', 'D_w', 'D_w0', 'D_w1', 'D_w2', 'D_w3', 'D_sp', 'D_sp0', 'D_sp1',
           'D_o0', 'D_o1', 'D_ld0', 'D_ld1', 'D_xr0', 'D_xr1']


def build():
    nc = bass.Bass("TRN2", target_bir_lowering=False)

    def din(name, shape, dt=F32):
        return nc.dram_tensor(name, list(shape), dt, kind="ExternalInput").ap()

    x_d = din("x", [SEQ, DM])
    gainT_d = din("gainT", [128, 8])
    watt_d = din("watt", [8, 128, 8, 512])
    wgla_d = din("wgla", [128, 8, 3088])
    wg_d = din("wg", [128, 8, 2048])
    woa_d = din("woa", [128, 8, 1024])
    wob_d = din("wob", [128, 8, 1024])
    wo_d = din("wo", [128, 8, 1024])
    bgT_d = din("bgT", [128, 16])
    wal_d = din("wal", [128, 512])
    ggT_d = din("ggT", [128, 8])
    fg_d = din("fg", [128, 1024])
    out_d = nc.dram_tensor("out", [SEQ, DM], F32, kind="ExternalOutput").ap()
    skind = "ExternalOutput" if DEBUG else "Internal"
    oa_scr = nc.dram_tensor("oa_scr", [DM, SEQ], BF16, kind=skind).ap()
    ob_scr = nc.dram_tensor("ob_scr", [DM, SEQ], BF16, kind=skind).ap()
    uT_dbg = nc.dram_tensor("uT_dbg", [128, 8, SEQ], BF16, kind="ExternalOutput").ap() if DEBUG else None
    oa_v = oa_scr.rearrange("(c p) t -> p c t", p=128)
    ob_v = ob_scr.rearrange("(c p) t -> p c t", p=128)

    S = Sched()
    final_evs = []

    with ExitStack() as G:
        def sb(es, name, shape, dt):
            return es.enter_context(nc.sbuf_tensor("s_" + name, list(shape), dt))

        sems = {k: G.enter_context(nc.semaphore(k)) for k in SEMKEYS}
        ps = [G.enter_context(nc.psum_tensor("ps%d" % i, [128, 512], F32)) for i in range(8)]
        b_ps = [Buf("ps%d" % i) for i in range(8)]
        rot = {'i': 0}

        def nextbank(pool=(0, 1, 2, 3, 4, 5, 6, 7)):
            k = rot.get(pool, 0)
            rot[pool] = k + 1
            return pool[k % len(pool)]

        uT = sb(G, "uT", [128, 8, SEQ], BF16)
        b_uT = Buf("uT")
        ident = sb(G, "ident", [128, 128], BF16)
        dmat = sb(G, "dmat", [128, 2, 128], F32)
        m01 = sb(G, "m01", [128, 2, 128], F32)
        dcl = sb(G, "dcl", [128, 2, 128], F32)
        tri = sb(G, "tri", [128, 128], F32)
        onec = sb(G, "onec", [128, 1], F32)
        gainT = sb(G, "gainT_s", [128, 8], F32)
        bgT = sb(G, "bgT_s", [128, 16], F32)
        ggT = sb(G, "ggT_s", [128, 8], F32)
        b_const3 = Buf("const3")
        b_const = Buf("const")
        b_const2 = Buf("const2")

        S.op('pool', lambda e: e.iota(dmat[:], pattern=[[-128, 2], [1, 128]], base=128, channel_multiplier=-1,
                                      allow_small_or_imprecise_dtypes=True), writes=[b_const])
        S.op('dve', TS(m01[:], dmat[:], 0.0, None, ALU.is_ge), reads=[b_const], writes=[b_const])
        S.op('dve', TS(dcl[:], dmat[:], 128.0, None, ALU.is_le), reads=[b_const], writes=[b_const])
        S.op('dve', TT(m01[:], m01[:], dcl[:], ALU.mult), reads=[b_const], writes=[b_const])
        S.op('dve', TS(dcl[:], dmat[:], 0.0, 128.0, ALU.max, ALU.min), reads=[b_const], writes=[b_const])
        S.op('dve', TS(tri[:], dmat[:, 1, :], 0.0, -1.0 / 16.0, ALU.is_ge, ALU.mult), reads=[b_const], writes=[b_const])
        S.op('dve', TS(ident[:], dmat[:, 1, :], 0.0, None, ALU.is_equal), reads=[b_const], writes=[b_const])
        S.op('dve', lambda e: e.memset(onec[:], 1.0), writes=[b_const])
        S.dma('sp', DMA(gainT[:], gainT_d[:, :]), 'D_c0', writes=[b_const])
        S.dma('sp', DMA(bgT[:], bgT_d[:, :]), 'D_c1', writes=[b_const2])
        S.dma('sp', DMA(ggT[:], ggT_d[:, :]), 'D_c3', writes=[b_const3])

        P12 = ExitStack()
        stage = sb(P12, "stage2", [128, 8, 512], F32)
        b_stage = Buf("stage2")
        Wb = sb(P12, "Wb2", [128, 8, 512], BF16)
        b_Wb = Buf("Wb2")

        with ExitStack() as P:
            S.dma('sp', DMA(stage[:], watt_d[0]), 'D_w', writes=[b_stage])
            xt = [sb(P, "xt%d" % i, [128, DM], F32) for i in range(6)]
            b_xt = [Buf("xt%d" % i) for i in range(6)]
            xn = [sb(P, "xn%d" % i, [128, DM], BF16) for i in range(2)]
            b_xn = [Buf("xn%d" % i) for i in range(2)]
            junk = sb(P, "junk1", [128, DM], BF16)
            b_junk = Buf("junk1")
            ss = sb(P, "ss", [128, NT], F32)
            rstd = sb(P, "rstd", [128, NT], F32)
            b_ss = [Buf("ss%d" % i) for i in range(NT)]
            def p1_dma(t):
                s3 = t % 6
                S.dma('sp', DMA(xt[s3][:], x_d[t * 128:(t + 1) * 128, :]), 'D_x%d' % s3, writes=[b_xt[s3]])

            epsc = sb(P, "epsc", [128, 1], F32)
            b_epsc = Buf("epsc")
            S.op('dve', lambda e: e.memset(epsc[:], 1e-6), writes=[b_epsc])
            xn3 = [sb(P, "xn3_%d" % i, [128, DM], BF16) for i in range(3)]
            b_xn3 = [Buf("xn3_%d" % i) for i in range(3)]
            pbank = {}

            def p1_A(t):
                s3 = t % 6
                S.op('act', ACTV(junk[:], xt[s3][:], AF.Square, accum_out=ss[:, t:t + 1]),
                     reads=[b_xt[s3]], writes=[b_junk, b_ss[t]])
                S.op('act', ACTV(rstd[:, t:t + 1], ss[:, t:t + 1], AF.Sqrt, scale=1.0 / DM, bias=epsc[:, 0:1]),
                     reads=[b_ss[t], b_epsc], writes=[b_ss[t]])

            def p1_D(t):
                s3 = t % 6
                s2 = t % 3
                S.op('dve', lambda e, t=t: e.reciprocal(out=rstd[:, t:t + 1], in_=rstd[:, t:t + 1]),
                     reads=[b_ss[t]], writes=[b_ss[t]])
                S.op('dve', TS(xn3[s2][:], xt[s3][:], rstd[:, t:t + 1], None, ALU.mult),
                     reads=[b_xt[s3], b_ss[t]], writes=[b_xn3[s2]])

            def p1_P(t):
                s2 = t % 3
                bks = []
                for half in range(2):
                    bk = nextbank()
                    bks.append(bk)
                    for cc in range(4):
                        c = half * 4 + cc
                        S.op('pe', MM(ps[bk][:, cc * 128:(cc + 1) * 128], lhsT=xn3[s2][:, c * 128:(c + 1) * 128],
                                      rhs=ident[:, :]), reads=[b_xn3[s2], b_const], writes=[b_ps[bk]], sig=(cc == 3))
                pbank[t] = bks

            def p1_E(t):
                bks = pbank.pop(t)
                for half in range(2):
                    bk = bks[half]
                    psv = ps[bk][:].rearrange("p (a b) -> p a b", b=128)
                    eng = 'act' if half == 0 else 'dve'
                    fn = (ACTV(uT[:, half * 4:half * 4 + 4, t * 128:(t + 1) * 128], psv, AF.Copy) if half == 0
                          else CP(uT[:, half * 4:half * 4 + 4, t * 128:(t + 1) * 128], psv))
                    S.op(eng, fn, reads=[b_ps[bk]], writes=[b_uT])

            for t in range(6):
                p1_dma(t)
            p1_A(0)
            p1_A(1)
            p1_D(0)
            for t in range(NT):
                if t + 2 < NT:
                    p1_A(t + 2)
                if t + 1 < NT:
                    p1_D(t + 1)
                p1_P(t)
                if t > 0:
                    p1_E(t - 1)
                if t + 6 < NT:
                    p1_dma(t + 6)
                if 8 <= t < 16:
                    c = t - 8
                    S.op('act', ACTV(Wb[:, c, :], stage[:, c, :], AF.Copy, scale=gainT[:, c:c + 1]),
                         reads=[b_stage, b_const], writes=[b_Wb])
                if t == 16:
                    S.dma('sp', DMA(stage[:], watt_d[1]), 'D_w', writes=[b_stage])
            p1_E(NT - 1)
            if DEBUG:
                S.dma('sp', DMA(uT_dbg[:, :, :], uT[:]), 'D_c5', reads=[b_uT])
            S.barrier()
            S.emit(nc, sems)

        with ExitStack() as P:
          if PHASES >= 2:
              qA0 = sb(P, "qA0", [128, SEQ], BF16)
              qB0 = sb(P, "qB0", [128, SEQ], BF16)
              kT = sb(P, "kT", [128, SEQ], BF16)
              vT = sb(P, "vT", [128, SEQ], BF16)
              zs = sb(P, "zs", [128, SEQ], BF16)
              oaT = sb(P, "oaT", [128, SEQ], BF16)
              b_oaT = Buf("oaT")
              b_q, b_k, b_v, b_z = Buf("q"), Buf("k"), Buf("v"), Buf("z")
              vaug = sb(P, "vaug", [128, NT, 192], BF16)
              b_vaug = Buf("vaug")
              acc = [sb(P, "acc%d" % i, [128, SEQ], F32) for i in range(2)]
              b_acc = [[Buf("acc%d_%d" % (i, j)) for j in range(8)] for i in range(2)]
              NSL = 6
              P0 = [sb(P, "P0_%d" % i, [128, 512], BF16) for i in range(NSL)]
              PT = [sb(P, "PT_%d" % i, [128, 512], BF16) for i in range(NSL)]
              b_P0 = [Buf("P0_%d" % i) for i in range(NSL)]
              b_PT = [Buf("PT_%d" % i) for i in range(NSL)]
              etab = [sb(P, "etab%d" % i, [128, 2, 256], BF16) for i in range(3)]
              b_etab = [Buf("etab%d" % i) for i in range(3)]
              tmpE = sb(P, "tmpE", [128, 256], F32)
              b_tmpE = Buf("tmpE")
              ntmp = [sb(P, "ntmp%d" % i, [128, 512], F32) for i in range(2)]
              b_ntmp = [Buf("ntmp%d" % i) for i in range(2)]

              S.op('pool', lambda e: e.memset(qA0[:], 0.0), writes=[b_q])
              S.op('pool', lambda e: e.memset(qB0[:], 0.0), writes=[b_q])
              S.op('pool', lambda e: e.memset(vaug[:], 1.0), writes=[b_vaug])

              def tokset(tl, r, c, jj):
                  return tl[:].rearrange("p (j i r) -> p j i r", i=128, r=r)[:, jj, :, c]

              def load_w(hp):
                  S.dma('sp', DMA(stage[:], watt_d[hp]), 'D_w', writes=[b_stage])

              def cast_pieces():
                  def mk(c):
                      def f():
                          S.op('act', ACTV(Wb[:, c, :], stage[:, c, :], AF.Copy, scale=gainT[:, c:c + 1]),
                               reads=[b_stage, b_const], writes=[b_Wb])
                      return f
                  return [mk(c) for c in range(8)]

              def cast_w():
                  for f in cast_pieces():
                      f()

              cnt = {'st': 0, 'et': 0, 'nt': 0}
              APOOL = (0, 1, 2, 3, 4, 5)
              OPOOL = (6, 7)

              def emit_proj(hp, gs, filler=None):
                  for g in gs:
                      for tb in range(8):
                          if filler:
                              filler.pop(0)()
                          bk = nextbank(APOOL)
                          for c in range(8):
                              S.op('pe', MM(ps[bk][:, :], lhsT=Wb[:, c, g * 128:(g + 1) * 128],
                                            rhs=uT[:, c, tb * 512:(tb + 1) * 512], start=(c == 0), stop=(c == 7)),
                                   reads=[b_Wb, b_uT], writes=[b_ps[bk]], sig=(c == 7))
                          tsl = slice(tb * 512, (tb + 1) * 512)
                          if g == 0:
                              S.op('act', ACTV(qA0[0:64, tsl], ps[bk][0:64, :], AF.Copy, scale=0.125),
                                   reads=[b_ps[bk]], writes=[b_q])
                              S.op('act', ACTV(qB0[64:128, tsl], ps[bk][64:128, :], AF.Copy, scale=0.125),
                                   reads=[b_ps[bk]], writes=[b_q])
                          elif g == 1:
                              S.op('act', ACTV(kT[:, tsl], ps[bk][:, :], AF.Copy), reads=[b_ps[bk]], writes=[b_k])
                          elif g == 2:
                              S.op('act', ACTV(vT[:, tsl], ps[bk][:, :], AF.Copy), reads=[b_ps[bk]], writes=[b_v])
                          else:
                              S.op('act', ACTV(zs[:, tsl], ps[bk][:, :], AF.Silu), reads=[b_ps[bk]], writes=[b_z])

              def emit_attn(hp, fillers=None):
                  tick = {'n': 0}
                  for pi, r in enumerate(PATTERNS):
                      nseg = NT // r
                      def build_vaug(r=r, nseg=nseg):
                          for T0 in range(0, NT, 4):
                              bk = nextbank(APOOL)
                              for u in range(4):
                                  c, jj = divmod(T0 + u, nseg)
                                  S.op('pe', MM(ps[bk][:, u * 128:(u + 1) * 128], lhsT=tokset(vT, r, c, jj), rhs=ident[:, :]),
                                       reads=[b_v, b_const], writes=[b_ps[bk]], sig=(u == 3))
                              psv = ps[bk][:].rearrange("p (u b f) -> p u b f", b=2, f=64)
                              dst = vaug[:, T0:T0 + 4, :].rearrange("p t (b f) -> p t b f", f=64)[:, :, 0:3:2, :]
                              if (T0 // 4) % 2 == 0:
                                  S.op('act', ACTV(dst, psv, AF.Copy), reads=[b_ps[bk]], writes=[b_vaug])
                              else:
                                  S.op('dve', CP(dst, psv), reads=[b_ps[bk]], writes=[b_vaug])
                      if r == 1:
                          groups = [[(0, jj) for jj in range(g * 4, g * 4 + 4)] for g in range(8)]
                      elif r == 4:
                          groups = [[(c, jj) for c in range(4)] for jj in range(8)]
                      else:
                          groups = [[(c, jj) for c in range(c0, c0 + 4)] for jj in range(2) for c0 in range(0, 16, 4)]
                      tasks = [(hh, gi) for hh in range(2) for gi in range(len(groups))]
                      etslot = {}
                      for hh in range(2):
                          h = hp * 2 + hh
                          slope = 2.0 ** (-8.0 * (h + 1) / 16.0)
                          es_ = cnt['et'] % 3
                          cnt['et'] += 1
                          etslot[hh] = es_
                          S.op('act', ACTV(tmpE[:], dcl[:].rearrange("p a b -> p (a b)"), AF.Exp, scale=-slope * r),
                               reads=[b_const], writes=[b_tmpE])
                          for u2 in range(2):
                              S.op('dve', TT(etab[es_][:, u2, :], tmpE[:], m01[:].rearrange("p a b -> p (a b)"), ALU.mult),
                                   reads=[b_tmpE, b_const], writes=[b_etab[es_]])

                      def emit_st(task):
                          hh, gi = task
                          grp = groups[gi]
                          qh = qA0 if hh == 0 else qB0
                          slots = []
                          for pair2 in range(2):
                              bk = nextbank(APOOL)
                              v4 = ps[bk][:].rearrange("p (u h i) -> p u h i", h=2, i=128)
                              mms = []
                              for u2 in range(2):
                                  c, jj = grp[pair2 * 2 + u2]
                                  qs = tokset(qh, r, c, jj)
                                  if jj > 0:
                                      mms.append(MM(v4[:, u2, 0, :], lhsT=tokset(kT, r, c, jj - 1), rhs=qs))
                                  mms.append(MM(v4[:, u2, 1, :], lhsT=tokset(kT, r, c, jj), rhs=qs))
                              for i, m in enumerate(mms):
                                  S.op('pe', m, reads=[b_k, b_q], writes=[b_ps[bk]], sig=(i == len(mms) - 1))
                              sl = cnt['st'] % NSL
                              cnt['st'] += 1
                              S.op('act', ACTV(P0[sl][:], ps[bk][:, :], AF.Exp), reads=[b_ps[bk]], writes=[b_P0[sl]])
                              S.op('pool' if cnt['st'] % 2 == 0 else 'dve',
                                   TT(PT[sl][:], P0[sl][:], etab[etslot[hh]][:].rearrange("p a b -> p (a b)"), ALU.mult),
                                   reads=[b_P0[sl], b_etab[etslot[hh]]], writes=[b_PT[sl]])
                              slots.append(sl)
                          return slots

                      def emit_pv(task, slots):
                          hh, gi = task
                          grp = groups[gi]
                          ob = nextbank(OPOOL)
                          cols = slice(0, 128) if hh == 0 else slice(64, 192)
                          mms = []
                          for u in range(4):
                              c, jj = grp[u]
                              T = c * nseg + jj
                              ptv = PT[slots[u // 2]][:].rearrange("p (u h i) -> p u h i", h=2, i=128)
                              o_ap = ps[ob][:, u * 128:(u + 1) * 128]
                              if jj > 0:
                                  mms.append(MM(o_ap, lhsT=vaug[:, T - 1, cols], rhs=ptv[:, u % 2, 0, :], start=True, stop=False))
                                  mms.append(MM(o_ap, lhsT=vaug[:, T, cols], rhs=ptv[:, u % 2, 1, :], start=False, stop=True))
                              else:
                                  mms.append(MM(o_ap, lhsT=vaug[:, T, cols], rhs=ptv[:, u % 2, 1, :], start=True, stop=True))
                          for i, m in enumerate(mms):
                              S.op('pe', m, reads=[b_vaug, b_PT[slots[0]], b_PT[slots[1]]], writes=[b_ps[ob]],
                                   sig=(i == len(mms) - 1))
                          psv = ps[ob][:].rearrange("p (u i) -> p u i", i=128)
                          a = acc[hh]
                          if r == 1:
                              accv = a[:, gi * 512:(gi + 1) * 512].rearrange("p (u i) -> p u i", i=128)
                              blks = [gi]
                          elif r == 4:
                              accv = a[:, gi * 512:(gi + 1) * 512].rearrange("p (i c) -> p c i", c=4)
                              blks = [gi]
                          else:
                              jj = grp[0][1]
                              c0 = grp[0][0]
                              accv = a[:, jj * 2048:(jj + 1) * 2048].rearrange("p (i c) -> p c i", c=16)[:, c0:c0 + 4, :]
                              blks = [jj * 4 + i for i in range(4)]
                          bb = [b_acc[hh][i] for i in blks]
                          if pi == 0:
                              S.op('dve', CP(accv, psv), reads=[b_ps[ob]], writes=bb)
                          else:
                              S.op('dve', TT(accv, psv, accv, ALU.add), reads=[b_ps[ob]] + bb, writes=bb)

                      pend = [emit_st(tasks[0])]
                      build_vaug()
                      pend.append(emit_st(tasks[1]))
                      for ti in range(len(tasks)):
                          if ti + 2 < len(tasks):
                              pend.append(emit_st(tasks[ti + 2]))
                          emit_pv(tasks[ti], pend.pop(0))
                          tick['n'] += 1
                          if fillers and tick['n'] % 5 == 0:
                              fillers.pop(0)()

              def norm_pieces(hp):
                  pieces = []
                  for hh in range(2):
                      npart = slice(0, 64) if hh == 0 else slice(64, 128)
                      dpart = slice(64, 128) if hh == 0 else slice(0, 64)

                      def ln_piece(hh=hh, dpart=dpart):
                          S.op('act', ACTV(acc[hh][dpart, :], acc[hh][dpart, :], AF.Ln), reads=b_acc[hh], writes=b_acc[hh])
                      pieces.append(ln_piece)
                      for tb in range(8):
                          def blk_piece(hh=hh, tb=tb, npart=npart, dpart=dpart):
                              tsl = slice(tb * 512, (tb + 1) * 512)
                              ns = cnt['nt'] % 2
                              cnt['nt'] += 1
                              S.op('act', ACTV(ntmp[ns][npart, :], acc[hh][dpart, tsl], AF.Exp, scale=-1.0),
                                   reads=[b_acc[hh][tb]], writes=[b_ntmp[ns]])
                              S.op('dve', TT(ntmp[ns][npart, :], ntmp[ns][npart, :], acc[hh][npart, tsl], ALU.mult),
                                   reads=[b_acc[hh][tb], b_ntmp[ns]], writes=[b_ntmp[ns]])
                              S.op('dve', TT(oaT[npart, tsl], ntmp[ns][npart, :], zs[npart, tsl], ALU.mult),
                                   reads=[b_ntmp[ns], b_z], writes=[b_oaT])
                          pieces.append(blk_piece)

                  def spill_piece(hp=hp):
                      S.dma('pool', DMA(oa_scr[hp * 128:(hp + 1) * 128, :], oaT[:, :]), 'D_sp', reads=[b_oaT])
                  pieces.append(spill_piece)
                  return pieces

              emit_proj(0, (0, 1, 2, 3))
              cast_w()
              load_w(2)
              castq = []
              for hp in range(8):
                  emit_attn(hp, fillers=castq)
                  while castq:
                      castq.pop(0)()
                  if hp >= 1 and hp + 2 < 8:
                      load_w(hp + 2)
                  pieces = norm_pieces(hp)
                  if hp + 1 < 8:
                      emit_proj(hp + 1, (0, 1, 2), filler=pieces)
                  while pieces:
                      pieces.pop(0)()
                  if hp + 1 < 8:
                      emit_proj(hp + 1, (3,))
                      if hp + 2 < 8:
                          castq = cast_pieces()
              S.barrier()
              S.emit(nc, sems)

        P12.close()

        with ExitStack() as P:
          if PHASES >= 3:
              Wg = sb(P, "Wg3", [128, 8, 3072], BF16)
              b_WgS = [Buf("Wg3_%d" % i) for i in range(6)]
              Wa = sb(P, "Wa3", [128, 8, 128], BF16)
              b_Wa = Buf("Wa3")
              stage3 = [sb(P, "stage3_%d" % i, [128, 2, 512], F32) for i in range(4)]
              b_stage3 = [Buf("stage3_%d" % i) for i in range(4)]
              wal = sb(P, "wal", [128, 512], F32)
              decs = sb(P, "decs", [128, 4, NT], F32)
              b_decs = [Buf("decs%d" % i) for i in range(NT)]
              b_c3 = Buf("c3")
              b_c3g = Buf("c3g")
              qk = sb(P, "qk_sb", [128, 8, 512], BF16)
              b_qk = Buf("qk")
              acT = sb(P, "acT", [128, 512], F32)
              b_acT = Buf("acT")
              Lsb = [sb(P, "Lsb%d" % i, [128, 512], F32) for i in range(2)]
              b_L = [Buf("L%d" % i) for i in range(2)]
              epos = [sb(P, "epos%d" % i, [128, 4, 128], F32) for i in range(2)]
              eneg = [sb(P, "eneg%d" % i, [128, 4, 128], F32) for i in range(2)]
              b_ep = [Buf("ep%d" % i) for i in range(2)]
              b_en = [Buf("en%d" % i) for i in range(2)]
              qd = [sb(P, "qd%d" % i, [128, 128], BF16) for i in range(8)]
              ki = [sb(P, "ki%d" % i, [128, 128], BF16) for i in range(8)]
              atm = [sb(P, "atm%d" % i, [128, 128], BF16) for i in range(4)]
              kit = [sb(P, "kit%d" % i, [128, 128], BF16) for i in range(4)]
              vsb = [sb(P, "vsb%d" % i, [128, 256], BF16) for i in range(8)]
              zsb = [sb(P, "zsb%d" % i, [128, 256], BF16) for i in range(12)]
              b_qd = [Buf("qd%d" % i) for i in range(8)]
              b_ki = [Buf("ki%d" % i) for i in range(8)]
              b_atm = [Buf("atm%d" % i) for i in range(4)]
              b_kit = [Buf("kit%d" % i) for i in range(4)]
              b_vsb = [Buf("vsb%d" % i) for i in range(8)]
              b_zsb = [Buf("zsb%d" % i) for i in range(12)]
              state = sb(P, "state", [128, 4, 256], F32)
              stbf = sb(P, "stbf", [128, 4, 256], BF16)
              b_state = [Buf("state%d" % i) for i in range(4)]
              b_stbf = [Buf("stbf%d" % i) for i in range(4)]
              bst = [sb(P, "bst%d" % i, [128, 6], F32) for i in range(4)]
              mv = [sb(P, "mv%d" % i, [128, 4], F32) for i in range(4)]
              b_mv = [Buf("mv%d" % i) for i in range(4)]
              onf = [sb(P, "onf%d" % i, [128, 256], BF16) for i in range(4)]
              b_onf = [Buf("onf%d" % i) for i in range(4)]
              obt = [sb(P, "obt%d" % i, [128, 1024], BF16) for i in range(2)]
              b_obt = [Buf("obt%d" % i) for i in range(2)]
              obT = [sb(P, "obT%d" % i, [128, 8, 512], BF16) for i in range(2)]
              b_obT = [Buf("obT%d" % i) for i in range(2)]

              S.dma('sp', DMA(wal[:], wal_d[:, :]), 'D_c2', writes=[b_c3])
              S.op('pool', lambda e: e.memset(Wa[:], 0.0), writes=[b_Wa])
              S.op('pool', lambda e: e.memset(acT[:], 0.0), writes=[b_acT])
              S.op('pool', lambda e: e.memset(acT[0:32, :], 1.0), writes=[b_acT])
              st_a = stage3[0][:].rearrange("p a b -> p (a b)")[:, 0:128].rearrange("p (c k) -> p c k", k=16)
              S.dma('sp', DMA(st_a, wgla_d[:, :, 3072:3088]), 'D_w0', writes=[b_stage3[0]])
              for c in range(8):
                  S.op('dve', TS(Wa[:, c, 0:16], st_a[:, c, :], gainT[:, c:c + 1], None, ALU.mult),
                       reads=[b_stage3[0], b_const], writes=[b_Wa])
              n3 = 0
              for blk in range(6):
                  for cp in range(4):
                      k4 = n3 % 4
                      n3 += 1
                      S.dma('sp', DMA(stage3[k4][:], wgla_d[:, 2 * cp:2 * cp + 2, blk * 512:(blk + 1) * 512]), 'D_w%d' % k4,
                            writes=[b_stage3[k4]])
                      for a in range(2):
                          c = 2 * cp + a
                          dst = Wg[:, c, blk * 512:(blk + 1) * 512]
                          if a == 0:
                              S.op('act', ACTV(dst, stage3[k4][:, a, :], AF.Copy, scale=gainT[:, c:c + 1]),
                                   reads=[b_stage3[k4], b_const], writes=[b_WgS[blk]])
                          else:
                              S.op('dve', TS(dst, stage3[k4][:, a, :], gainT[:, c:c + 1], None, ALU.mult),
                                   reads=[b_stage3[k4], b_const], writes=[b_WgS[blk]])

              def emit_B(tb):
                  tsl = slice(tb * 512, (tb + 1) * 512)
                  bk = nextbank((0, 1, 2, 3))
                  for c in range(8):
                      S.op('pe', MM(ps[bk][:, :], lhsT=Wa[:, c, :], rhs=uT[:, c, tsl], start=(c == 0), stop=(c == 7)),
                           reads=[b_Wa, b_uT], writes=[b_ps[bk]], sig=(c == 7))
                  S.op('dve', CP(acT[0:16, :], ps[bk][0:16, :]), reads=[b_ps[bk]], writes=[b_acT])
                  for j in range(8):
                      bk = nextbank((0, 1, 2, 3))
                      for c in range(8):
                          S.op('pe', MM(ps[bk][:, :], lhsT=Wg[:, c, j * 128:(j + 1) * 128], rhs=uT[:, c, tsl],
                                        start=(c == 0), stop=(c == 7)),
                               reads=[b_WgS[j // 4], b_uT], writes=[b_ps[bk]], sig=(c == 7))
                      if j % 2 == 0:
                          S.op('act', ACTV(qk[:, j, :], ps[bk][:, :], AF.Copy), reads=[b_ps[bk]], writes=[b_qk])
                      else:
                          S.op('dve', CP(qk[:, j, :], ps[bk][:, :]), reads=[b_ps[bk]], writes=[b_qk])

              GP = (0, 1, 2, 3)
              OP = (4, 5, 6, 7)

              def emit_G1(t):
                  tt = t % 4
                  s2 = t % 2
                  csl = slice(tt * 128, (tt + 1) * 128)
                  bk = nextbank(GP)
                  S.op('pe', MM(ps[bk][:, :], lhsT=acT[:, csl], rhs=wal[:, :]), reads=[b_acT, b_c3], writes=[b_ps[bk]])
                  S.op('act', ACTV(Lsb[s2][:], ps[bk][:, :], AF.Exp, scale=-1.0), reads=[b_ps[bk]], writes=[b_L[s2]])
                  S.op('act', ACTV(Lsb[s2][:], Lsb[s2][:], AF.Ln, bias=onec[:, 0:1]), reads=[b_L[s2], b_const], writes=[b_L[s2]])

              def emit_G3(t):
                  s2 = t % 2
                  bk = nextbank(GP)
                  for h in range(4):
                      S.op('pe', MM(ps[bk][:, h * 128:(h + 1) * 128], lhsT=Lsb[s2][:, h * 128:(h + 1) * 128], rhs=tri[:, :]),
                           reads=[b_L[s2], b_const], writes=[b_ps[bk]], sig=(h == 3))
                  psb = ps[bk][:].rearrange("p (h i) -> p h i", i=128)
                  S.op('act', ACTV(epos[s2][:], psb, AF.Exp), reads=[b_ps[bk]], writes=[b_ep[s2]])
                  S.op('act', ACTV(eneg[s2][:], psb, AF.Exp, scale=-1.0), reads=[b_ps[bk]], writes=[b_en[s2]])
                  S.op('act', ACTV(decs[:, :, t], psb[:, :, 127], AF.Exp), reads=[b_ps[bk]], writes=[b_decs[t]])

              def emit_G5(t):
                  tt = t % 4
                  s2 = t % 2
                  csl = slice(tt * 128, (tt + 1) * 128)
                  for h in range(4):
                      x = s2 * 4 + h
                      S.op('dve', STT(qd[x][:], qk[:, h, csl], 128.0 ** -0.5, epos[s2][:, h, :], ALU.mult, ALU.mult),
                           reads=[b_qk, b_ep[s2]], writes=[b_qd[x]])
                      S.op('dve', TT(ki[x][:], qk[:, 4 + h, csl], eneg[s2][:, h, :], ALU.mult),
                           reads=[b_qk, b_en[s2]], writes=[b_ki[x]])

              vz_pending = []

              def emit_VZ_silu():
                  while vz_pending:
                      x, bk = vz_pending.pop(0)
                      S.op('act', ACTV(zsb[x][:], ps[bk][:, 256:512], AF.Silu), reads=[b_ps[bk]], writes=[b_zsb[x]])

              def emit_VZ(t, heads, defer_silu=False):
                  s2 = t % 2
                  for h in heads:
                      x = s2 * 4 + h
                      bk = nextbank(GP)
                      for c in range(8):
                          S.op('pe', MM(ps[bk][:, :], lhsT=uT[:, c, t * 128:(t + 1) * 128],
                                        rhs=Wg[:, c, 1024 + h * 512:1024 + (h + 1) * 512], start=(c == 0), stop=(c == 7)),
                               reads=[b_WgS[2 + h], b_uT], writes=[b_ps[bk]], sig=(c == 7))
                      if h < 2:
                          S.op('dve', CP(vsb[x][:], ps[bk][:, 0:256]), reads=[b_ps[bk]], writes=[b_vsb[x]])
                          vz_pending.append(((t % 3) * 4 + h, bk))
                      else:
                          zx = (t % 3) * 4 + h
                          S.op('act', ACTV(zsb[zx][:], ps[bk][:, 256:512], AF.Silu), reads=[b_ps[bk]], writes=[b_zsb[zx]])
                          S.op('act', ACTV(vsb[x][:], ps[bk][:, 0:256], AF.Copy), reads=[b_ps[bk]], writes=[b_vsb[x]])
                  if not defer_silu:
                      emit_VZ_silu()

              def emit_H1(t):
                  s2 = t % 2
                  abk = []
                  for h in range(4):
                      x = s2 * 4 + h
                      bk = nextbank(GP)
                      abk.append(bk)
                      S.op('pe', MM(ps[bk][:, 0:128], lhsT=ki[x][:], rhs=qd[x][:]), reads=[b_ki[x], b_qd[x]],
                           writes=[b_ps[bk]], sig=False)
                      S.op('pe', MM(ps[bk][:, 128:256], lhsT=ki[x][:], rhs=ident[:, :]), reads=[b_ki[x], b_const],
                           writes=[b_ps[bk]])
                  for h in range(4):
                      bk = abk[h]
                      S.op('dve', TT(atm[h][:], ps[bk][:, 0:128], m01[:, 1, :], ALU.mult), reads=[b_ps[bk], b_const],
                           writes=[b_atm[h]])
                      S.op('act', ACTV(kit[h][:], ps[bk][:, 128:256], AF.Copy), reads=[b_ps[bk]], writes=[b_kit[h]])

              obk_of = {}

              def emit_H2a(t):
                  s2 = t % 2
                  obk = []
                  obk_of[t] = obk
                  for h in range(4):
                      x = s2 * 4 + h
                      bk = nextbank(OP)
                      obk.append(bk)
                      if t > 0:
                          S.op('pe', MM(ps[bk][:, 0:256], lhsT=atm[h][:], rhs=vsb[x][:], start=True, stop=False),
                               reads=[b_atm[h], b_vsb[x]], writes=[b_ps[bk]], sig=False)
                          S.op('pe', MM(ps[bk][:, 0:256], lhsT=qd[x][:], rhs=stbf[:, h, :], start=False, stop=True),
                               reads=[b_qd[x], b_stbf[h]], writes=[b_ps[bk]], sig=False)
                      else:
                          S.op('pe', MM(ps[bk][:, 0:256], lhsT=atm[h][:], rhs=vsb[x][:]),
                               reads=[b_atm[h], b_vsb[x]], writes=[b_ps[bk]], sig=False)
                      S.op('pe', MM(ps[bk][:, 256:512], lhsT=kit[h][:], rhs=vsb[x][:]),
                           reads=[b_kit[h], b_vsb[x]], writes=[b_ps[bk]])

              def emit_ST(t):
                  obk = obk_of[t]
                  for h in range(4):
                      bk = obk[h]
                      if t > 0:
                          S.op('dve', STT(state[:, h, :], state[:, h, :], decs[:, h, t - 1:t], ps[bk][:, 256:512], ALU.mult, ALU.add),
                               reads=[b_ps[bk], b_decs[t - 1], b_state[h]], writes=[b_state[h]])
                      else:
                          S.op('dve', CP(state[:, h, :], ps[bk][:, 256:512]), reads=[b_ps[bk]], writes=[b_state[h]])
                      S.op('dve', TS(stbf[:, h, :], state[:, h, :], decs[:, h, t:t + 1], None, ALU.mult),
                           reads=[b_state[h], b_decs[t]], writes=[b_stbf[h]])

              def emit_gnA(t):
                  obk = obk_of[t]
                  for h in range(4):
                      bk = obk[h]
                      S.op('dve', lambda e, h=h, bk=bk: e.bn_stats(out=bst[h][:], in_=ps[bk][:, 0:256]),
                           reads=[b_ps[bk]], writes=[b_mv[h]])
                      S.op('dve', lambda e, h=h: e.bn_aggr(out=mv[h][:, 0:2], in_=bst[h][:]), reads=[b_mv[h]], writes=[b_mv[h]])
                      S.op('dve', TS(mv[h][:, 1:2], mv[h][:, 1:2], 1e-5, None, ALU.add),
                           reads=[b_mv[h]], writes=[b_mv[h]])

              def emit_gnB(t):
                  s2 = t % 2
                  obk = obk_of.pop(t)
                  for h in range(4):
                      S.op('act', ACTV(mv[h][:, 1:2], mv[h][:, 1:2], AF.Sqrt), reads=[b_mv[h]], writes=[b_mv[h]])
                  for h in range(4):
                      S.op('dve', lambda e, h=h: e.reciprocal(out=mv[h][:, 1:2], in_=mv[h][:, 1:2]),
                           reads=[b_mv[h]], writes=[b_mv[h]])
                      S.op('dve', TS(mv[h][:, 2:3], mv[h][:, 0:1], mv[h][:, 1:2], -1.0, ALU.mult, ALU.mult),
                           reads=[b_mv[h]], writes=[b_mv[h]])
                  for h in range(4):
                      x = (t % 3) * 4 + h
                      bk = obk[h]
                      S.op('dve', TS(onf[h][:], ps[bk][:, 0:256], mv[h][:, 0:1], mv[h][:, 1:2], ALU.subtract, ALU.mult),
                           reads=[b_ps[bk], b_mv[h]], writes=[b_onf[h]])
                      S.op('dve', TT(obt[s2][:, h * 256:(h + 1) * 256], onf[h][:], zsb[x][:], ALU.mult),
                           reads=[b_onf[h], b_zsb[x]], writes=[b_obt[s2]])

              def emit_T(t):
                  tb, tt = divmod(t, 4)
                  s2 = t % 2
                  ot = tb % 2
                  csl = slice(tt * 128, (tt + 1) * 128)
                  for half in range(2):
                      bk = nextbank(GP)
                      for cc in range(4):
                          c = half * 4 + cc
                          S.op('pe', MM(ps[bk][:, cc * 128:(cc + 1) * 128], lhsT=obt[s2][:, c * 128:(c + 1) * 128],
                                        rhs=ident[:, :]), reads=[b_obt[s2], b_const], writes=[b_ps[bk]], sig=(cc == 3))
                      psv = ps[bk][:].rearrange("p (a b) -> p a b", b=128)
                      if half == 0:
                          S.op('act', ACTV(obT[ot][:, half * 4:half * 4 + 4, csl], psv, AF.Copy),
                               reads=[b_ps[bk]], writes=[b_obT[ot]])
                      else:
                          S.op('dve', CP(obT[ot][:, half * 4:half * 4 + 4, csl], psv), reads=[b_ps[bk]], writes=[b_obT[ot]])
                  if tt == 3:
                      tsl = slice(tb * 512, (tb + 1) * 512)
                      S.dma('pool', DMA(ob_v[:, :, tsl], obT[ot][:]), 'D_sp%d' % ot, reads=[b_obT[ot]])

              emit_B(0)
              emit_G1(0)
              emit_VZ(0, (0, 1))
              emit_G3(0)
              emit_VZ(0, (2, 3))
              emit_G5(0)
              for t in range(NT):
                  nx = t + 1 < NT
                  emit_H1(t)
                  if t > 0:
                      emit_gnA(t - 1)
                      emit_gnB(t - 1)
                  if nx:
                      if (t + 1) % 4 == 0:
                          emit_B((t + 1) // 4)
                      emit_G1(t + 1)
                      emit_VZ(t + 1, (0, 1))
                      emit_G3(t + 1)
                  emit_H2a(t)
                  emit_ST(t)
                  if nx:
                      emit_G5(t + 1)
                      emit_VZ(t + 1, (2, 3))
                  if t > 0:
                      emit_T(t - 1)
              emit_gnA(NT - 1)
              emit_gnB(NT - 1)
              emit_T(NT - 1)
              S.barrier()
              S.emit(nc, sems)

        with ExitStack() as P:
          if PHASES >= 4:
              Wga = sb(P, "Wga", [128, 8, 1024], BF16)
              Wgb = sb(P, "Wgb", [128, 8, 1024], BF16)
              Woa = sb(P, "Woa", [128, 8, 1024], BF16)
              Wob = sb(P, "Wob", [128, 8, 1024], BF16)
              Wo = sb(P, "Wo", [128, 8, 1024], BF16)
              b_W4 = Buf("W4")
              stage4 = [sb(P, "stage4_%d" % i, [128, 512], F32) for i in range(2)]
              b_stage4 = [Buf("stage4_%d" % i) for i in range(2)]
              fg = sb(P, "fg", [128, 1024], F32)
              b_c4 = Buf("c4")
              oab = sb(P, "oab", [128, 8, 512], BF16)
              obb = sb(P, "obb", [128, 8, 512], BF16)
              b_oab, b_obb = Buf("oab"), Buf("obb")
              mT = sb(P, "mT", [128, 8, 512], BF16)
              b_mT = Buf("mT")
              sg = [sb(P, "sg%d" % i, [128, 512], F32) for i in range(4)]
              b_sg = [Buf("sg%d" % i) for i in range(4)]
              xr = [sb(P, "xr%d" % i, [128, 1024], F32) for i in range(2)]
              b_xr = [Buf("xr%d" % i) for i in range(2)]
              hs = [sb(P, "hs%d" % i, [128, 1024], F32) for i in range(2)]
              b_hs = [Buf("hs%d" % i) for i in range(2)]
              junk = sb(P, "junk4", [128, 1024], BF16)
              b_junk = Buf("junk4")
              s4 = sb(P, "s4", [128, NT], F32)
              b_s4 = [Buf("s4_%d" % i) for i in range(NT)]

              S.dma('sp', DMA(fg[:], fg_d[:, :]), 'D_c4', writes=[b_c4])
              b_Wd = {id(Woa): Buf("W4oa"), id(Wob): Buf("W4ob"), id(Wga): Buf("W4ga"), id(Wgb): Buf("W4gb"), id(Wo): Buf("W4o")}
              W4spec = ((Woa, woa_d, 0, None), (Wob, wob_d, 0, ggT), (Wga, wg_d, 0, gainT), (Wgb, wg_d, 1024, gainT))
              b_Wh = {}
              for (wt, _s, _n, _g) in W4spec + ((Wo, wo_d, 0, None),):
                  for k2 in range(4):
                      b_Wh[(id(wt), k2)] = Buf("W4_%d_%d" % (len(b_Wh), k2))
              ldcnt = {'n': 0}

              def load_half(wt, src, ncol, use_gain, k2):
                  for c in range(8):
                      sl = ldcnt['n'] % 2
                      ldcnt['n'] += 1
                      S.dma('sp', DMA(stage4[sl][:], src[:, c, ncol + k2 * 512:ncol + (k2 + 1) * 512]), 'D_w%d' % sl,
                            writes=[b_stage4[sl]])
                      dst = wt[:, c, k2 * 512:(k2 + 1) * 512]
                      bw = b_Wh[(id(wt), k2)]
                      if sl == 0:
                          if use_gain is not None:
                              S.op('act', ACTV(dst, stage4[sl][:], AF.Copy, scale=use_gain[:, c:c + 1]),
                                   reads=[b_stage4[sl], b_const, b_const3], writes=[bw])
                          else:
                              S.op('act', ACTV(dst, stage4[sl][:], AF.Copy), reads=[b_stage4[sl]], writes=[bw])
                      else:
                          if use_gain is not None:
                              S.op('dve', TS(dst, stage4[sl][:], use_gain[:, c:c + 1], None, ALU.mult),
                                   reads=[b_stage4[sl], b_const, b_const3], writes=[bw])
                          else:
                              S.op('dve', CP(dst, stage4[sl][:]), reads=[b_stage4[sl]], writes=[bw])

              S.dma('sp', DMA(oab[:], oa_v[:, :, 0:512]), 'D_ld0', writes=[b_oab])
              S.dma('sp', DMA(obb[:], ob_v[:, :, 0:512]), 'D_ld1', writes=[b_obb])
              def load_quarter(wt, src, ncol, use_gain, q):
                  for c0 in range(0, 8, 2):
                      sl = ldcnt['n'] % 2
                      ldcnt['n'] += 1
                      stv = stage4[sl][:].rearrange("p (a b) -> p a b", b=256)
                      S.dma('sp', DMA(stv, src[:, c0:c0 + 2, ncol + q * 256:ncol + (q + 1) * 256]), 'D_w%d' % sl,
                            writes=[b_stage4[sl]])
                      bw = b_Wh[(id(wt), q)]
                      for a in range(2):
                          c = c0 + a
                          dst = wt[:, c, q * 256:(q + 1) * 256]
                          eng = 'act' if a == 0 else 'dve'
                          if use_gain is not None:
                              fn = (ACTV(dst, stv[:, a, :], AF.Copy, scale=use_gain[:, c:c + 1]) if a == 0
                                    else TS(dst, stv[:, a, :], use_gain[:, c:c + 1], None, ALU.mult))
                              S.op(eng, fn, reads=[b_stage4[sl], b_const, b_const3], writes=[bw])
                          else:
                              fn = ACTV(dst, stv[:, a, :], AF.Copy) if a == 0 else CP(dst, stv[:, a, :])
                              S.op(eng, fn, reads=[b_stage4[sl]], writes=[bw])

              for q in range(4):
                  for spec in W4spec:
                      load_quarter(*spec, q)
              for k2 in range(2):
                  load_half(Wo, wo_d, 0, None, k2)
              for tb in range(8):
                  tsl = slice(tb * 512, (tb + 1) * 512)
                  for m in range(8):
                      msl = slice(m * 128, (m + 1) * 128)
                      banks = []
                      for (wt, rhs_t, rb) in ((Woa, oab, b_oab), (Wob, obb, b_obb), (Wga, None, b_uT), (Wgb, None, b_uT)):
                          bk = nextbank()
                          for c in range(8):
                              rhs = rhs_t[:, c, :] if rhs_t is not None else uT[:, c, tsl]
                              S.op('pe', MM(ps[bk][:, :], lhsT=wt[:, c, msl], rhs=rhs, start=(c == 0), stop=(c == 7)),
                                   reads=[b_Wh[(id(wt), m // 2)], rb], writes=[b_ps[bk]], sig=(c == 7))
                          banks.append(bk)
                      S.op('act', ACTV(sg[0][:], ps[banks[2]][:, :], AF.Sigmoid, bias=bgT[:, m:m + 1]),
                           reads=[b_ps[banks[2]], b_const2], writes=[b_sg[0]])
                      S.op('act', ACTV(sg[1][:], ps[banks[3]][:, :], AF.Sigmoid, bias=bgT[:, 8 + m:9 + m]),
                           reads=[b_ps[banks[3]], b_const2], writes=[b_sg[1]])
                      S.op('dve', TT(sg[2][:], ps[banks[0]][:, :], sg[0][:], ALU.mult), reads=[b_ps[banks[0]], b_sg[0]],
                           writes=[b_sg[2]])
                      S.op('dve', TT(sg[3][:], ps[banks[1]][:, :], sg[1][:], ALU.mult), reads=[b_ps[banks[1]], b_sg[1]],
                           writes=[b_sg[3]])
                      S.op('pool', TT(mT[:, m, :], sg[2][:], sg[3][:], ALU.add), reads=[b_sg[2], b_sg[3]], writes=[b_mT])
                  if tb + 1 < 8:
                      nsl = slice((tb + 1) * 512, (tb + 2) * 512)
                      S.dma('sp', DMA(oab[:], oa_v[:, :, nsl]), 'D_ld0', writes=[b_oab])
                      S.dma('sp', DMA(obb[:], ob_v[:, :, nsl]), 'D_ld1', writes=[b_obb])
                  for tt in range(4):
                      t = tb * 4 + tt
                      s2 = t % 2
                      S.dma('sp', DMA(xr[s2][:], x_d[t * 128:(t + 1) * 128, :]), 'D_xr%d' % s2, writes=[b_xr[s2]])
                      for half in range(2):
                          bk = nextbank()
                          for m in range(8):
                              S.op('pe', MM(ps[bk][:, :], lhsT=mT[:, m, tt * 128:(tt + 1) * 128],
                                            rhs=Wo[:, m, half * 512:(half + 1) * 512], start=(m == 0), stop=(m == 7)),
                                   reads=[b_mT, b_Wh[(id(Wo), half)]], writes=[b_ps[bk]], sig=(m == 7))
                          S.op('dve', TT(hs[s2][:, half * 512:(half + 1) * 512], ps[bk][:, :], xr[s2][:, half * 512:(half + 1) * 512],
                                         ALU.add), reads=[b_ps[bk], b_xr[s2]], writes=[b_hs[s2]])
                      S.op('act', ACTV(junk[:], hs[s2][:], AF.Square, accum_out=s4[:, t:t + 1]),
                           reads=[b_hs[s2]], writes=[b_junk, b_s4[t]])
                      S.op('dve', TS(s4[:, t:t + 1], s4[:, t:t + 1], 1.0 / DM, 1e-6, ALU.mult, ALU.add),
                           reads=[b_s4[t]], writes=[b_s4[t]])
                      S.op('act', ACTV(s4[:, t:t + 1], s4[:, t:t + 1], AF.Sqrt), reads=[b_s4[t]], writes=[b_s4[t]])
                      S.op('dve', lambda e, t=t: e.reciprocal(out=s4[:, t:t + 1], in_=s4[:, t:t + 1]),
                           reads=[b_s4[t]], writes=[b_s4[t]])
                      S.op('dve', STT(hs[s2][:], hs[s2][:], s4[:, t:t + 1], fg[:], ALU.mult, ALU.mult),
                           reads=[b_hs[s2], b_s4[t], b_c4], writes=[b_hs[s2]])
                      ev = S.dma('pool', DMA(out_d[t * 128:(t + 1) * 128, :], hs[s2][:]), 'D_o%d' % s2, reads=[b_hs[s2]])
                      final_evs.append(ev)
              S.barrier()
              S.emit(nc, sems)
    return nc


def _lay8(w):
    n = w.shape[1]
    return np.ascontiguousarray(w.reshape(8, 128, n).transpose(1, 0, 2))


_NC_CACHE = {}


def kernel(x, norm_gain, w_in, b_gate, w_alpha, b_alpha, gla_norm_gain,
           w_out_attn, w_out_gla, w_out, final_norm_gain):
    f = np.float32
    x = np.asarray(x, f)
    W = np.asarray(w_in, f)[0]
    gainT = np.ascontiguousarray(np.asarray(norm_gain, f)[0].reshape(8, 128).T)
    watt = np.stack([
        _lay8(np.concatenate([W[:, hp * 128:(hp + 1) * 128], W[:, 1024 + hp * 128:1024 + (hp + 1) * 128],
                              W[:, 2048 + hp * 128:2048 + (hp + 1) * 128], W[:, 3072 + hp * 128:3072 + (hp + 1) * 128]], axis=1))
        for hp in range(8)])
    cols = [W[:, 4096:4608], W[:, 4608:5120]]
    for h in range(4):
        cols.append(W[:, 5120 + h * 256:5120 + (h + 1) * 256])
        cols.append(W[:, 6144 + h * 256:6144 + (h + 1) * 256])
    cols.append(W[:, 7168:7184])
    wgla = _lay8(np.concatenate(cols, axis=1))
    wg = _lay8(W[:, 7184:9232])
    woa = _lay8(np.asarray(w_out_attn, f)[0])
    wob = _lay8(np.asarray(w_out_gla, f)[0])
    wo = _lay8(np.asarray(w_out, f)[0])
    bgT = np.ascontiguousarray(np.asarray(b_gate, f)[0].reshape(16, 128).T)
    wal = np.zeros((128, 512), f)
    wal[0:16] = np.asarray(w_alpha, f)[0]
    wal[16] = np.asarray(b_alpha, f)[0]
    ggT = np.ascontiguousarray(np.asarray(gla_norm_gain, f)[0].reshape(8, 128).T)
    fg = np.ascontiguousarray(np.broadcast_to(np.asarray(final_norm_gain, f)[None, :], (128, 1024)))

    if 'nc' not in _NC_CACHE:
        _NC_CACHE['nc'] = build()
    nc = _NC_CACHE['nc']
    shared = dict(gainT=gainT, watt=watt, wgla=wgla, wg=wg, woa=woa, wob=wob, wo=wo, bgT=bgT, wal=wal, ggT=ggT, fg=fg)
    in_maps = [dict(shared, x=np.ascontiguousarray(x[b])) for b in range(8)]
    res = run_bass_kernel_spmd(nc, in_maps, core_ids=list(range(8)))
    if DEBUG:
        kernel.dbg = res.results
    return np.stack([np.asarray(res.results[b]["out"], f) for b in range(8)], axis=0)
```

```python
import numpy as np
from contextlib import ExitStack
import concourse.bass as bass
import concourse.mybir as mybir
from concourse.bass_utils import run_bass_kernel_spmd

F32 = mybir.dt.float32
BF16 = mybir.dt.bfloat16
ALU = mybir.AluOpType
AF = mybir.ActivationFunctionType

SEQ = 4096
DM = 1024
NT = 32
PATTERNS = (1, 4, 16)
DEBUG = False
PHASES = 4


class Buf:
    def __init__(self, name):
        self.name = name
        self.w = None
        self.r = {}


class Sched:
    ENG = ('pe', 'act', 'dve', 'pool', 'sp')

    def __init__(self):
        self.q = {e: [] for e in self.ENG}
        self.cnt = {'E_' + e: 0 for e in self.ENG}
        self.known = {e: {} for e in self.ENG}
        self.pend_r = {e: [] for e in self.ENG}
        self.pend_w = {e: [] for e in self.ENG}

    def _deps(self, reads, writes):
        evs = []
        for b in reads:
            if b.w is not None:
                evs.append(b.w)
        for b in writes:
            if b.w is not None:
                evs.append(b.w)
            evs.extend(b.r.items())
        return evs

    def _waits(self, eng, evs):
        need = {}
        for (k, v) in evs:
            if v > need.get(k, 0):
                need[k] = v
        out = []
        for k, v in need.items():
            if self.known[eng].get(k, 0) >= v:
                continue
            self.known[eng][k] = v
            out.append((k, v))
        return out

    def _record(self, ev, reads, writes):
        for b in writes:
            b.w = ev
            b.r = {}
        for b in reads:
            if ev[1] > b.r.get(ev[0], 0):
                b.r[ev[0]] = ev[1]

    def op(self, eng, fn, reads=(), writes=(), sig=True):
        writes = list(writes) + [b for b in reads if b.name.startswith("ps") and b not in writes]
        waits = self._waits(eng, self._deps(reads, writes))
        key = 'E_' + eng
        if not sig:
            self.pend_r[eng].extend(reads)
            self.pend_w[eng].extend(writes)
            self.q[eng].append((fn, waits, None))
            return None
        self.cnt[key] += 1
        ev = (key, self.cnt[key])
        self.q[eng].append((fn, waits, (key, 1)))
        self._record(ev, list(reads) + self.pend_r[eng], list(writes) + self.pend_w[eng])
        self.pend_r[eng] = []
        self.pend_w[eng] = []
        return ev

    def dma(self, eng, fn, semkey, reads=(), writes=()):
        waits = self._waits(eng, self._deps(reads, writes))
        self.cnt[semkey] = self.cnt.get(semkey, 0) + 16
        ev = (semkey, self.cnt[semkey])
        self.q[eng].append((fn, waits, (semkey, 16)))
        self._record(ev, reads, writes)
        return ev

    def barrier(self):
        evs = [(k, v) for k, v in self.cnt.items() if v > 0]
        for e in self.ENG:
            w = self._waits(e, evs)
            if w:
                self.q[e].append((None, w, None))

    def emit(self, nc, sems):
        q = self.q
        self.q = {e: [] for e in self.ENG}
        with nc.Block() as block:
            def run(engname):
                def body(e):
                    for fn, waits, inc in q[engname]:
                        for (k, v) in waits:
                            e.wait_ge(sems[k], v)
                        if fn is None:
                            continue
                        ins = fn(e)
                        if inc is not None:
                            ins.then_inc(sems[inc[0]], inc[1])
                return body
            block.tensor(run('pe'))
            block.scalar(run('act'))
            block.vector(run('dve'))
            block.gpsimd(run('pool'))
            block.sync(run('sp'))


def MM(out, lhsT, rhs, start=True, stop=True):
    return lambda e: e.matmul(out, lhsT=lhsT, rhs=rhs, start=start, stop=stop)


def ACTV(out, in_, func, **kw):
    return lambda e: e.activation(out=out, in_=in_, func=func, **kw)


def TT(out, in0, in1, op):
    return lambda e: e.tensor_tensor(out=out, in0=in0, in1=in1, op=op)


def TS(out, in0, s1, s2, op0, op1=None):
    if op1 is None:
        return lambda e: e.tensor_scalar(out=out, in0=in0, scalar1=s1, scalar2=None, op0=op0)
    return lambda e: e.tensor_scalar(out=out, in0=in0, scalar1=s1, scalar2=s2, op0=op0, op1=op1)


def STT(out, in0, scalar, in1, op0, op1):
    return lambda e: e.scalar_tensor_tensor(out=out, in0=in0, scalar=scalar, in1=in1, op0=op0, op1=op1)


def CP(out, in_):
    return lambda e: e.tensor_copy(out=out, in_=in_)


def DMA(out, in_):
    return lambda e: e.dma_start(out=out, in_=in_)


SEMKEYS = ['E_pe', 'E_act', 'E_dve', 'E_pool', 'E_sp',
           'D_c0', 'D_c1', 'D_c2', 'D_c3', 'D_c4', 'D_c5', 'D_x0', 'D_x1', 'D_x2', 'D_x3', 'D_x4', 'D_x5', 'D_w', 'D_w0', 'D_w1', 'D_w2', 'D_w3', 'D_sp', 'D_sp0', 'D_sp1',
           'D_o0', 'D_o1', 'D_ld0', 'D_ld1', 'D_xr0', 'D_xr1']


def build():
    nc = bass.Bass("TRN2", target_bir_lowering=False)

    def din(name, shape, dt=F32):
        return nc.dram_tensor(name, list(shape), dt, kind="ExternalInput").ap()

    x_d = din("x", [SEQ, DM])
    gainT_d = din("gainT", [128, 8])
    watt_d = din("watt", [8, 128, 8, 512])
    wgla_d = din("wgla", [128, 8, 3088])
    wg_d = din("wg", [128, 8, 2048])
    woa_d = din("woa", [128, 8, 1024])
    wob_d = din("wob", [128, 8, 1024])
    wo_d = din("wo", [128, 8, 1024])
    bgT_d = din("bgT", [128, 16])
    wal_d = din("wal", [128, 512])
    ggT_d = din("ggT", [128, 8])
    fg_d = din("fg", [128, 1024])
    out_d = nc.dram_tensor("out", [SEQ, DM], F32, kind="ExternalOutput").ap()
    skind = "ExternalOutput" if DEBUG else "Internal"
    oa_scr = nc.dram_tensor("oa_scr", [DM, SEQ], BF16, kind=skind).ap()
    ob_scr = nc.dram_tensor("ob_scr", [DM, SEQ], BF16, kind=skind).ap()
    uT_dbg = nc.dram_tensor("uT_dbg", [128, 8, SEQ], BF16, kind="ExternalOutput").ap() if DEBUG else None
    oa_v = oa_scr.rearrange("(c p) t -> p c t", p=128)
    ob_v = ob_scr.rearrange("(c p) t -> p c t", p=128)

    S = Sched()
    final_evs = []

    with ExitStack() as G:
        def sb(es, name, shape, dt):
            return es.enter_context(nc.sbuf_tensor("s_" + name, list(shape), dt))

        sems = {k: G.enter_context(nc.semaphore(k)) for k in SEMKEYS}
        ps = [G.enter_context(nc.psum_tensor("ps%d" % i, [128, 512], F32)) for i in range(8)]
        b_ps = [Buf("ps%d" % i) for i in range(8)]
        rot = {'i': 0}

        def nextbank(pool=(0, 1, 2, 3, 4, 5, 6, 7)):
            k = rot.get(pool, 0)
            rot[pool] = k + 1
            return pool[k % len(pool)]

        uT = sb(G, "uT", [128, 8, SEQ], BF16)
        b_uT = Buf("uT")
        ident = sb(G, "ident", [128, 128], BF16)
        dmat = sb(G, "dmat", [128, 2, 128], F32)
        m01 = sb(G, "m01", [128, 2, 128], F32)
        dcl = sb(G, "dcl", [128, 2, 128], F32)
        tri = sb(G, "tri", [128, 128], F32)
        onec = sb(G, "onec", [128, 1], F32)
        gainT = sb(G, "gainT_s", [128, 8], F32)
        bgT = sb(G, "bgT_s", [128, 16], F32)
        ggT = sb(G, "ggT_s", [128, 8], F32)
        b_const3 = Buf("const3")
        b_const = Buf("const")
        b_const2 = Buf("const2")

        S.op('pool', lambda e: e.iota(dmat[:], pattern=[[-128, 2], [1, 128]], base=128, channel_multiplier=-1,
                                      allow_small_or_imprecise_dtypes=True), writes=[b_const])
        S.op('dve', TS(m01[:], dmat[:], 0.0, None, ALU.is_ge), reads=[b_const], writes=[b_const])
        S.op('dve', TS(dcl[:], dmat[:], 128.0, None, ALU.is_le), reads=[b_const], writes=[b_const])
        S.op('dve', TT(m01[:], m01[:], dcl[:], ALU.mult), reads=[b_const], writes=[b_const])
        S.op('dve', TS(dcl[:], dmat[:], 0.0, 128.0, ALU.max, ALU.min), reads=[b_const], writes=[b_const])
        S.op('dve', TS(tri[:], dmat[:, 1, :], 0.0, -1.0 / 16.0, ALU.is_ge, ALU.mult), reads=[b_const], writes=[b_const])
        S.op('dve', TS(ident[:], dmat[:, 1, :], 0.0, None, ALU.is_equal), reads=[b_const], writes=[b_const])
        S.op('dve', lambda e: e.memset(onec[:], 1.0), writes=[b_const])
        S.dma('sp', DMA(gainT[:], gainT_d[:, :]), 'D_c0', writes=[b_const])
        S.dma('sp', DMA(bgT[:], bgT_d[:, :]), 'D_c1', writes=[b_const2])
        S.dma('sp', DMA(ggT[:], ggT_d[:, :]), 'D_c3', writes=[b_const3])

        P12 = ExitStack()
        stage = sb(P12, "stage2", [128, 8, 512], F32)
        b_stage = Buf("stage2")
        Wb = sb(P12, "Wb2", [128, 8, 512], BF16)
        b_Wb = Buf("Wb2")

        with ExitStack() as P:
            S.dma('sp', DMA(stage[:], watt_d[0]), 'D_w', writes=[b_stage])
            xt = [sb(P, "xt%d" % i, [128, DM], F32) for i in range(6)]
            b_xt = [Buf("xt%d" % i) for i in range(6)]
            xn = [sb(P, "xn%d" % i, [128, DM], BF16) for i in range(2)]
            b_xn = [Buf("xn%d" % i) for i in range(2)]
            junk = sb(P, "junk1", [128, DM], BF16)
            b_junk = Buf("junk1")
            ss = sb(P, "ss", [128, NT], F32)
            rstd = sb(P, "rstd", [128, NT], F32)
            b_ss = [Buf("ss%d" % i) for i in range(NT)]
            def p1_dma(t):
                s3 = t % 6
                S.dma('sp', DMA(xt[s3][:], x_d[t * 128:(t + 1) * 128, :]), 'D_x%d' % s3, writes=[b_xt[s3]])

            epsc = sb(P, "epsc", [128, 1], F32)
            b_epsc = Buf("epsc")
            S.op('dve', lambda e: e.memset(epsc[:], 1e-6), writes=[b_epsc])
            xn3 = [sb(P, "xn3_%d" % i, [128, DM], BF16) for i in range(3)]
            b_xn3 = [Buf("xn3_%d" % i) for i in range(3)]
            pbank = {}

            def p1_A(t):
                s3 = t % 6
                S.op('act', ACTV(junk[:], xt[s3][:], AF.Square, accum_out=ss[:, t:t + 1]),
                     reads=[b_xt[s3]], writes=[b_junk, b_ss[t]])
                S.op('act', ACTV(rstd[:, t:t + 1], ss[:, t:t + 1], AF.Sqrt, scale=1.0 / DM, bias=epsc[:, 0:1]),
                     reads=[b_ss[t], b_epsc], writes=[b_ss[t]])

            def p1_D(t):
                s3 = t % 6
                s2 = t % 3
                S.op('dve', lambda e, t=t: e.reciprocal(out=rstd[:, t:t + 1], in_=rstd[:, t:t + 1]),
                     reads=[b_ss[t]], writes=[b_ss[t]])
                S.op('dve', TS(xn3[s2][:], xt[s3][:], rstd[:, t:t + 1], None, ALU.mult),
                     reads=[b_xt[s3], b_ss[t]], writes=[b_xn3[s2]])

            def p1_P(t):
                s2 = t % 3
                bks = []
                for half in range(2):
                    bk = nextbank()
                    bks.append(bk)
                    for cc in range(4):
                        c = half * 4 + cc
                        S.op('pe', MM(ps[bk][:, cc * 128:(cc + 1) * 128], lhsT=xn3[s2][:, c * 128:(c + 1) * 128],
                                      rhs=ident[:, :]), reads=[b_xn3[s2], b_const], writes=[b_ps[bk]], sig=(cc == 3))
                pbank[t] = bks

            def p1_E(t):
                bks = pbank.pop(t)
                for half in range(2):
                    bk = bks[half]
                    psv = ps[bk][:].rearrange("p (a b) -> p a b", b=128)
                    eng = 'act' if half == 0 else 'dve'
                    fn = (ACTV(uT[:, half * 4:half * 4 + 4, t * 128:(t + 1) * 128], psv, AF.Copy) if half == 0
                          else CP(uT[:, half * 4:half * 4 + 4, t * 128:(t + 1) * 128], psv))
                    S.op(eng, fn, reads=[b_ps[bk]], writes=[b_uT])

            for t in range(6):
                p1_dma(t)
            p1_A(0)
            p1_A(1)
            p1_D(0)
            for t in range(NT):
                if t + 2 < NT:
                    p1_A(t + 2)
                if t + 1 < NT:
                    p1_D(t + 1)
                p1_P(t)
                if t > 0:
                    p1_E(t - 1)
                if t + 6 < NT:
                    p1_dma(t + 6)
                if 8 <= t < 16:
                    c = t - 8
                    S.op('act', ACTV(Wb[:, c, :], stage[:, c, :], AF.Copy, scale=gainT[:, c:c + 1]),
                         reads=[b_stage, b_const], writes=[b_Wb])
                if t == 16:
                    S.dma('sp', DMA(stage[:], watt_d[1]), 'D_w', writes=[b_stage])
            p1_E(NT - 1)
            if DEBUG:
                S.dma('sp', DMA(uT_dbg[:, :, :], uT[:]), 'D_c5', reads=[b_uT])
            S.barrier()
            S.emit(nc, sems)

        with ExitStack() as P:
          if PHASES >= 2:
              qA0 = sb(P, "qA0", [128, SEQ], BF16)
              qB0 = sb(P, "qB0", [128, SEQ], BF16)
              kT = sb(P, "kT", [128, SEQ], BF16)
              vT = sb(P, "vT", [128, SEQ], BF16)
              zs = sb(P, "zs", [128, SEQ], BF16)
              oaT = sb(P, "oaT", [128, SEQ], BF16)
              b_oaT = Buf("oaT")
              b_q, b_k, b_v, b_z = Buf("q"), Buf("k"), Buf("v"), Buf("z")
              vaug = sb(P, "vaug", [128, NT, 192], BF16)
              b_vaug = Buf("vaug")
              acc = [sb(P, "acc%d" % i, [128, SEQ], F32) for i in range(2)]
              b_acc = [[Buf("acc%d_%d" % (i, j)) for j in range(8)] for i in range(2)]
              NSL = 6
              P0 = [sb(P, "P0_%d" % i, [128, 512], BF16) for i in range(NSL)]
              PT = [sb(P, "PT_%d" % i, [128, 512], BF16) for i in range(NSL)]
              b_P0 = [Buf("P0_%d" % i) for i in range(NSL)]
              b_PT = [Buf("PT_%d" % i) for i in range(NSL)]
              etab = [sb(P, "etab%d" % i, [128, 2, 256], BF16) for i in range(3)]
              b_etab = [Buf("etab%d" % i) for i in range(3)]
              tmpE = sb(P, "tmpE", [128, 256], F32)
              b_tmpE = Buf("tmpE")
              ntmp = [sb(P, "ntmp%d" % i, [128, 512], F32) for i in range(2)]
              b_ntmp = [Buf("ntmp%d" % i) for i in range(2)]

              S.op('pool', lambda e: e.memset(qA0[:], 0.0), writes=[b_q])
              S.op('pool', lambda e: e.memset(qB0[:], 0.0), writes=[b_q])
              S.op('pool', lambda e: e.memset(vaug[:], 1.0), writes=[b_vaug])

              def tokset(tl, r, c, jj):
                  return tl[:].rearrange("p (j i r) -> p j i r", i=128, r=r)[:, jj, :, c]

              def load_w(hp):
                  S.dma('sp', DMA(stage[:], watt_d[hp]), 'D_w', writes=[b_stage])

              def cast_pieces():
                  def mk(c):
                      def f():
                          S.op('act', ACTV(Wb[:, c, :], stage[:, c, :], AF.Copy, scale=gainT[:, c:c + 1]),
                               reads=[b_stage, b_const], writes=[b_Wb])
                      return f
                  return [mk(c) for c in range(8)]

              def cast_w():
                  for f in cast_pieces():
                      f()

              cnt = {'st': 0, 'et': 0, 'nt': 0}
              APOOL = (0, 1, 2, 3, 4, 5)
              OPOOL = (6, 7)

              def emit_proj(hp, gs, filler=None):
                  for g in gs:
                      for tb in range(8):
                          if filler:
                              filler.pop(0)()
                          bk = nextbank(APOOL)
                          for c in range(8):
                              S.op('pe', MM(ps[bk][:, :], lhsT=Wb[:, c, g * 128:(g + 1) * 128],
                                            rhs=uT[:, c, tb * 512:(tb + 1) * 512], start=(c == 0), stop=(c == 7)),
                                   reads=[b_Wb, b_uT], writes=[b_ps[bk]], sig=(c == 7))
                          tsl = slice(tb * 512, (tb + 1) * 512)
                          if g == 0:
                              S.op('act', ACTV(qA0[0:64, tsl], ps[bk][0:64, :], AF.Copy, scale=0.125),
                                   reads=[b_ps[bk]], writes=[b_q])
                              S.op('act', ACTV(qB0[64:128, tsl], ps[bk][64:128, :], AF.Copy, scale=0.125),
                                   reads=[b_ps[bk]], writes=[b_q])
                          elif g == 1:
                              S.op('act', ACTV(kT[:, tsl], ps[bk][:, :], AF.Copy), reads=[b_ps[bk]], writes=[b_k])
                          elif g == 2:
                              S.op('act', ACTV(vT[:, tsl], ps[bk][:, :], AF.Copy), reads=[b_ps[bk]], writes=[b_v])
                          else:
                              S.op('act', ACTV(zs[:, tsl], ps[bk][:, :], AF.Silu), reads=[b_ps[bk]], writes=[b_z])

              def emit_attn(hp, fillers=None):
                  tick = {'n': 0}
                  for pi, r in enumerate(PATTERNS):
                      nseg = NT // r
                      def build_vaug(r=r, nseg=nseg):
                          for T0 in range(0, NT, 4):
                              bk = nextbank(APOOL)
                              for u in range(4):
                                  c, jj = divmod(T0 + u, nseg)
                                  S.op('pe', MM(ps[bk][:, u * 128:(u + 1) * 128], lhsT=tokset(vT, r, c, jj), rhs=ident[:, :]),
                                       reads=[b_v, b_const], writes=[b_ps[bk]], sig=(u == 3))
                              psv = ps[bk][:].rearrange("p (u b f) -> p u b f", b=2, f=64)
                              dst = vaug[:, T0:T0 + 4, :].rearrange("p t (b f) -> p t b f", f=64)[:, :, 0:3:2, :]
                              if (T0 // 4) % 2 == 0:
                                  S.op('act', ACTV(dst, psv, AF.Copy), reads=[b_ps[bk]], writes=[b_vaug])
                              else:
                                  S.op('dve', CP(dst, psv), reads=[b_ps[bk]], writes=[b_vaug])
                      if r == 1:
                          groups = [[(0, jj) for jj in range(g * 4, g * 4 + 4)] for g in range(8)]
                      elif r == 4:
                          groups = [[(c, jj) for c in range(4)] for jj in range(8)]
                      else:
                          groups = [[(c, jj) for c in range(c0, c0 + 4)] for jj in range(2) for c0 in range(0, 16, 4)]
                      tasks = [(hh, gi) for hh in range(2) for gi in range(len(groups))]
                      etslot = {}
                      for hh in range(2):
                          h = hp * 2 + hh
                          slope = 2.0 ** (-8.0 * (h + 1) / 16.0)
                          es_ = cnt['et'] % 3
                          cnt['et'] += 1
                          etslot[hh] = es_
                          S.op('act', ACTV(tmpE[:], dcl[:].rearrange("p a b -> p (a b)"), AF.Exp, scale=-slope * r),
                               reads=[b_const], writes=[b_tmpE])
                          for u2 in range(2):
                              S.op('dve', TT(etab[es_][:, u2, :], tmpE[:], m01[:].rearrange("p a b -> p (a b)"), ALU.mult),
                                   reads=[b_tmpE, b_const], writes=[b_etab[es_]])

                      def emit_st(task):
                          hh, gi = task
                          grp = groups[gi]
                          qh = qA0 if hh == 0 else qB0
                          slots = []
                          for pair2 in range(2):
                              bk = nextbank(APOOL)
                              v4 = ps[bk][:].rearrange("p (u h i) -> p u h i", h=2, i=128)
                              mms = []
                              for u2 in range(2):
                                  c, jj = grp[pair2 * 2 + u2]
                                  qs = tokset(qh, r, c, jj)
                                  if jj > 0:
                                      mms.append(MM(v4[:, u2, 0, :], lhsT=tokset(kT, r, c, jj - 1), rhs=qs))
                                  mms.append(MM(v4[:, u2, 1, :], lhsT=tokset(kT, r, c, jj), rhs=qs))
                              for i, m in enumerate(mms):
                                  S.op('pe', m, reads=[b_k, b_q], writes=[b_ps[bk]], sig=(i == len(mms) - 1))
                              sl = cnt['st'] % NSL
                              cnt['st'] += 1
                              S.op('act', ACTV(P0[sl][:], ps[bk][:, :], AF.Exp), reads=[b_ps[bk]], writes=[b_P0[sl]])
                              S.op('pool' if cnt['st'] % 2 == 0 else 'dve',
                                   TT(PT[sl][:], P0[sl][:], etab[etslot[hh]][:].rearrange("p a b -> p (a b)"), ALU.mult),
                                   reads=[b_P0[sl], b_etab[etslot[hh]]], writes=[b_PT[sl]])
                              slots.append(sl)
                          return slots

                      def emit_pv(task, slots):
                          hh, gi = task
                          grp = groups[gi]
                          ob = nextbank(OPOOL)
                          cols = slice(0, 128) if hh == 0 else slice(64, 192)
                          mms = []
                          for u in range(4):
                              c, jj = grp[u]
                              T = c * nseg + jj
                              ptv = PT[slots[u // 2]][:].rearrange("p (u h i) -> p u h i", h=2, i=128)
                              o_ap = ps[ob][:, u * 128:(u + 1) * 128]
                              if jj > 0:
                                  mms.append(MM(o_ap, lhsT=vaug[:, T - 1, cols], rhs=ptv[:, u % 2, 0, :], start=True, stop=False))
                                  mms.append(MM(o_ap, lhsT=vaug[:, T, cols], rhs=ptv[:, u % 2, 1, :], start=False, stop=True))
                              else:
                                  mms.append(MM(o_ap, lhsT=vaug[:, T, cols], rhs=ptv[:, u % 2, 1, :], start=True, stop=True))
                          for i, m in enumerate(mms):
                              S.op('pe', m, reads=[b_vaug, b_PT[slots[0]], b_PT[slots[1]]], writes=[b_ps[ob]],
                                   sig=(i == len(mms) - 1))
                          psv = ps[ob][:].rearrange("p (u i) -> p u i", i=128)
                          a = acc[hh]
                          if r == 1:
                              accv = a[:, gi * 512:(gi + 1) * 512].rearrange("p (u i) -> p u i", i=128)
                              blks = [gi]
                          elif r == 4:
                              accv = a[:, gi * 512:(gi + 1) * 512].rearrange("p (i c) -> p c i", c=4)
                              blks = [gi]
                          else:
                              jj = grp[0][1]
                              c0 = grp[0][0]
                              accv = a[:, jj * 2048:(jj + 1) * 2048].rearrange("p (i c) -> p c i", c=16)[:, c0:c0 + 4, :]
                              blks = [jj * 4 + i for i in range(4)]
                          bb = [b_acc[hh][i] for i in blks]
                          if pi == 0:
                              S.op('dve', CP(accv, psv), reads=[b_ps[ob]], writes=bb)
                          else:
                              S.op('dve', TT(accv, psv, accv, ALU.add), reads=[b_ps[ob]] + bb, writes=bb)

                      pend = [emit_st(tasks[0])]
                      build_vaug()
                      pend.append(emit_st(tasks[1]))
                      for ti in range(len(tasks)):
                          if ti + 2 < len(tasks):
                              pend.append(emit_st(tasks[ti + 2]))
                          emit_pv(tasks[ti], pend.pop(0))
                          tick['n'] += 1
                          if fillers and tick['n'] % 5 == 0:
                              fillers.pop(0)()

              def norm_pieces(hp):
                  pieces = []
                  for hh in range(2):
                      npart = slice(0, 64) if hh == 0 else slice(64, 128)
                      dpart = slice(64, 128) if hh == 0 else slice(0, 64)

                      def ln_piece(hh=hh, dpart=dpart):
                          S.op('act', ACTV(acc[hh][dpart, :], acc[hh][dpart, :], AF.Ln), reads=b_acc[hh], writes=b_acc[hh])
                      pieces.append(ln_piece)
                      for tb in range(8):
                          def blk_piece(hh=hh, tb=tb, npart=npart, dpart=dpart):
                              tsl = slice(tb * 512, (tb + 1) * 512)
                              ns = cnt['nt'] % 2
                              cnt['nt'] += 1
                              S.op('act', ACTV(ntmp[ns][npart, :], acc[hh][dpart, tsl], AF.Exp, scale=-1.0),
                                   reads=[b_acc[hh][tb]], writes=[b_ntmp[ns]])
                              S.op('dve', TT(ntmp[ns][npart, :], ntmp[ns][npart, :], acc[hh][npart, tsl], ALU.mult),
                                   reads=[b_acc[hh][tb], b_ntmp[ns]], writes=[b_ntmp[ns]])
                              S.op('dve', TT(oaT[npart, tsl], ntmp[ns][npart, :], zs[npart, tsl], ALU.mult),
                                   reads=[b_ntmp[ns], b_z], writes=[b_oaT])
                          pieces.append(blk_piece)

                  def spill_piece(hp=hp):
                      S.dma('pool', DMA(oa_scr[hp * 128:(hp + 1) * 128, :], oaT[:, :]), 'D_sp', reads=[b_oaT])
                  pieces.append(spill_piece)
                  return pieces

              emit_proj(0, (0, 1, 2, 3))
              cast_w()
              load_w(2)
              castq = []
              for hp in range(8):
                  emit_attn(hp, fillers=castq)
                  while castq:
                      castq.pop(0)()
                  if hp >= 1 and hp + 2 < 8:
                      load_w(hp + 2)
                  pieces = norm_pieces(hp)
                  if hp + 1 < 8:
                      emit_proj(hp + 1, (0, 1, 2), filler=pieces)
                  while pieces:
                      pieces.pop(0)()
                  if hp + 1 < 8:
                      emit_proj(hp + 1, (3,))
                      if hp + 2 < 8:
                          castq = cast_pieces()
              S.barrier()
              S.emit(nc, sems)

        P12.close()

        with ExitStack() as P:
          if PHASES >= 3:
              Wg = sb(P, "Wg3", [128, 8, 3072], BF16)
              b_WgS = [Buf("Wg3_%d" % i) for i in range(6)]
              Wa = sb(P, "Wa3", [128, 8, 128], BF16)
              b_Wa = Buf("Wa3")
              stage3 = [sb(P, "stage3_%d" % i, [128, 2, 512], F32) for i in range(4)]
              b_stage3 = [Buf("stage3_%d" % i) for i in range(4)]
              wal = sb(P, "wal", [128, 512], F32)
              decs = sb(P, "decs", [128, 4, NT], F32)
              b_decs = [Buf("decs%d" % i) for i in range(NT)]
              b_c3 = Buf("c3")
              b_c3g = Buf("c3g")
              qk = sb(P, "qk_sb", [128, 8, 512], BF16)
              b_qk = Buf("qk")
              acT = sb(P, "acT", [128, 512], F32)
              b_acT = Buf("acT")
              Lsb = [sb(P, "Lsb%d" % i, [128, 512], F32) for i in range(2)]
              b_L = [Buf("L%d" % i) for i in range(2)]
              epos = [sb(P, "epos%d" % i, [128, 4, 128], F32) for i in range(2)]
              eneg = [sb(P, "eneg%d" % i, [128, 4, 128], F32) for i in range(2)]
              b_ep = [Buf("ep%d" % i) for i in range(2)]
              b_en = [Buf("en%d" % i) for i in range(2)]
              qd = [sb(P, "qd%d" % i, [128, 128], BF16) for i in range(8)]
              ki = [sb(P, "ki%d" % i, [128, 128], BF16) for i in range(8)]
              atm = [sb(P, "atm%d" % i, [128, 128], BF16) for i in range(4)]
              kit = [sb(P, "kit%d" % i, [128, 128], BF16) for i in range(4)]
              vsb = [sb(P, "vsb%d" % i, [128, 256], BF16) for i in range(8)]
              zsb = [sb(P, "zsb%d" % i, [128, 256], BF16) for i in range(12)]
              b_qd = [Buf("qd%d" % i) for i in range(8)]
              b_ki = [Buf("ki%d" % i) for i in range(8)]
              b_atm = [Buf("atm%d" % i) for i in range(4)]
              b_kit = [Buf("kit%d" % i) for i in range(4)]
              b_vsb = [Buf("vsb%d" % i) for i in range(8)]
              b_zsb = [Buf("zsb%d" % i) for i in range(12)]
              state = sb(P, "state", [128, 4, 256], F32)
              stbf = sb(P, "stbf", [128, 4, 256], BF16)
              b_state = [Buf("state%d" % i) for i in range(4)]
              b_stbf = [Buf("stbf%d" % i) for i in range(4)]
              bst = [sb(P, "bst%d" % i, [128, 6], F32) for i in range(4)]
              mv = [sb(P, "mv%d" % i, [128, 4], F32) for i in range(4)]
              b_mv = [Buf("mv%d" % i) for i in range(4)]
              onf = [sb(P, "onf%d" % i, [128, 256], BF16) for i in range(4)]
              b_onf = [Buf("onf%d" % i) for i in range(4)]
              obt = [sb(P, "obt%d" % i, [128, 1024], BF16) for i in range(2)]
              b_obt = [Buf("obt%d" % i) for i in range(2)]
              obT = [sb(P, "obT%d" % i, [128, 8, 512], BF16) for i in range(2)]
              b_obT = [Buf("obT%d" % i) for i in range(2)]

              S.dma('sp', DMA(wal[:], wal_d[:, :]), 'D_c2', writes=[b_c3])
              S.op('pool', lambda e: e.memset(Wa[:], 0.0), writes=[b_Wa])
              S.op('pool', lambda e: e.memset(acT[:], 0.0), writes=[b_acT])
              S.op('pool', lambda e: e.memset(acT[0:32, :], 1.0), writes=[b_acT])
              st_a = stage3[0][:].rearrange("p a b -> p (a b)")[:, 0:128].rearrange("p (c k) -> p c k", k=16)
              S.dma('sp', DMA(st_a, wgla_d[:, :, 3072:3088]), 'D_w0', writes=[b_stage3[0]])
              for c in range(8):
                  S.op('dve', TS(Wa[:, c, 0:16], st_a[:, c, :], gainT[:, c:c + 1], None, ALU.mult),
                       reads=[b_stage3[0], b_const], writes=[b_Wa])
              n3 = 0
              for blk in range(6):
                  for cp in range(4):
                      k4 = n3 % 4
                      n3 += 1
                      S.dma('sp', DMA(stage3[k4][:], wgla_d[:, 2 * cp:2 * cp + 2, blk * 512:(blk + 1) * 512]), 'D_w%d' % k4,
                            writes=[b_stage3[k4]])
                      for a in range(2):
                          c = 2 * cp + a
                          dst = Wg[:, c, blk * 512:(blk + 1) * 512]
                          if a == 0:
                              S.op('act', ACTV(dst, stage3[k4][:, a, :], AF.Copy, scale=gainT[:, c:c + 1]),
                                   reads=[b_stage3[k4], b_const], writes=[b_WgS[blk]])
                          else:
                              S.op('dve', TS(dst, stage3[k4][:, a, :], gainT[:, c:c + 1], None, ALU.mult),
                                   reads=[b_stage3[k4], b_const], writes=[b_WgS[blk]])

              def emit_B(tb):
                  tsl = slice(tb * 512, (tb + 1) * 512)
                  bk = nextbank((0, 1, 2, 3))
                  for c in range(8):
                      S.op('pe', MM(ps[bk][:, :], lhsT=Wa[:, c, :], rhs=uT[:, c, tsl], start=(c == 0), stop=(c == 7)),
                           reads=[b_Wa, b_uT], writes=[b_ps[bk]], sig=(c == 7))
                  S.op('dve', CP(acT[0:16, :], ps[bk][0:16, :]), reads=[b_ps[bk]], writes=[b_acT])
                  for j in range(8):
                      bk = nextbank((0, 1, 2, 3))
                      for c in range(8):
                          S.op('pe', MM(ps[bk][:, :], lhsT=Wg[:, c, j * 128:(j + 1) * 128], rhs=uT[:, c, tsl],
                                        start=(c == 0), stop=(c == 7)),
                               reads=[b_WgS[j // 4], b_uT], writes=[b_ps[bk]], sig=(c == 7))
                      if j % 2 == 0:
                          S.op('act', ACTV(qk[:, j, :], ps[bk][:, :], AF.Copy), reads=[b_ps[bk]], writes=[b_qk])
                      else:
                          S.op('dve', CP(qk[:, j, :], ps[bk][:, :]), reads=[b_ps[bk]], writes=[b_qk])

              GP = (0, 1, 2, 3)
              OP = (4, 5, 6, 7)

              def emit_G1(t):
                  tt = t % 4
                  s2 = t % 2
                  csl = slice(tt * 128, (tt + 1) * 128)
                  bk = nextbank(GP)
                  S.op('pe', MM(ps[bk][:, :], lhsT=acT[:, csl], rhs=wal[:, :]), reads=[b_acT, b_c3], writes=[b_ps[bk]])
                  S.op('act', ACTV(Lsb[s2][:], ps[bk][:, :], AF.Exp, scale=-1.0), reads=[b_ps[bk]], writes=[b_L[s2]])
                  S.op('act', ACTV(Lsb[s2][:], Lsb[s2][:], AF.Ln, bias=onec[:, 0:1]), reads=[b_L[s2], b_const], writes=[b_L[s2]])

              def emit_G3(t):
                  s2 = t % 2
                  bk = nextbank(GP)
                  for h in range(4):
                      S.op('pe', MM(ps[bk][:, h * 128:(h + 1) * 128], lhsT=Lsb[s2][:, h * 128:(h + 1) * 128], rhs=tri[:, :]),
                           reads=[b_L[s2], b_const], writes=[b_ps[bk]], sig=(h == 3))
                  psb = ps[bk][:].rearrange("p (h i) -> p h i", i=128)
                  S.op('act', ACTV(epos[s2][:], psb, AF.Exp), reads=[b_ps[bk]], writes=[b_ep[s2]])
                  S.op('act', ACTV(eneg[s2][:], psb, AF.Exp, scale=-1.0), reads=[b_ps[bk]], writes=[b_en[s2]])
                  S.op('act', ACTV(decs[:, :, t], psb[:, :, 127], AF.Exp), reads=[b_ps[bk]], writes=[b_decs[t]])

              def emit_G5(t):
                  tt = t % 4
                  s2 = t % 2
                  csl = slice(tt * 128, (tt + 1) * 128)
                  for h in range(4):
                      x = s2 * 4 + h
                      S.op('dve', STT(qd[x][:], qk[:, h, csl], 128.0 ** -0.5, epos[s2][:, h, :], ALU.mult, ALU.mult),
                           reads=[b_qk, b_ep[s2]], writes=[b_qd[x]])
                      S.op('dve', TT(ki[x][:], qk[:, 4 + h, csl], eneg[s2][:, h, :], ALU.mult),
                           reads=[b_qk, b_en[s2]], writes=[b_ki[x]])

              vz_pending = []

              def emit_VZ_silu():
                  while vz_pending:
                      x, bk = vz_pending.pop(0)
                      S.op('act', ACTV(zsb[x][:], ps[bk][:, 256:512], AF.Silu), reads=[b_ps[bk]], writes=[b_zsb[x]])

              def emit_VZ(t, heads, defer_silu=False):
                  s2 = t % 2
                  for h in heads:
                      x = s2 * 4 + h
                      bk = nextbank(GP)
                      for c in range(8):
                          S.op('pe', MM(ps[bk][:, :], lhsT=uT[:, c, t * 128:(t + 1) * 128],
                                        rhs=Wg[:, c, 1024 + h * 512:1024 + (h + 1) * 512], start=(c == 0), stop=(c == 7)),
                               reads=[b_WgS[2 + h], b_uT], writes=[b_ps[bk]], sig=(c == 7))
                      if h < 2:
                          S.op('dve', CP(vsb[x][:], ps[bk][:, 0:256]), reads=[b_ps[bk]], writes=[b_vsb[x]])
                          vz_pending.append(((t % 3) * 4 + h, bk))
                      else:
                          zx = (t % 3) * 4 + h
                          S.op('act', ACTV(zsb[zx][:], ps[bk][:, 256:512], AF.Silu), reads=[b_ps[bk]], writes=[b_zsb[zx]])
                          S.op('act', ACTV(vsb[x][:], ps[bk][:, 0:256], AF.Copy), reads=[b_ps[bk]], writes=[b_vsb[x]])
                  if not defer_silu:
                      emit_VZ_silu()

              def emit_H1(t):
                  s2 = t % 2
                  abk = []
                  for h in range(4):
                      x = s2 * 4 + h
                      bk = nextbank(GP)
                      abk.append(bk)
                      S.op('pe', MM(ps[bk][:, 0:128], lhsT=ki[x][:], rhs=qd[x][:]), reads=[b_ki[x], b_qd[x]],
                           writes=[b_ps[bk]], sig=False)
                      S.op('pe', MM(ps[bk][:, 128:256], lhsT=ki[x][:], rhs=ident[:, :]), reads=[b_ki[x], b_const],
                           writes=[b_ps[bk]])
                  for h in range(4):
                      bk = abk[h]
                      S.op('dve', TT(atm[h][:], ps[bk][:, 0:128], m01[:, 1, :], ALU.mult), reads=[b_ps[bk], b_const],
                           writes=[b_atm[h]])
                      S.op('act', ACTV(kit[h][:], ps[bk][:, 128:256], AF.Copy), reads=[b_ps[bk]], writes=[b_kit[h]])

              obk_of = {}

              def emit_H2a(t):
                  s2 = t % 2
                  obk = []
                  obk_of[t] = obk
                  for h in range(4):
                      x = s2 * 4 + h
                      bk = nextbank(OP)
                      obk.append(bk)
                      if t > 0:
                          S.op('pe', MM(ps[bk][:, 0:256], lhsT=atm[h][:], rhs=vsb[x][:], start=True, stop=False),
                               reads=[b_atm[h], b_vsb[x]], writes=[b_ps[bk]], sig=False)
                          S.op('pe', MM(ps[bk][:, 0:256], lhsT=qd[x][:], rhs=stbf[:, h, :], start=False, stop=True),
                               reads=[b_qd[x], b_stbf[h]], writes=[b_ps[bk]], sig=False)
                      else:
                          S.op('pe', MM(ps[bk][:, 0:256], lhsT=atm[h][:], rhs=vsb[x][:]),
                               reads=[b_atm[h], b_vsb[x]], writes=[b_ps[bk]], sig=False)
                      S.op('pe', MM(ps[bk][:, 256:512], lhsT=kit[h][:], rhs=vsb[x][:]),
                           reads=[b_kit[h], b_vsb[x]], writes=[b_ps[bk]])

              def emit_ST(t):
                  obk = obk_of[t]
                  for h in range(4):
                      bk = obk[h]
                      if t > 0:
                          S.op('dve', STT(state[:, h, :], state[:, h, :], decs[:, h, t - 1:t], ps[bk][:, 256:512], ALU.mult, ALU.add),
                               reads=[b_ps[bk], b_decs[t - 1], b_state[h]], writes=[b_state[h]])
                      else:
                          S.op('dve', CP(state[:, h, :], ps[bk][:, 256:512]), reads=[b_ps[bk]], writes=[b_state[h]])
                      S.op('dve', TS(stbf[:, h, :], state[:, h, :], decs[:, h, t:t + 1], None, ALU.mult),
                           reads=[b_state[h], b_decs[t]], writes=[b_stbf[h]])

              def emit_gnA(t):
                  obk = obk_of[t]
                  for h in range(4):
                      bk = obk[h]
                      S.op('dve', lambda e, h=h, bk=bk: e.bn_stats(out=bst[h][:], in_=ps[bk][:, 0:256]),
                           reads=[b_ps[bk]], writes=[b_mv[h]])
                      S.op('dve', lambda e, h=h: e.bn_aggr(out=mv[h][:, 0:2], in_=bst[h][:]), reads=[b_mv[h]], writes=[b_mv[h]])
                      S.op('dve', TS(mv[h][:, 1:2], mv[h][:, 1:2], 1e-5, None, ALU.add),
                           reads=[b_mv[h]], writes=[b_mv[h]])

              def emit_gnB(t):
                  s2 = t % 2
                  obk = obk_of.pop(t)
                  for h in range(4):
                      S.op('act', ACTV(mv[h][:, 1:2], mv[h][:, 1:2], AF.Sqrt), reads=[b_mv[h]], writes=[b_mv[h]])
                  for h in range(4):
                      S.op('dve', lambda e, h=h: e.reciprocal(out=mv[h][:, 1:2], in_=mv[h][:, 1:2]),
                           reads=[b_mv[h]], writes=[b_mv[h]])
                      S.op('dve', TS(mv[h][:, 2:3], mv[h][:, 0:1], mv[h][:, 1:2], -1.0, ALU.mult, ALU.mult),
                           reads=[b_mv[h]], writes=[b_mv[h]])
                  for h in range(4):
                      x = (t % 3) * 4 + h
                      bk = obk[h]
                      S.op('dve', TS(onf[h][:], ps[bk][:, 0:256], mv[h][:, 0:1], mv[h][:, 1:2], ALU.subtract, ALU.mult),
                           reads=[b_ps[bk], b_mv[h]], writes=[b_onf[h]])
                      S.op('pool', TT(obt[s2][:, h * 256:(h + 1) * 256], onf[h][:], zsb[x][:], ALU.mult),
                           reads=[b_onf[h], b_zsb[x]], writes=[b_obt[s2]])

              def emit_T(t):
                  tb, tt = divmod(t, 4)
                  s2 = t % 2
                  ot = tb % 2
                  csl = slice(tt * 128, (tt + 1) * 128)
                  for half in range(2):
                      bk = nextbank(GP)
                      for cc in range(4):
                          c = half * 4 + cc
                          S.op('pe', MM(ps[bk][:, cc * 128:(cc + 1) * 128], lhsT=obt[s2][:, c * 128:(c + 1) * 128],
                                        rhs=ident[:, :]), reads=[b_obt[s2], b_const], writes=[b_ps[bk]], sig=(cc == 3))
                      psv = ps[bk][:].rearrange("p (a b) -> p a b", b=128)
                      if half == 0:
                          S.op('act', ACTV(obT[ot][:, half * 4:half * 4 + 4, csl], psv, AF.Copy),
                               reads=[b_ps[bk]], writes=[b_obT[ot]])
                      else:
                          S.op('dve', CP(obT[ot][:, half * 4:half * 4 + 4, csl], psv), reads=[b_ps[bk]], writes=[b_obT[ot]])
                  if tt == 3:
                      tsl = slice(tb * 512, (tb + 1) * 512)
                      S.dma('pool', DMA(ob_v[:, :, tsl], obT[ot][:]), 'D_sp%d' % ot, reads=[b_obT[ot]])

              emit_B(0)
              emit_G1(0)
              emit_VZ(0, (0, 1))
              emit_G3(0)
              emit_VZ(0, (2, 3))
              emit_G5(0)
              for t in range(NT):
                  nx = t + 1 < NT
                  emit_H1(t)
                  if t > 0:
                      emit_gnA(t - 1)
                      emit_gnB(t - 1)
                  if nx:
                      if (t + 1) % 4 == 0:
                          emit_B((t + 1) // 4)
                      emit_G1(t + 1)
                      emit_VZ(t + 1, (0, 1))
                      emit_G3(t + 1)
                  emit_H2a(t)
                  emit_ST(t)
                  if nx:
                      emit_G5(t + 1)
                      emit_VZ(t + 1, (2, 3))
                  if t > 0:
                      emit_T(t - 1)
              emit_gnA(NT - 1)
              emit_gnB(NT - 1)
              emit_T(NT - 1)
              S.barrier()
              S.emit(nc, sems)

        with ExitStack() as P:
          if PHASES >= 4:
              Wga = sb(P, "Wga", [128, 8, 1024], BF16)
              Wgb = sb(P, "Wgb", [128, 8, 1024], BF16)
              Woa = sb(P, "Woa", [128, 8, 1024], BF16)
              Wob = sb(P, "Wob", [128, 8, 1024], BF16)
              Wo = sb(P, "Wo", [128, 8, 1024], BF16)
              b_W4 = Buf("W4")
              stage4 = [sb(P, "stage4_%d" % i, [128, 512], F32) for i in range(2)]
              b_stage4 = [Buf("stage4_%d" % i) for i in range(2)]
              fg = sb(P, "fg", [128, 1024], F32)
              b_c4 = Buf("c4")
              oab = sb(P, "oab", [128, 8, 512], BF16)
              obb = sb(P, "obb", [128, 8, 512], BF16)
              b_oab, b_obb = Buf("oab"), Buf("obb")
              mT = sb(P, "mT", [128, 8, 512], BF16)
              b_mT = Buf("mT")
              sg = [sb(P, "sg%d" % i, [128, 512], F32) for i in range(4)]
              b_sg = [Buf("sg%d" % i) for i in range(4)]
              xr = [sb(P, "xr%d" % i, [128, 1024], F32) for i in range(2)]
              b_xr = [Buf("xr%d" % i) for i in range(2)]
              hs = [sb(P, "hs%d" % i, [128, 1024], F32) for i in range(2)]
              b_hs = [Buf("hs%d" % i) for i in range(2)]
              junk = sb(P, "junk4", [128, 1024], BF16)
              b_junk = Buf("junk4")
              s4 = sb(P, "s4", [128, NT], F32)
              b_s4 = [Buf("s4_%d" % i) for i in range(NT)]

              S.dma('sp', DMA(fg[:], fg_d[:, :]), 'D_c4', writes=[b_c4])
              b_Wd = {id(Woa): Buf("W4oa"), id(Wob): Buf("W4ob"), id(Wga): Buf("W4ga"), id(Wgb): Buf("W4gb"), id(Wo): Buf("W4o")}
              W4spec = ((Woa, woa_d, 0, None), (Wob, wob_d, 0, ggT), (Wga, wg_d, 0, gainT), (Wgb, wg_d, 1024, gainT))
              b_Wh = {}
              for (wt, _s, _n, _g) in W4spec + ((Wo, wo_d, 0, None),):
                  for k2 in range(4):
                      b_Wh[(id(wt), k2)] = Buf("W4_%d_%d" % (len(b_Wh), k2))
              ldcnt = {'n': 0}

              def load_half(wt, src, ncol, use_gain, k2):
                  for c in range(8):
                      sl = ldcnt['n'] % 2
                      ldcnt['n'] += 1
                      S.dma('sp', DMA(stage4[sl][:], src[:, c, ncol + k2 * 512:ncol + (k2 + 1) * 512]), 'D_w%d' % sl,
                            writes=[b_stage4[sl]])
                      dst = wt[:, c, k2 * 512:(k2 + 1) * 512]
                      bw = b_Wh[(id(wt), k2)]
                      if sl == 0:
                          if use_gain is not None:
                              S.op('act', ACTV(dst, stage4[sl][:], AF.Copy, scale=use_gain[:, c:c + 1]),
                                   reads=[b_stage4[sl], b_const, b_const3], writes=[bw])
                          else:
                              S.op('act', ACTV(dst, stage4[sl][:], AF.Copy), reads=[b_stage4[sl]], writes=[bw])
                      else:
                          if use_gain is not None:
                              S.op('dve', TS(dst, stage4[sl][:], use_gain[:, c:c + 1], None, ALU.mult),
                                   reads=[b_stage4[sl], b_const, b_const3], writes=[bw])
                          else:
                              S.op('dve', CP(dst, stage4[sl][:]), reads=[b_stage4[sl]], writes=[bw])

              S.dma('sp', DMA(oab[:], oa_v[:, :, 0:512]), 'D_ld0', writes=[b_oab])
              S.dma('sp', DMA(obb[:], ob_v[:, :, 0:512]), 'D_ld1', writes=[b_obb])
              def load_quarter(wt, src, ncol, use_gain, q):
                  for c0 in range(0, 8, 2):
                      sl = ldcnt['n'] % 2
                      ldcnt['n'] += 1
                      stv = stage4[sl][:].rearrange("p (a b) -> p a b", b=256)
                      S.dma('sp', DMA(stv, src[:, c0:c0 + 2, ncol + q * 256:ncol + (q + 1) * 256]), 'D_w%d' % sl,
                            writes=[b_stage4[sl]])
                      bw = b_Wh[(id(wt), q)]
                      for a in range(2):
                          c = c0 + a
                          dst = wt[:, c, q * 256:(q + 1) * 256]
                          eng = 'act' if a == 0 else 'dve'
                          if use_gain is not None:
                              fn = (ACTV(dst, stv[:, a, :], AF.Copy, scale=use_gain[:, c:c + 1]) if a == 0
                                    else TS(dst, stv[:, a, :], use_gain[:, c:c + 1], None, ALU.mult))
                              S.op(eng, fn, reads=[b_stage4[sl], b_const, b_const3], writes=[bw])
                          else:
                              fn = ACTV(dst, stv[:, a, :], AF.Copy) if a == 0 else CP(dst, stv[:, a, :])
                              S.op(eng, fn, reads=[b_stage4[sl]], writes=[bw])

              for q in range(4):
                  for spec in W4spec:
                      load_quarter(*spec, q)
              for k2 in range(2):
                  load_half(Wo, wo_d, 0, None, k2)
              for tb in range(8):
                  tsl = slice(tb * 512, (tb + 1) * 512)
                  for m in range(8):
                      msl = slice(m * 128, (m + 1) * 128)
                      banks = []
                      for (wt, rhs_t, rb) in ((Woa, oab, b_oab), (Wob, obb, b_obb), (Wga, None, b_uT), (Wgb, None, b_uT)):
                          bk = nextbank()
                          for c in range(8):
                              rhs = rhs_t[:, c, :] if rhs_t is not None else uT[:, c, tsl]
                              S.op('pe', MM(ps[bk][:, :], lhsT=wt[:, c, msl], rhs=rhs, start=(c == 0), stop=(c == 7)),
                                   reads=[b_Wh[(id(wt), m // 2)], rb], writes=[b_ps[bk]], sig=(c == 7))
                          banks.append(bk)
                      S.op('act', ACTV(sg[0][:], ps[banks[2]][:, :], AF.Sigmoid, bias=bgT[:, m:m + 1]),
                           reads=[b_ps[banks[2]], b_const2], writes=[b_sg[0]])
                      S.op('act', ACTV(sg[1][:], ps[banks[3]][:, :], AF.Sigmoid, bias=bgT[:, 8 + m:9 + m]),
                           reads=[b_ps[banks[3]], b_const2], writes=[b_sg[1]])
                      S.op('dve', TT(sg[2][:], ps[banks[0]][:, :], sg[0][:], ALU.mult), reads=[b_ps[banks[0]], b_sg[0]],
                           writes=[b_sg[2]])
                      S.op('dve', TT(sg[3][:], ps[banks[1]][:, :], sg[1][:], ALU.mult), reads=[b_ps[banks[1]], b_sg[1]],
                           writes=[b_sg[3]])
                      S.op('pool', TT(mT[:, m, :], sg[2][:], sg[3][:], ALU.add), reads=[b_sg[2], b_sg[3]], writes=[b_mT])
                  if tb + 1 < 8:
                      nsl = slice((tb + 1) * 512, (tb + 2) * 512)
                      S.dma('sp', DMA(oab[:], oa_v[:, :, nsl]), 'D_ld0', writes=[b_oab])
                      S.dma('sp', DMA(obb[:], ob_v[:, :, nsl]), 'D_ld1', writes=[b_obb])
                  for tt in range(4):
                      t = tb * 4 + tt
                      s2 = t % 2
                      S.dma('sp', DMA(xr[s2][:], x_d[t * 128:(t + 1) * 128, :]), 'D_xr%d' % s2, writes=[b_xr[s2]])
                      for half in range(2):
                          bk = nextbank()
                          for m in range(8):
                              S.op('pe', MM(ps[bk][:, :], lhsT=mT[:, m, tt * 128:(tt + 1) * 128],
                                            rhs=Wo[:, m, half * 512:(half + 1) * 512], start=(m == 0), stop=(m == 7)),
                                   reads=[b_mT, b_Wh[(id(Wo), half)]], writes=[b_ps[bk]], sig=(m == 7))
                          S.op('dve', TT(hs[s2][:, half * 512:(half + 1) * 512], ps[bk][:, :], xr[s2][:, half * 512:(half + 1) * 512],
                                         ALU.add), reads=[b_ps[bk], b_xr[s2]], writes=[b_hs[s2]])
                      S.op('act', ACTV(junk[:], hs[s2][:], AF.Square, accum_out=s4[:, t:t + 1]),
                           reads=[b_hs[s2]], writes=[b_junk, b_s4[t]])
                      S.op('dve', TS(s4[:, t:t + 1], s4[:, t:t + 1], 1.0 / DM, 1e-6, ALU.mult, ALU.add),
                           reads=[b_s4[t]], writes=[b_s4[t]])
                      S.op('act', ACTV(s4[:, t:t + 1], s4[:, t:t + 1], AF.Sqrt), reads=[b_s4[t]], writes=[b_s4[t]])
                      S.op('dve', lambda e, t=t: e.reciprocal(out=s4[:, t:t + 1], in_=s4[:, t:t + 1]),
                           reads=[b_s4[t]], writes=[b_s4[t]])
                      S.op('dve', STT(hs[s2][:], hs[s2][:], s4[:, t:t + 1], fg[:], ALU.mult, ALU.mult),
                           reads=[b_hs[s2], b_s4[t], b_c4], writes=[b_hs[s2]])
                      ev = S.dma('pool', DMA(out_d[t * 128:(t + 1) * 128, :], hs[s2][:]), 'D_o%d' % s2, reads=[b_hs[s2]])
                      final_evs.append(ev)
              S.barrier()
              S.emit(nc, sems)
    return nc


def _lay8(w):
    n = w.shape[1]
    return np.ascontiguousarray(w.reshape(8, 128, n).transpose(1, 0, 2))


_NC_CACHE = {}


def kernel(x, norm_gain, w_in, b_gate, w_alpha, b_alpha, gla_norm_gain,
           w_out_attn, w_out_gla, w_out, final_norm_gain):
    f = np.float32
    x = np.asarray(x, f)
    W = np.asarray(w_in, f)[0]
    gainT = np.ascontiguousarray(np.asarray(norm_gain, f)[0].reshape(8, 128).T)
    watt = np.stack([
        _lay8(np.concatenate([W[:, hp * 128:(hp + 1) * 128], W[:, 1024 + hp * 128:1024 + (hp + 1) * 128],
                              W[:, 2048 + hp * 128:2048 + (hp + 1) * 128], W[:, 3072 + hp * 128:3072 + (hp + 1) * 128]], axis=1))
        for hp in range(8)])
    cols = [W[:, 4096:4608], W[:, 4608:5120]]
    for h in range(4):
        cols.append(W[:, 5120 + h * 256:5120 + (h + 1) * 256])
        cols.append(W[:, 6144 + h * 256:6144 + (h + 1) * 256])
    cols.append(W[:, 7168:7184])
    wgla = _lay8(np.concatenate(cols, axis=1))
    wg = _lay8(W[:, 7184:9232])
    woa = _lay8(np.asarray(w_out_attn, f)[0])
    wob = _lay8(np.asarray(w_out_gla, f)[0])
    wo = _lay8(np.asarray(w_out, f)[0])
    bgT = np.ascontiguousarray(np.asarray(b_gate, f)[0].reshape(16, 128).T)
    wal = np.zeros((128, 512), f)
    wal[0:16] = np.asarray(w_alpha, f)[0]
    wal[16] = np.asarray(b_alpha, f)[0]
    ggT = np.ascontiguousarray(np.asarray(gla_norm_gain, f)[0].reshape(8, 128).T)
    fg = np.ascontiguousarray(np.broadcast_to(np.asarray(final_norm_gain, f)[None, :], (128, 1024)))

    if 'nc' not in _NC_CACHE:
        _NC_CACHE['nc'] = build()
    nc = _NC_CACHE['nc']
    shared = dict(gainT=gainT, watt=watt, wgla=wgla, wg=wg, woa=woa, wob=wob, wo=wo, bgT=bgT, wal=wal, ggT=ggT, fg=fg)
    in_maps = [dict(shared, x=np.ascontiguousarray(x[b])) for b in range(8)]
    res = run_bass_kernel_spmd(nc, in_maps, core_ids=list(range(8)))
    if DEBUG:
        kernel.dbg = res.results
    return np.stack([np.asarray(res.results[b]["out"], f) for b in range(8)], axis=0)
```
